# Optimizing a Trainium2 kernel written in Bass

```python
import jax, jax.numpy as jnp
from jax import lax
import numpy as np

D_MODEL = 1024
BATCH = 4
SEQ = 4096
DEPTH = 4

GRID_W = 64
CTX_LEN = 256
MLA_HEADS = 4
MLA_Q_RANK = 256
MLA_KV_RANK = 128
MLA_NOPE = 128
MLA_ROPE = 64
MLA_V = 128
MLA_SCALE = (MLA_NOPE + MLA_ROPE) ** -0.5
Q_BLOCK = 128
ROPE_BASE = 10000.0
ML_HEADS = 4
ML_DH = 64
ML_W = ML_HEADS * ML_DH
ML_CHUNK = 128
LRU_W = 256
LRU_BLOCKS = 4
LRU_BD = LRU_W // LRU_BLOCKS
CONV_W = 4
CONV_LEFT = 2
LRU_C = 8.0
D_FF = 4 * D_MODEL
EPS = 1e-6
MLA_IN = MLA_Q_RANK + MLA_KV_RANK + MLA_ROPE
ML_IN = 4 * ML_W + 4 * ML_HEADS
LRU_IN = 2 * LRU_W
D_IN = MLA_IN + ML_IN + LRU_IN
MIX_W = MLA_HEADS * MLA_V + ML_W + LRU_W

kernel_name = "hymba_style_mla_mlstm_rglru_diffusion_trunk"


def rms_norm(x, gain=None):
    xf = x.astype(jnp.float32)
    y = xf * lax.rsqrt(jnp.mean(xf * xf, axis=-1, keepdims=True) + EPS)
    if gain is not None:
        y = y * gain.astype(jnp.float32)
    return y.astype(x.dtype)


def modulate(x, shift, scale):
    return x * (1.0 + scale) + shift


def axial_rope_tables(T):
    rows_n = T // GRID_W
    row = jnp.repeat(jnp.arange(rows_n), GRID_W).astype(jnp.float32)
    col = jnp.tile(jnp.arange(GRID_W), rows_n).astype(jnp.float32)
    half = MLA_ROPE // 2
    freqs = 1.0 / (ROPE_BASE ** (jnp.arange(0, half, 2, dtype=jnp.float32) / half))
    ang = jnp.concatenate([row[:, None] * freqs, col[:, None] * freqs], axis=-1)
    return jnp.cos(ang), jnp.sin(ang)


def apply_rope(x, cos, sin):
    x1, x2 = jnp.split(x.astype(jnp.float32), 2, axis=-1)
    return jnp.concatenate([x1 * cos - x2 * sin, x1 * sin + x2 * cos], axis=-1).astype(x.dtype)


def merge_heads(y):
    B, H, T, d = y.shape
    return y.transpose(0, 2, 1, 3).reshape(B, T, H * d)


def mla_qkv(a, g_q, w_uq, g_kv, w_ukv, rope):
    B, T, _ = a.shape
    c_q, c_kv, k_rope = jnp.split(a, [MLA_Q_RANK, MLA_Q_RANK + MLA_KV_RANK], axis=-1)
    q = (rms_norm(c_q, g_q) @ w_uq).reshape(B, T, MLA_HEADS, MLA_NOPE + MLA_ROPE).transpose(0, 2, 1, 3)
    kv = (rms_norm(c_kv, g_kv) @ w_ukv).reshape(B, T, MLA_HEADS, MLA_NOPE + MLA_V).transpose(0, 2, 1, 3)
    q_nope, q_rope = jnp.split(q, [MLA_NOPE], axis=-1)
    k_nope, v = jnp.split(kv, [MLA_NOPE], axis=-1)
    if rope is not None:
        q_rope = apply_rope(q_rope, *rope)
        k_rope = apply_rope(k_rope, *rope)
    k_rope = jnp.broadcast_to(k_rope[:, None], (B, MLA_HEADS, T, MLA_ROPE))
    q = jnp.concatenate([q_nope, q_rope], axis=-1)
    k = jnp.concatenate([k_nope, k_rope], axis=-1)
    return q, k, v


def attend(q, k, v):
    s = jnp.einsum("bhqd,bhkd->bhqk", q, k).astype(jnp.float32) * MLA_SCALE
    p = jax.nn.softmax(s, axis=-1).astype(v.dtype)
    return jnp.einsum("bhqk,bhkd->bhqd", p, v)


def blocked_attention(q, k, v):
    B, H, T, dk = q.shape
    nb = T // Q_BLOCK
    qb = jnp.moveaxis(q.reshape(B, H, nb, Q_BLOCK, dk), 2, 0)
    ob = lax.map(lambda qi: attend(qi, k, v), qb)
    return jnp.moveaxis(ob, 0, 2).reshape(B, H, T, v.shape[-1])


def mla_mixer(al, ac, rope, g_q, w_uq, g_kv, w_ukv, with_ctx_out):
    ql, kl, vl = mla_qkv(al, g_q, w_uq, g_kv, w_ukv, rope)
    qc, kc, vc = mla_qkv(ac, g_q, w_uq, g_kv, w_ukv, None)
    k_all = jnp.concatenate([kc, kl], axis=2)
    v_all = jnp.concatenate([vc, vl], axis=2)
    y_l = merge_heads(blocked_attention(ql, k_all, v_all))
    y_c = merge_heads(attend(qc, kc, vc)) if with_ctx_out else None
    return y_l, y_c


def mlstm_zero_state(B):
    return (jnp.zeros((B, ML_HEADS, ML_DH, ML_DH), jnp.float32),
            jnp.zeros((B, ML_HEADS, ML_DH), jnp.float32),
            jnp.zeros((B, ML_HEADS), jnp.float32))


def mlstm_scan(q, k, v, i_pre, f_pre, state):
    B, H, T, dh = q.shape
    nc = T // ML_CHUNK

    def chunks(a):
        return jnp.moveaxis(a.reshape(a.shape[:2] + (nc, ML_CHUNK) + a.shape[3:]), 2, 0)

    xs = (chunks(q), chunks(k), chunks(v), chunks(i_pre), chunks(jax.nn.log_sigmoid(f_pre)))
    lower = jnp.tril(jnp.ones((ML_CHUNK, ML_CHUNK), dtype=bool))

    def step(carry, chunk):
        C, n, m = carry
        qc, kc, vc, ic, lfc = chunk
        b = jnp.cumsum(lfc, axis=-1)
        d = jnp.where(lower, b[..., :, None] - b[..., None, :] + ic[..., None, :], -jnp.inf)
        inter = b + m[..., None]
        m_row = jnp.maximum(inter, jnp.max(d, axis=-1))
        w_intra = jnp.exp(d - m_row[..., None])
        w_inter = jnp.exp(inter - m_row)
        s = jnp.einsum("bhtd,bhsd->bhts", qc, kc) * w_intra
        num = jnp.einsum("bhts,bhsd->bhtd", s, vc) + w_inter[..., None] * jnp.einsum("bhtk,bhkv->bhtv", qc, C)
        den = jnp.sum(s, axis=-1) + w_inter * jnp.einsum("bhtk,bhk->bht", qc, n)
        h = num / jnp.maximum(jnp.abs(den), jnp.exp(-m_row))[..., None]
        b_last = b[..., -1]
        g = b_last[..., None] - b + ic
        m_new = jnp.maximum(b_last + m, jnp.max(g, axis=-1))
        w_old = jnp.exp(b_last + m - m_new)
        w_s = jnp.exp(g - m_new[..., None])
        C_new = w_old[..., None, None] * C + jnp.einsum("bhs,bhsk,bhsv->bhkv", w_s, kc, vc)
        n_new = w_old[..., None] * n + jnp.einsum("bhs,bhsk->bhk", w_s, kc)
        return (C_new, n_new, m_new), h

    state, hs = lax.scan(step, state, xs)
    h = jnp.moveaxis(hs, 0, 2).reshape(B, H, T, dh)
    return h, state


def mlstm_bidir(q, k, v, gf, gb, s_f, s_b):
    flip = lambda t: jnp.flip(t, axis=2)
    h_f, s_f = mlstm_scan(q, k, v, gf[0], gf[1], s_f)
    h_b, s_b = mlstm_scan(flip(q), flip(k), flip(v), flip(gb[0]), flip(gb[1]), s_b)
    return h_f + flip(h_b), s_f, s_b


def mlstm_mixer(zl, zc, gate_bias, with_ctx_out):
    def prep(z):
        Bz, T, _ = z.shape
        q, k, v, o, g = jnp.split(z, [ML_W, 2 * ML_W, 3 * ML_W, 4 * ML_W], axis=-1)
        heads = lambda t: t.reshape(Bz, T, ML_HEADS, ML_DH).transpose(0, 2, 1, 3).astype(jnp.float32)
        g = (g + gate_bias).astype(jnp.float32).transpose(0, 2, 1)
        i_f, f_f, i_b, f_b = jnp.split(g, 4, axis=1)
        return (heads(q) * ML_DH ** -0.5, heads(k), heads(v)), o, (i_f, f_f), (i_b, f_b)

    def finish(h, o):
        Bz, _, T, _ = h.shape
        h = rms_norm(h).transpose(0, 2, 1, 3).reshape(Bz, T, ML_W)
        return (jax.nn.sigmoid(o.astype(jnp.float32)) * h).astype(o.dtype)

    qkv_c, o_c, gf_c, gb_c = prep(zc)
    qkv_l, o_l, gf_l, gb_l = prep(zl)
    zero = mlstm_zero_state(zl.shape[0])
    h_c, s_f, s_b = mlstm_bidir(*qkv_c, gf_c, gb_c, zero, zero)
    h_l, _, _ = mlstm_bidir(*qkv_l, gf_l, gb_l, s_f, s_b)
    return finish(h_l, o_l), (finish(h_c, o_c) if with_ctx_out else None)


def dwconv_centred(x, w, b):
    T = x.shape[1]
    xp = jnp.pad(x, ((0, 0), (CONV_LEFT, CONV_W - 1 - CONV_LEFT), (0, 0)))
    return b + sum(xp[:, j:j + T] * w[j] for j in range(CONV_W))


def block_diag(x, w, b):
    B, T, _ = x.shape
    y = jnp.einsum("btgi,gio->btgo", x.reshape(B, T, LRU_BLOCKS, LRU_BD), w)
    return y.reshape(B, T, LRU_W) + b


def linear_scan(a, b, h0):
    b = b.at[:, 0].add(a[:, 0] * h0)
    comb = lambda l, r: (l[0] * r[0], r[0] * l[1] + r[1])
    _, h = lax.associative_scan(comb, (a, b), axis=1)
    return h


def rglru_mixer(rl, rc, conv_w, conv_b, w_a, b_a, w_x, b_x, lam, with_ctx_out):
    def branch(z):
        xb, gb = jnp.split(z, 2, axis=-1)
        return dwconv_centred(xb, conv_w, conv_b).astype(jnp.float32), jax.nn.gelu(gb)

    xs_c, gate_c = branch(rc)
    xs_l, gate_l = branch(rl)

    def gates(xs, d):
        r = jax.nn.sigmoid(block_diag(xs, w_a[d], b_a[d]))
        i = jax.nn.sigmoid(block_diag(xs, w_x[d], b_x[d]))
        log_a = -LRU_C * r * jax.nn.softplus(-lam[d].astype(jnp.float32))
        return jnp.exp(log_a), jnp.sqrt(-jnp.expm1(2.0 * log_a)) * (i * xs)

    def direction(d, rev):
        flip = (lambda t: jnp.flip(t, axis=1)) if rev else (lambda t: t)
        a_c, u_c = gates(flip(xs_c), d)
        h_c = linear_scan(a_c, u_c, jnp.zeros_like(u_c[:, 0]))
        a_l, u_l = gates(flip(xs_l), d)
        h_l = linear_scan(a_l, u_l, h_c[:, -1])
        return flip(h_l), flip(h_c)

    hf_l, hf_c = direction(0, False)
    hb_l, hb_c = direction(1, True)
    y_l = (gate_l * (hf_l + hb_l)).astype(rl.dtype)
    y_c = (gate_c * (hf_c + hb_c)).astype(rc.dtype) if with_ctx_out else None
    return y_l, y_c


def token_mixing(ul, uc, rope, w_in, w_out, g_q, w_uq, g_kv, w_ukv, ml_gate_bias,
                 conv_w, conv_b, w_a, b_a, w_x, b_x, lam, with_ctx_out):
    split_at = [MLA_IN, MLA_IN + ML_IN]
    al, ml_, rl = jnp.split(ul @ w_in, split_at, axis=-1)
    ac, mc, rc = jnp.split(uc @ w_in, split_at, axis=-1)
    ya_l, ya_c = mla_mixer(al, ac, rope, g_q, w_uq, g_kv, w_ukv, with_ctx_out)
    yb_l, yb_c = mlstm_mixer(ml_, mc, ml_gate_bias, with_ctx_out)
    yc_l, yc_c = rglru_mixer(rl, rc, conv_w, conv_b, w_a, b_a, w_x, b_x, lam, with_ctx_out)
    y_l = jnp.concatenate([ya_l, yb_l, yc_l], axis=-1) @ w_out
    y_c = (jnp.concatenate([ya_c, yb_c, yc_c], axis=-1) @ w_out) if with_ctx_out else None
    return y_l, y_c


def squared_relu_mlp(u, w1, w2):
    return jnp.square(jax.nn.relu(u @ w1)) @ w2


def setup_inputs(seed: int = 0) -> dict:
    key = jax.random.key(seed)
    ks = jax.random.split(key, 24)
    L = DEPTH
    nrm = lambda k, shape, s: jax.random.normal(k, shape, jnp.float32) * s
    gk = jax.random.split(ks[11], 4)
    f_bias = jnp.linspace(3.0, 6.0, ML_HEADS, dtype=jnp.float32)
    ml_gate_bias = jnp.concatenate([
        nrm(gk[0], (L, ML_HEADS), 0.1),
        f_bias + nrm(gk[1], (L, ML_HEADS), 0.1),
        nrm(gk[2], (L, ML_HEADS), 0.1),
        f_bias + nrm(gk[3], (L, ML_HEADS), 0.1)], axis=-1)
    a0 = jax.random.uniform(ks[18], (L, 2, LRU_W), jnp.float32, minval=0.9, maxval=0.999)
    return {
        "x": nrm(ks[0], (BATCH, SEQ, D_MODEL), 1.0),
        "c": nrm(ks[1], (BATCH, D_MODEL), 1.0),
        "ctx": nrm(ks[2], (BATCH, CTX_LEN, D_MODEL), 1.0),
        "c_ctx": nrm(ks[3], (D_MODEL,), 1.0),
        "w_mod": nrm(ks[4], (L, D_MODEL, 6 * D_MODEL), 0.5 * D_MODEL ** -0.5),
        "b_mod": nrm(ks[5], (L, 6 * D_MODEL), 0.02),
        "w_in": nrm(ks[6], (L, D_MODEL, D_IN), D_MODEL ** -0.5),
        "mla_g_q": 1.0 + nrm(ks[7], (L, MLA_Q_RANK), 0.1),
        "mla_w_uq": nrm(ks[8], (L, MLA_Q_RANK, MLA_HEADS * (MLA_NOPE + MLA_ROPE)), MLA_Q_RANK ** -0.5),
        "mla_g_kv": 1.0 + nrm(ks[9], (L, MLA_KV_RANK), 0.1),
        "mla_w_ukv": nrm(ks[10], (L, MLA_KV_RANK, MLA_HEADS * (MLA_NOPE + MLA_V)), MLA_KV_RANK ** -0.5),
        "ml_gate_bias": ml_gate_bias,
        "lru_conv_w": nrm(ks[12], (L, CONV_W, LRU_W), CONV_W ** -0.5),
        "lru_conv_b": nrm(ks[13], (L, LRU_W), 0.02),
        "lru_w_a": nrm(ks[14], (L, 2, LRU_BLOCKS, LRU_BD, LRU_BD), LRU_BD ** -0.5),
        "lru_b_a": nrm(ks[15], (L, 2, LRU_W), 0.02),
        "lru_w_x": nrm(ks[16], (L, 2, LRU_BLOCKS, LRU_BD, LRU_BD), LRU_BD ** -0.5),
        "lru_b_x": nrm(ks[17], (L, 2, LRU_W), 0.02),
        "lru_lam": jnp.log(a0) - jnp.log1p(-a0),
        "w_out": nrm(ks[19], (L, MIX_W, D_MODEL), MIX_W ** -0.5),
        "w_ff1": nrm(ks[20], (L, D_MODEL, D_FF), D_MODEL ** -0.5),
        "w_ff2": nrm(ks[21], (L, D_FF, D_MODEL), D_FF ** -0.5),
        "final_g": 1.0 + nrm(ks[22], (D_MODEL,), 0.1),
    }


def reference(x, c, ctx, c_ctx, w_mod, b_mod, w_in, mla_g_q, mla_w_uq, mla_g_kv, mla_w_ukv,
              ml_gate_bias, lru_conv_w, lru_conv_b, lru_w_a, lru_b_a, lru_w_x, lru_b_x, lru_lam,
              w_out, w_ff1, w_ff2, final_g):
    rope = axial_rope_tables(x.shape[1])
    xl, xc = x, ctx
    for l in range(DEPTH):
        with_ctx_out = l < DEPTH - 1
        mod_l = (jax.nn.silu(c) @ w_mod[l] + b_mod[l])[:, None, :]
        mod_c = jax.nn.silu(c_ctx) @ w_mod[l] + b_mod[l]
        sh1l, sc1l, g1l, sh2l, sc2l, g2l = jnp.split(mod_l, 6, axis=-1)
        sh1c, sc1c, g1c, sh2c, sc2c, g2c = jnp.split(mod_c, 6, axis=-1)
        ul = modulate(rms_norm(xl), sh1l, sc1l)
        uc = modulate(rms_norm(xc), sh1c, sc1c)
        yl, yc = token_mixing(ul, uc, rope, w_in[l], w_out[l], mla_g_q[l], mla_w_uq[l], mla_g_kv[l],
                              mla_w_ukv[l], ml_gate_bias[l], lru_conv_w[l], lru_conv_b[l], lru_w_a[l],
                              lru_b_a[l], lru_w_x[l], lru_b_x[l], lru_lam[l], with_ctx_out)
        xl = xl + g1l * yl
        xl = xl + g2l * squared_relu_mlp(modulate(rms_norm(xl), sh2l, sc2l), w_ff1[l], w_ff2[l])
        if with_ctx_out:
            xc = xc + g1c * yc
            xc = xc + g2c * squared_relu_mlp(modulate(rms_norm(xc), sh2c, sc2c), w_ff1[l], w_ff2[l])
    return rms_norm(xl, final_g)
```

```python
import numpy as np
from contextlib import ExitStack
import concourse.bass as bass
import concourse.mybir as mybir
from concourse.bass_utils import run_bass_kernel_spmd
from concourse.ap import AP

F32 = mybir.dt.float32
BF16 = mybir.dt.bfloat16
AF = mybir.ActivationFunctionType
ALU = mybir.AluOpType
AX = mybir.AxisListType

D = 1024
NLAT = 4096
NCTX = 256
NT = NLAT + NCTX
NTILE = NT // 128
EPS = 1e-6
DFF = 4096
SCALE = (128 + 64) ** -0.5
ENG = ("pe", "act", "dve", "pool", "sp")
NSLOT = 24
BLOCKS = [(j * 512, 512) for j in range(8)] + [(4096, 256)]


class _Rec:
    def __getattr__(self, name):
        return lambda *a, **k: (name, a, k)


_REC = _Rec()


class T:
    __slots__ = ("ap", "w", "r")

    def __init__(self, ap):
        self.ap = ap
        self.w = []
        self.r = {}


class Prog:
    def __init__(self, nc):
        self.nc = nc
        self.ops = {e: [] for e in ENG}
        self.cnt = {}
        self.seen = {e: {} for e in ENG}
        self.layer = 0
        self.slot_cum = [0] * NSLOT
        self.rr = 0
        self.sb_off = 16512
        self.nalloc = 0

    def alloc(self, shape, dtype, name=None):
        nbytes = int(np.prod(shape[1:])) * (2 if dtype == BF16 else 4)
        off = (self.sb_off + 63) // 64 * 64
        self.sb_off = off + nbytes
        assert self.sb_off <= 229344, ("sbuf overflow", self.sb_off, name)
        self.nalloc += 1
        h = self.nc.alloc_sbuf_tensor_at(f"sb{self.nalloc}_{name or ''}", list(shape), dtype, offset=off)
        return T(h[:] if hasattr(h, "__getitem__") else h.ap())

    def mark(self):
        return self.sb_off

    def release(self, m):
        self.sb_off = m

    def emit(self, eng, fn, reads=(), writes=(), dma=False):
        raw, oth = {}, {}

        def add(d, tok):
            k, v = tok
            if d.get(k, 0) < v:
                d[k] = v

        for t in reads:
            for tok in t.w:
                add(raw, tok)
        for t in writes:
            for tok in t.w:
                add(oth, tok)
            for k, v in t.r.items():
                add(oth, (k, v))
        if dma:
            slot = self.rr % NSLOT
            self.rr += 1
            prev = self.slot_cum[slot]
            if prev:
                add(oth, (("dma", slot), prev))
            self.slot_cum[slot] = prev + 16
            mytok = (("dma", slot), prev + 16)
        else:
            key = (eng, self.layer)
            self.cnt[key] = self.cnt.get(key, 0) + 1
            mytok = (key, self.cnt[key])
        waits = []
        seen = self.seen[eng]
        for d, is_raw in ((raw, True), (oth, False)):
            for k, v in d.items():
                if (not is_raw) and (not dma) and k[0] == eng:
                    continue
                if seen.get(k, 0) >= v:
                    continue
                seen[k] = v
                waits.append((k, v))
        self.ops[eng].append((waits, fn(_REC), mytok, dma))
        for t in reads:
            k, v = mytok
            if t.r.get(k, 0) < v:
                t.r[k] = v
        for t in writes:
            t.w = [mytok]
            t.r = {}
        return mytok

    def barrier(self):
        toks = dict(self.cnt)
        for s in range(NSLOT):
            if self.slot_cum[s]:
                toks[("dma", s)] = self.slot_cum[s]
        for e in ENG:
            waits = []
            seen = self.seen[e]
            for k, v in toks.items():
                if k[0] == e:
                    continue
                if seen.get(k, 0) >= v:
                    continue
                seen[k] = v
                waits.append((k, v))
            if waits:
                self.ops[e].append((waits, None, None, False))

    def finalize(self):
        nc = self.nc
        keys = set(self.cnt.keys())
        for s in range(NSLOT):
            keys.add(("dma", s))
        with ExitStack() as st:
            sems = {}
            for k in sorted(keys, key=str):
                sems[k] = st.enter_context(nc.semaphore(f"s_{k[0]}_{k[1]}"))
            block = st.enter_context(nc.Block())

            def replay(name):
                def run(e):
                    for waits, fn, tok, dma in self.ops[name]:
                        for k, v in waits:
                            e.wait_ge(sems[k], v)
                        if fn is not None:
                            ins = getattr(e, fn[0])(*fn[1], **fn[2])
                            ins.then_inc(sems[tok[0]], 16 if dma else 1)
                return run

            block.tensor(replay("pe"))
            block.scalar(replay("act"))
            block.vector(replay("dve"))
            block.gpsimd(replay("pool"))
            block.sync(replay("sp"))


def bc_last(ap, n):
    return AP(ap.tensor, ap.offset, [list(x) for x in ap.ap] + [[0, n]])


class _Stop(Exception):
    pass


def build(depth=4, debug=False, stop=None):
    nc = bass.Bass("TRN2", target_bir_lowering=False)
    P = Prog(nc)
    L = depth

    def din(name, shape, dt=F32):
        return nc.dram_tensor(name, list(shape), dt, kind="ExternalInput").ap()

    xin = din("xin", [NT, D])
    cc_d = din("cc", [128, 8, 2])
    wmod_d = din("w_mod", [L, D, 6 * D])
    bmod_d = din("b_mod", [L, 6 * D])
    winfm_d = din("w_in_fm", [L, D, 13 * 128])
    wintm_d = din("w_in_tm", [L, D, 784])
    wuq_d = din("w_uq", [L, 256, 1024])
    wk_d = din("w_k", [L, 128, 512])
    wv_d = din("w_v", [L, 128, 512])
    wout_d = din("w_out", [L, D, D])
    wff1_d = din("w_ff1", [L, D, DFF])
    wff2_d = din("w_ff2", [L, DFF, D])
    wbd_d = din("w_bd", [L, 128, 8, 128])
    small_d = din("small", [128, L, 40])
    gbias_d = din("gbias", [128, L, 16])
    fing_d = din("final_g", [128, D])
    ropeC_d = din("ropeC", [128, NT])
    ropeS_d = din("ropeS", [128, NT])
    const_d = din("consts", [128, 4, 128])
    out_d = nc.dram_tensor("out", [NLAT, D], F32, kind="ExternalOutput").ap()
    X_d = nc.dram_tensor("Xs", [NT, D], F32, kind="Internal").ap()
    UT_d = nc.dram_tensor("UTs", [8, 128, NT], BF16, kind="Internal").ap()
    MT_d = nc.dram_tensor("MTs", [8, 128, NT], BF16, kind="Internal").ap()
    dbg = {}
    if debug:
        dbg["UT"] = nc.dram_tensor("dbgUT", [8, 128, NT], BF16, kind="ExternalOutput").ap()
        dbg["MT"] = nc.dram_tensor("dbgMT", [8, 128, NT], BF16, kind="ExternalOutput").ap()
        dbg["X"] = nc.dram_tensor("dbgX", [NT, D], F32, kind="ExternalOutput").ap()

    Xt = [T(None) for _ in range(NTILE)]
    UTt = [T(None) for _ in range(9)]
    MTt = [[T(None) for _ in range(9)] for _ in range(8)]
    OUTt = [T(None) for _ in range(32)]

    ps = []
    for i in range(8):
        h = nc.alloc_psum_tensor(f"ps{i}", [128, 512], F32)
        ps.append(T(h[:]))

    def psb(t):
        return t.ap.bitcast(BF16)

    consts = P.alloc([128, 4, 128], F32, "consts")
    identF = consts.ap[:, 0, :]
    maskU = consts.ap[:, 1, :]
    maskL = consts.ap[:, 2, :]
    onesF = consts.ap[:, 3, :]
    cb = P.alloc([128, 2, 128], BF16, "constb")
    identB = cb.ap[:, 0, :]
    onesB = cb.ap[:, 1, :]
    small = P.alloc([128, L, 40], F32, "small")
    gbias = P.alloc([128, L, 16], F32, "gbias")
    siluT = P.alloc([128, 8, 33], F32, "siluT")
    modT = P.alloc([128, 48, 2], F32, "modT")
    gbt = P.alloc([128, 4, D], F32, "gbt")
    stage = [P.alloc([128, 2048], F32, f"stage{i}") for i in range(2)]
    stage_rr = [0]


    def rsqrt_ops(P_, dst, src_ap, src_T, shape_ap, epsv):
        P_.emit("dve", lambda e: e.tensor_scalar(out=shape_ap, in0=src_ap, scalar1=epsv, scalar2=None, op0=ALU.add), reads=[src_T], writes=[dst])
        P_.emit("act", lambda e: e.activation(out=shape_ap, in_=shape_ap, func=AF.Ln), reads=[dst], writes=[dst])
        P_.emit("act", lambda e: e.activation(out=shape_ap, in_=shape_ap, func=AF.Exp, scale=-0.5), reads=[dst], writes=[dst])

    def sigmoid_ops(P_, dst, dst_ap, src_ap, src_Ts, scale=1.0, bias=None, eng2="dve"):
        if bias is None:
            P_.emit("act", lambda e: e.activation(out=dst_ap, in_=src_ap, func=AF.Exp, scale=-scale), reads=list(src_Ts), writes=[dst])
        else:
            P_.emit("act", lambda e: e.activation(out=dst_ap, in_=src_ap, func=AF.Exp, scale=-scale, bias=bias), reads=list(src_Ts), writes=[dst])
        P_.emit(eng2, lambda e: e.tensor_scalar(out=dst_ap, in0=dst_ap, scalar1=1.0, scalar2=None, op0=ALU.add), reads=[dst], writes=[dst])
        P_.emit("dve", lambda e: e.reciprocal(out=dst_ap, in_=dst_ap), reads=[dst], writes=[dst])

    def dma(eng, out_ap, in_ap, reads, writes):
        P.emit(eng, lambda e: e.dma_start(out=out_ap, in_=in_ap), reads=reads, writes=writes, dma=True)

    def load_cast(dst_T, dst_ap3, src_ap3, nk, ncols):
        per = max(1, 2048 // ncols)
        k = 0
        while k < nk:
            kk = min(per, nk - k)
            st = stage[stage_rr[0] % 2]
            stage_rr[0] += 1
            sv = st.ap[:, 0:kk * ncols].rearrange("p (k n) -> p k n", k=kk)
            dma("sp", sv, src_ap3[:, k:k + kk, :], [], [st])
            d = dst_ap3[:, k:k + kk, :]
            P.emit("pool", lambda e, d=d, sv=sv: e.tensor_copy(out=d, in_=sv), reads=[st], writes=[dst_T])
            k += kk

    dma("sp", consts.ap, const_d, [], [consts])
    dma("sp", small.ap, small_d, [], [small])
    dma("sp", gbias.ap, gbias_d, [], [gbias])
    P.emit("dve", lambda e: e.tensor_copy(out=identB, in_=identF), reads=[consts], writes=[cb])
    P.emit("dve", lambda e: e.tensor_copy(out=onesB, in_=onesF), reads=[consts], writes=[cb])
    m0 = P.mark()
    cct = P.alloc([128, 8, 2], F32, "cct")
    cth = P.alloc([128, 8, 2], F32, "cth")
    dma("sp", cct.ap, cc_d, [], [cct])
    P.emit("pool", lambda e: e.memset(siluT.ap, 0.0), writes=[siluT])
    sigmoid_ops(P, cth, cth.ap, cct.ap, [cct])
    for j, col in ((0, 0), (1, 32)):
        P.emit("dve", lambda e, j=j, col=col: e.tensor_tensor(out=siluT.ap[:, :, col], in0=cth.ap[:, :, j], in1=cct.ap[:, :, j], op=ALU.mult),
               reads=[cth, cct, siluT], writes=[siluT])
    for l in range(L):
        s = small.ap[:, l, :]
        P.emit("dve", lambda e, s=s: e.tensor_scalar(out=s[:, 25:27], in0=s[:, 0:2], scalar1=16.0, scalar2=None, op0=ALU.mult), reads=[small], writes=[small])
        P.emit("dve", lambda e, s=s: e.tensor_scalar(out=s[:, 27:28], in0=s[:, 2:3], scalar1=float(np.sqrt(128.0)), scalar2=None, op0=ALU.mult), reads=[small], writes=[small])
        P.emit("dve", lambda e, s=s: e.tensor_scalar(out=s[:, 28:36], in0=s[:, 13:21], scalar1=-1.0, scalar2=None, op0=ALU.mult), reads=[small], writes=[small])
        P.emit("act", lambda e, s=s: e.activation(out=s[:, 36:40], in_=s[:, 21:25], func=AF.Exp, scale=-1.0), reads=[small], writes=[small])
        P.emit("act", lambda e, s=s: e.activation(out=s[:, 36:40], in_=s[:, 36:40], func=AF.Ln, bias=1.0), reads=[small], writes=[small])
        P.emit("dve", lambda e, s=s: e.tensor_scalar(out=s[:, 36:40], in0=s[:, 36:40], scalar1=-8.0, scalar2=None, op0=ALU.mult), reads=[small], writes=[small])
    P.barrier()
    P.release(m0)
    base_mark = P.mark()

    def xsrc(l):
        return xin if l == 0 else X_d

    def ln_tile(xt, uT_T, uT_ap, c0, shc, scc, col, work):
        sq, ssum, rs, xn, pst = work
        P.emit("act", lambda e: e.activation(out=sq.ap, in_=xt.ap, func=AF.Square), reads=[xt], writes=[sq])
        P.emit("dve", lambda e: e.reduce_sum(out=ssum.ap, in_=sq.ap, axis=AX.X), reads=[sq], writes=[ssum])
        rsqrt_ops(P, rs, ssum.ap, ssum, rs.ap, D * EPS)
        P.emit("dve", lambda e: e.tensor_scalar(out=xn.ap, in0=xt.ap, scalar1=rs.ap[:, 0:1], scalar2=32.0, op0=ALU.mult, op1=ALU.mult),
               reads=[xt, rs], writes=[xn])
        pv = psb(pst)
        for k in range(8):
            P.emit("pe", lambda e, k=k: e.transpose(pv[:, k * 128:(k + 1) * 128], xn.ap[:, k * 128:(k + 1) * 128], identB),
                   reads=[xn, cb], writes=[pst])
        for k in range(8):
            P.emit("act", lambda e, k=k: e.activation(out=uT_ap[:, k, c0:c0 + 128], in_=pv[:, k * 128:(k + 1) * 128], func=AF.Identity,
                                                      scale=modT.ap[:, scc + k, col:col + 1], bias=modT.ap[:, shc + k, col:col + 1]),
                   reads=[pst, modT], writes=[uT_T])

    phase_ctr = [0]

    def phase_done(name):
        phase_ctr[0] += 1
        if stop is not None and phase_ctr[0] >= stop:
            raise _Stop(name)

    for l in range(L):
      try:
        P.layer = l
        sm = small.ap[:, l, :]
        P.release(base_mark)
        m = P.mark()
        modrow = P.alloc([33, 6 * D], F32, "modrow")
        bmrow = P.alloc([33, 6 * D], F32, "bmrow")
        wm = [P.alloc([128, 8, 512], F32, f"wm{i}") for i in range(2)]
        P.emit("pool", lambda e: e.memset(bmrow.ap, 0.0), writes=[bmrow])
        dma("sp", bmrow.ap[0:1, :], bmod_d[l:l + 1, :], [], [bmrow])
        dma("sp", bmrow.ap[32:33, :], bmod_d[l:l + 1, :], [], [bmrow])
        for nb in range(12):
            w = wm[nb % 2]
            dma("sp", w.ap, wmod_d[l, :, nb * 512:(nb + 1) * 512].rearrange("(k p) n -> p k n", p=128), [], [w])
            pt = ps[nb % 2]
            for k in range(8):
                P.emit("pe", lambda e, k=k, w=w, pt=pt: e.matmul(pt.ap[0:33, :], lhsT=siluT.ap[:, k, :], rhs=w.ap[:, k, :], start=(k == 0), stop=(k == 7)),
                       reads=[siluT, w], writes=[pt])
            P.emit("dve", lambda e, nb=nb, pt=pt: e.tensor_tensor(out=modrow.ap[:, nb * 512:(nb + 1) * 512], in0=pt.ap[0:33, :],
                                                                 in1=bmrow.ap[:, nb * 512:(nb + 1) * 512], op=ALU.add),
                   reads=[pt, bmrow], writes=[modrow])
        for c in list(range(0, 16)) + list(range(24, 40)):
            pt = ps[2 + c % 2]
            P.emit("pe", lambda e, c=c, pt=pt: e.transpose(pt.ap[:, 0:33], modrow.ap[:, c * 128:(c + 1) * 128], identF[0:33, 0:33]),
                   reads=[modrow, consts], writes=[pt])
            is_scale = (8 <= c < 16) or (32 <= c < 40)
            for j, col in ((0, 0), (1, 32)):
                P.emit("dve", lambda e, c=c, j=j, col=col, pt=pt, a=(1.0 if is_scale else 0.0):
                       e.tensor_scalar(out=modT.ap[:, c, j:j + 1], in0=pt.ap[:, col:col + 1], scalar1=a, scalar2=None, op0=ALU.add),
                       reads=[pt], writes=[modT])
        gi = 0
        for gcol in (2 * D, 5 * D):
            for row in (0, 32):
                for half in range(2):
                    pt = ps[4 + half]
                    P.emit("pe", lambda e, row=row, gcol=gcol, half=half, pt=pt:
                           e.matmul(pt.ap, lhsT=onesF[row:row + 1, :], rhs=modrow.ap[row:row + 1, gcol + half * 512: gcol + half * 512 + 512], start=True, stop=True),
                           reads=[consts, modrow], writes=[pt])
                    P.emit("act", lambda e, gi=gi, half=half, pt=pt: e.activation(out=gbt.ap[:, gi, half * 512:(half + 1) * 512], in_=pt.ap, func=AF.Identity),
                           reads=[pt], writes=[gbt])
                gi += 1
        P.barrier()
        P.release(m)

        phase_done('P0')
        m = P.mark()
        xts = [P.alloc([128, D], F32, f"xt{i}") for i in range(2)]
        lnw = [(P.alloc([128, D], F32, f"sq{i}"), P.alloc([128, 1], F32, f"ssum{i}"), P.alloc([128, 1], F32, f"rs{i}"), P.alloc([128, D], BF16, f"xn{i}")) for i in range(2)]
        uTb = [P.alloc([128, 8, 512], BF16, f"uTb{i}") for i in range(2)]
        for bj, (c0, w) in enumerate(BLOCKS):
            ub = uTb[bj % 2]
            col = 1 if bj == 8 else 0
            for tt in range(w // 128):
                ti = c0 // 128 + tt
                xt = xts[ti % 2]
                dma("sp", xt.ap, xsrc(l)[ti * 128:(ti + 1) * 128, :], [Xt[ti]], [xt])
                ln_tile(xt, ub, ub.ap, tt * 128, 0, 8, col, lnw[ti % 2] + (ps[ti % 2],))
            dma("pool", UT_d[:, :, c0:c0 + w].rearrange("k p n -> p k n"), ub.ap[:, :, 0:w], [ub], [UTt[bj]])
            if debug and l == 0:
                dma("pool", dbg["UT"][:, :, c0:c0 + w].rearrange("k p n -> p k n"), ub.ap[:, :, 0:w], [ub], [])
        P.barrier()
        P.release(m)

        phase_done('P1')
        m_mla = P.mark()
        wfm = P.alloc([128, 8, 5 * 128], BF16, "wfm")
        load_cast(wfm, wfm.ap, winfm_d[l][:, 0:5 * 128].rearrange("(k p) n -> p k n", p=128), 8, 5 * 128)
        wuq = P.alloc([128, 2, 1024], BF16, "wuq")
        load_cast(wuq, wuq.ap, wuq_d[l].rearrange("(k p) n -> p k n", p=128), 2, 1024)
        wkv = P.alloc([128, 2, 512], BF16, "wkv")
        load_cast(wkv, wkv.ap[:, 0:1, :], wk_d[l].rearrange("(k p) n -> p k n", p=128), 1, 512)
        load_cast(wkv, wkv.ap[:, 1:2, :], wv_d[l].rearrange("(k p) n -> p k n", p=128), 1, 512)
        knT = [P.alloc([128, 4, 512], BF16, f"knT{j}") for j in range(9)]
        krT = [P.alloc([128, 512], BF16, f"krT{j}") for j in range(9)]
        Vt = [P.alloc([128, 512], BF16, f"V{i}") for i in range(NTILE)]
        m_k = P.mark()
        ublk = [P.alloc([128, 8, 512], BF16, f"ublk{i}") for i in range(2)]
        ropc = [P.alloc([128, 512], F32, f"ropc{i}") for i in range(2)]
        rops = [P.alloc([128, 512], F32, f"rops{i}") for i in range(2)]
        sqb = P.alloc([128, 2, 512], BF16, "sqb")
        rstd = P.alloc([128, 512], F32, "rstd")
        ckvn = P.alloc([128, 512], BF16, "ckvn")
        t1 = P.alloc([128, 512], F32, "t1")
        t2 = P.alloc([128, 512], F32, "t2")

        def inproj_fm(ub, chunk, pt, w):
            for k in range(8):
                P.emit("pe", lambda e, k=k: e.matmul(pt.ap[:, 0:w], lhsT=wfm.ap[:, k, chunk * 128:(chunk + 1) * 128], rhs=ub.ap[:, k, 0:w],
                                                     start=(k == 0), stop=(k == 7)), reads=[wfm, ub], writes=[pt])

        for bj, (c0, w) in enumerate(BLOCKS):
            ub = ublk[bj % 2]
            rc, rsn = ropc[bj % 2], rops[bj % 2]
            dma("sp", ub.ap[:, :, 0:w], UT_d[:, :, c0:c0 + w].rearrange("k p n -> p k n"), [UTt[bj]], [ub])
            dma("sp", rc.ap[:, 0:w], ropeC_d[:, c0:c0 + w], [], [rc])
            dma("sp", rsn.ap[:, 0:w], ropeS_d[:, c0:c0 + w], [], [rsn])
            inproj_fm(ub, 2, ps[0], w)
            inproj_fm(ub, 3, ps[1], w)
            inproj_fm(ub, 4, ps[2], w)
            P.emit("act", lambda e, w=w: e.activation(out=sqb.ap[:, 0, 0:w], in_=ps[0].ap[:, 0:w], func=AF.Square), reads=[ps[0]], writes=[sqb])
            P.emit("pe", lambda e, w=w: e.matmul(ps[3].ap[:, 0:w], lhsT=onesB, rhs=sqb.ap[:, 0, 0:w], start=True, stop=True), reads=[cb, sqb], writes=[ps[3]])
            rsqrt_ops(P, rstd, ps[3].ap[:, 0:w], ps[3], rstd.ap[:, 0:w], 128 * EPS)
            P.emit("dve", lambda e, w=w: e.scalar_tensor_tensor(out=ckvn.ap[:, 0:w], in0=ps[0].ap[:, 0:w], scalar=sm[:, 27:28], in1=rstd.ap[:, 0:w], op0=ALU.mult, op1=ALU.mult),
                   reads=[ps[0], rstd, small], writes=[ckvn])
            for h in range(4):
                pt = ps[4 + h % 2]
                P.emit("pe", lambda e, h=h, pt=pt, w=w: e.matmul(pt.ap[:, 0:w], lhsT=wkv.ap[:, 0, h * 128:(h + 1) * 128], rhs=ckvn.ap[:, 0:w], start=True, stop=True),
                       reads=[wkv, ckvn], writes=[pt])
                P.emit("act", lambda e, h=h, pt=pt, w=w, bj=bj: e.activation(out=knT[bj].ap[:, h, 0:w], in_=pt.ap[:, 0:w], func=AF.Identity), reads=[pt], writes=[knT[bj]])
            for tt in range(w // 128):
                ti = c0 // 128 + tt
                pt = ps[6 + tt % 2]
                P.emit("pe", lambda e, tt=tt, pt=pt: e.matmul(pt.ap, lhsT=ckvn.ap[:, tt * 128:(tt + 1) * 128], rhs=wkv.ap[:, 1, :], start=True, stop=True),
                       reads=[ckvn, wkv], writes=[pt])
                P.emit("dve", lambda e, ti=ti, pt=pt: e.tensor_copy(out=Vt[ti].ap, in_=pt.ap), reads=[pt], writes=[Vt[ti]])
            P.emit("dve", lambda e, w=w, rc=rc: e.tensor_tensor(out=t1.ap[:, 0:w], in0=ps[1].ap[:, 0:w], in1=rc.ap[:, 0:w], op=ALU.mult), reads=[ps[1], rc], writes=[t1])
            P.emit("dve", lambda e, w=w, rsn=rsn: e.tensor_tensor(out=t2.ap[:, 0:w], in0=ps[2].ap[:, 0:w], in1=rsn.ap[:, 0:w], op=ALU.mult), reads=[ps[2], rsn], writes=[t2])
            P.emit("pool", lambda e, w=w, bj=bj: e.tensor_tensor(out=krT[bj].ap[:, 0:w], in0=t1.ap[:, 0:w], in1=t2.ap[:, 0:w], op=ALU.add), reads=[t1, t2], writes=[krT[bj]])
        P.barrier()
        P.release(m_k)

        phase_done('P2')
        ublk = [P.alloc([128, 8, 512], BF16, f"ublkq{i}") for i in range(2)]
        ropc = [P.alloc([128, 512], F32, f"ropcq{i}") for i in range(2)]
        rops = [P.alloc([128, 512], F32, f"ropsq{i}") for i in range(2)]
        sqb = P.alloc([128, 2, 512], BF16, "sqbq")
        rstd = P.alloc([128, 512], F32, "rstdq")
        cqn = P.alloc([128, 2, 512], BF16, "cqn")
        qnT = [P.alloc([128, 4, 512], BF16, f"qnT{i}") for i in range(2)]
        qrT = [P.alloc([128, 2, 512], BF16, f"qrT{i}") for i in range(2)]
        t1 = P.alloc([128, 512], F32, "t1q")
        t2 = P.alloc([128, 512], F32, "t2q")
        pT = [P.alloc([128, 512], BF16, f"pT{i}") for i in range(3)]
        rden = P.alloc([128, 512], F32, "rden")
        yT = [P.alloc([128, 512], BF16, f"yT{i}") for i in range(2)]
        pti = 0
        for bj, (c0, w) in enumerate(BLOCKS):
            ub = ublk[bj % 2]
            rc, rsn = ropc[bj % 2], rops[bj % 2]
            qn, qr = qnT[bj % 2], qrT[bj % 2]
            dma("sp", ub.ap[:, :, 0:w], UT_d[:, :, c0:c0 + w].rearrange("k p n -> p k n"), [UTt[bj]], [ub])
            dma("sp", rc.ap[:, 0:w], ropeC_d[:, c0:c0 + w], [], [rc])
            dma("sp", rsn.ap[:, 0:w], ropeS_d[:, c0:c0 + w], [], [rsn])
            inproj_fm(ub, 0, ps[0], w)
            inproj_fm(ub, 1, ps[1], w)
            for c in range(2):
                P.emit("act", lambda e, c=c, w=w: e.activation(out=sqb.ap[:, c, 0:w], in_=ps[c].ap[:, 0:w], func=AF.Square), reads=[ps[c]], writes=[sqb])
            for c in range(2):
                P.emit("pe", lambda e, c=c, w=w: e.matmul(ps[2].ap[:, 0:w], lhsT=onesB, rhs=sqb.ap[:, c, 0:w], start=(c == 0), stop=(c == 1)), reads=[cb, sqb], writes=[ps[2]])
            rsqrt_ops(P, rstd, ps[2].ap[:, 0:w], ps[2], rstd.ap[:, 0:w], 256 * EPS)
            for c in range(2):
                P.emit("dve", lambda e, c=c, w=w: e.scalar_tensor_tensor(out=cqn.ap[:, c, 0:w], in0=ps[c].ap[:, 0:w], scalar=sm[:, 25 + c:26 + c], in1=rstd.ap[:, 0:w], op0=ALU.mult, op1=ALU.mult),
                       reads=[ps[c], rstd, small], writes=[cqn])

            def qproj(ch, pt):
                for kc in range(2):
                    P.emit("pe", lambda e, kc=kc: e.matmul(pt.ap[:, 0:w], lhsT=wuq.ap[:, kc, ch * 128:(ch + 1) * 128], rhs=cqn.ap[:, kc, 0:w], start=(kc == 0), stop=(kc == 1)),
                           reads=[wuq, cqn], writes=[pt])
            for h in range(4):
                pt = ps[3 + h % 2]
                qproj(h, pt)
                P.emit("dve", lambda e, h=h, pt=pt, w=w, qn=qn: e.tensor_copy(out=qn.ap[:, h, 0:w], in_=pt.ap[:, 0:w]), reads=[pt], writes=[qn])
            for pr in range(2):
                qproj(4 + pr, ps[5])
                qproj(6 + pr, ps[6])
                P.emit("dve", lambda e, w=w, rc=rc: e.tensor_tensor(out=t1.ap[:, 0:w], in0=ps[5].ap[:, 0:w], in1=rc.ap[:, 0:w], op=ALU.mult), reads=[ps[5], rc], writes=[t1])
                P.emit("dve", lambda e, w=w, rsn=rsn: e.tensor_tensor(out=t2.ap[:, 0:w], in0=ps[6].ap[:, 0:w], in1=rsn.ap[:, 0:w], op=ALU.mult), reads=[ps[6], rsn], writes=[t2])
                P.emit("pool", lambda e, w=w, pr=pr, qr=qr: e.tensor_tensor(out=qr.ap[:, pr, 0:w], in0=t1.ap[:, 0:w], in1=t2.ap[:, 0:w], op=ALU.add), reads=[t1, t2], writes=[qr])
            ktiles = [32, 33] if bj == 8 else list(range(NTILE))
            for h in range(4):
                po, pd = ps[0], ps[1]
                hp = 64 * (h % 2)
                for ki, kt in enumerate(ktiles):
                    kb, ko = kt // 4, (kt % 4) * 128
                    pss = ps[2 + ki % 2]
                    pp = pT[pti % 3]
                    pti += 1
                    P.emit("pe", lambda e, kb=kb, ko=ko, pss=pss, h=h, w=w, qn=qn: e.matmul(pss.ap[:, 0:w], lhsT=knT[kb].ap[:, h, ko:ko + 128], rhs=qn.ap[:, h, 0:w], start=True, stop=False),
                           reads=[knT[kb], qn], writes=[pss])
                    P.emit("pe", lambda e, kb=kb, ko=ko, pss=pss, h=h, hp=hp, w=w, qr=qr: e.matmul(pss.ap[:, 0:w], lhsT=krT[kb].ap[hp:hp + 64, ko:ko + 128], rhs=qr.ap[hp:hp + 64, h // 2, 0:w], start=False, stop=True),
                           reads=[krT[kb], qr], writes=[pss])
                    P.emit("act", lambda e, pss=pss, pp=pp, w=w: e.activation(out=pp.ap[:, 0:w], in_=pss.ap[:, 0:w], func=AF.Exp, scale=SCALE), reads=[pss], writes=[pp])
                    first, last = ki == 0, ki == len(ktiles) - 1
                    P.emit("pe", lambda e, kt=kt, pp=pp, h=h, w=w, first=first, last=last: e.matmul(po.ap[:, 0:w], lhsT=Vt[kt].ap[:, h * 128:(h + 1) * 128], rhs=pp.ap[:, 0:w], start=first, stop=last),
                           reads=[Vt[kt], pp], writes=[po])
                    P.emit("pe", lambda e, pp=pp, w=w, first=first, last=last: e.matmul(pd.ap[:, 0:w], lhsT=onesB, rhs=pp.ap[:, 0:w], start=first, stop=last),
                           reads=[cb, pp], writes=[pd])
                yt = yT[h % 2]
                P.emit("dve", lambda e, w=w: e.reciprocal(out=rden.ap[:, 0:w], in_=pd.ap[:, 0:w]), reads=[pd], writes=[rden])
                P.emit("dve", lambda e, w=w, yt=yt: e.tensor_tensor(out=yt.ap[:, 0:w], in0=po.ap[:, 0:w], in1=rden.ap[:, 0:w], op=ALU.mult), reads=[po, rden], writes=[yt])
                dma("pool", MT_d[h, :, c0:c0 + w], yt.ap[:, 0:w], [yt], [MTt[h][bj]])
        P.barrier()
        P.release(m_mla)

        phase_done('P3')
        m = P.mark()
        wfm = P.alloc([128, 8, 4 * 128], BF16, "wfm_ml")
        load_cast(wfm, wfm.ap, winfm_d[l][:, 5 * 128:9 * 128].rearrange("(k p) n -> p k n", p=128), 8, 512)
        wtm = P.alloc([128, 8, 784], BF16, "wtm")
        load_cast(wtm, wtm.ap, wintm_d[l].rearrange("(k p) n -> p k n", p=128), 8, 784)
        mqT = [P.alloc([128, 4, 512], BF16, f"mqT{j}") for j in range(9)]
        mkT = [P.alloc([128, 2, 512], BF16, f"mkT{j}") for j in range(9)]
        mk = [P.alloc([128, 256], BF16, f"mk{i}") for i in range(NTILE)]
        mvA = [P.alloc([128, 4, 66], BF16, f"mvA{i}") for i in range(NTILE)]
        moS = [P.alloc([128, 256], BF16, f"moS{i}") for i in range(NTILE)]
        ebt = [P.alloc([128, 8], F32, f"eb{i}") for i in range(NTILE)]
        eit = [P.alloc([128, 8], F32, f"ei{i}") for i in range(NTILE)]
        eblt = [P.alloc([128, 8], F32, f"ebl{i}") for i in range(NTILE)]
        hf = [P.alloc([128, 256], BF16, f"hf{i}") for i in range(NTILE)]
        m2 = P.mark()
        ublk = [P.alloc([128, 8, 512], BF16, "ublkm0")] * 2
        mlw = [(P.alloc([128, 16], F32, f"gt{i}"), P.alloc([128, 8], F32, f"lf{i}"), P.alloc([128, 8], F32, f"dd{i}"), P.alloc([128, 256], F32, f"tnh{i}")) for i in range(2)]
        for bj, (c0, w) in enumerate(BLOCKS):
            ub = ublk[bj % 2]
            dma("sp", ub.ap[:, :, 0:w], UT_d[:, :, c0:c0 + w].rearrange("k p n -> p k n"), [UTt[bj]], [ub])
            for ch in range(4):
                pt = ps[ch % 2]
                for k in range(8):
                    P.emit("pe", lambda e, k=k, ch=ch, pt=pt, w=w, ub=ub: e.matmul(pt.ap[:, 0:w], lhsT=wfm.ap[:, k, ch * 128:(ch + 1) * 128], rhs=ub.ap[:, k, 0:w], start=(k == 0), stop=(k == 7)),
                           reads=[wfm, ub], writes=[pt])
                if ch < 2:
                    if ch == 0:
                        P.emit("pool", lambda e, bj=bj: e.memset(mqT[bj].ap, 0.0), writes=[mqT[bj]])
                    for hh in range(2):
                        P.emit("act", lambda e, ch=ch, hh=hh, pt=pt, w=w, bj=bj: e.activation(out=mqT[bj].ap[64 * hh:64 * hh + 64, 2 * ch + hh, 0:w], in_=pt.ap[64 * hh:64 * hh + 64, 0:w], func=AF.Identity, scale=0.125), reads=[pt], writes=[mqT[bj]])
                else:
                    P.emit("act", lambda e, ch=ch, pt=pt, w=w, bj=bj: e.activation(out=mkT[bj].ap[:, ch - 2, 0:w], in_=pt.ap[:, 0:w], func=AF.Identity), reads=[pt], writes=[mkT[bj]])
            for tt in range(w // 128):
                ti = c0 // 128 + tt
                gt, lf, dd, tnh = mlw[ti % 2]
                pa, pb = ps[2 + tt % 2], ps[4 + tt % 2]
                for k in range(8):
                    P.emit("pe", lambda e, k=k, tt=tt, pa=pa, ub=ub: e.matmul(pa.ap, lhsT=ub.ap[:, k, tt * 128:(tt + 1) * 128], rhs=wtm.ap[:, k, 0:512], start=(k == 0), stop=(k == 7)),
                           reads=[ub, wtm], writes=[pa])
                for k in range(8):
                    P.emit("pe", lambda e, k=k, tt=tt, pb=pb, ub=ub: e.matmul(pb.ap[:, 0:272], lhsT=ub.ap[:, k, tt * 128:(tt + 1) * 128], rhs=wtm.ap[:, k, 512:784], start=(k == 0), stop=(k == 7)),
                           reads=[ub, wtm], writes=[pb])
                P.emit("dve", lambda e, ti=ti, pa=pa: e.tensor_copy(out=mk[ti].ap, in_=pa.ap[:, 0:256]), reads=[pa], writes=[mk[ti]])
                P.emit("pool", lambda e, ti=ti: e.memset(mvA[ti].ap, 1.0), writes=[mvA[ti]])
                P.emit("dve", lambda e, ti=ti, pa=pa: e.tensor_copy(out=mvA[ti].ap[:, :, 0:64], in_=pa.ap[:, 256:512].rearrange("p (h d) -> p h d", h=4)), reads=[pa], writes=[mvA[ti]])
                sigmoid_ops(P, tnh, tnh.ap, pb.ap[:, 0:256], [pb], eng2="pool")
                P.emit("pool", lambda e, ti=ti: e.tensor_copy(out=moS[ti].ap, in_=tnh.ap), reads=[tnh], writes=[moS[ti]])
                P.emit("dve", lambda e, pb=pb: e.tensor_tensor(out=gt.ap, in0=pb.ap[:, 256:272], in1=gbias.ap[:, l, :], op=ALU.add), reads=[pb, gbias], writes=[gt])
                gv = gt.ap.rearrange("p (a b) -> p a b", a=4)
                lfv = lf.ap.rearrange("p (a b) -> p a b", a=2)
                P.emit("act", lambda e, gv=gv, lfv=lfv: e.activation(out=lfv, in_=gv[:, 1::2, :], func=AF.Exp, scale=-1.0), reads=[gt], writes=[lf])
                P.emit("act", lambda e: e.activation(out=lf.ap, in_=lf.ap, func=AF.Ln, bias=1.0), reads=[lf], writes=[lf])
                P.emit("dve", lambda e: e.tensor_scalar(out=lf.ap, in0=lf.ap, scalar1=-1.0, scalar2=None, op0=ALU.mult), reads=[lf], writes=[lf])
                pc = ps[6 + ti % 2]
                P.emit("pe", lambda e, pc=pc: e.matmul(pc.ap[:, 0:4], lhsT=maskU, rhs=lf.ap[:, 0:4], start=True, stop=True), reads=[consts, lf], writes=[pc])
                P.emit("pe", lambda e, pc=pc: e.matmul(pc.ap[:, 4:8], lhsT=maskL, rhs=lf.ap[:, 4:8], start=True, stop=True), reads=[consts, lf], writes=[pc])
                P.emit("pe", lambda e, pc=pc: e.matmul(pc.ap[:, 8:16], lhsT=onesF, rhs=lf.ap[:, 0:8], start=True, stop=True), reads=[consts, lf], writes=[pc])
                P.emit("act", lambda e, ti=ti, pc=pc: e.activation(out=ebt[ti].ap, in_=pc.ap[:, 0:8], func=AF.Exp), reads=[pc], writes=[ebt[ti]])
                P.emit("act", lambda e, ti=ti, pc=pc: e.activation(out=eblt[ti].ap, in_=pc.ap[:, 8:16], func=AF.Exp), reads=[pc], writes=[eblt[ti]])
                ddv = dd.ap.rearrange("p (a b) -> p a b", a=2)
                P.emit("dve", lambda e, pc=pc, gv=gv, ddv=ddv: e.tensor_tensor(out=ddv, in0=gv[:, 0::2, :], in1=pc.ap[:, 0:8].rearrange("p (a b) -> p a b", a=2), op=ALU.subtract),
                       reads=[gt, pc], writes=[dd])
                P.emit("act", lambda e, ti=ti: e.activation(out=eit[ti].ap, in_=dd.ap, func=AF.Exp), reads=[dd], writes=[eit[ti]])
        P.barrier()
        P.release(m2)
        phase_done('P4a')
        ptT = [[P.alloc([128, 4, 128], BF16, f"ptT{d}{i}") for i in range(2)] for d in range(2)]
        ktp = [[P.alloc([128, 4, 128], BF16, f"ktp{d}{i}") for i in range(2)] for d in range(2)]
        Cst = [P.alloc([128, 2, 65], F32, f"Cst{d}") for d in range(2)]
        Cbf = [P.alloc([128, 2, 66], BF16, f"Cbf{d}") for d in range(2)]
        ctmp = P.alloc([128, 2, 65], F32, "ctmp")
        den4 = P.alloc([128, 4], F32, "den4")
        r4 = P.alloc([128, 4], F32, "r4")
        hb = P.alloc([128, 256], F32, "hb")
        hs = P.alloc([128, 256], F32, "hs")
        hq = P.alloc([128, 256], F32, "hq")
        ss4 = P.alloc([128, 4], F32, "ss4")
        ybf = P.alloc([128, 256], BF16, "ybf")
        ybT = [P.alloc([128, 2, 512], BF16, f"ybT{i}") for i in range(2)]
        for d in range(2):
            for i in range(2):
                P.emit("pool", lambda e, d=d, i=i: e.memset(ktp[d][i].ap, 0.0), writes=[ktp[d][i]])
        fwd_order = [32, 33] + list(range(32))
        bwd_order = [33, 32] + list(range(31, -1, -1))
        ybcount = {}
        import os as _os
        _ns = int(_os.environ.get('DBG_STEPS', '99'))
        _nd = int(_os.environ.get('DBG_DIRS', '2'))
        _lvl = int(_os.environ.get('DBG_LVL', '99'))
        _skip = _os.environ.get('DBG_SKIP', '')
        for d, order in ((0, fwd_order[:_ns]), (1, bwd_order[:_ns]))[:_nd]:
            mask = maskU if d == 0 else maskL
            for step, ti in enumerate(order):
                bj, to = (8, (ti - 32) * 128) if ti >= 32 else (ti // 4, (ti % 4) * 128)
                if step == 0:
                    P.emit("pool", lambda e, d=d: e.memset(Cst[d].ap, 0.0), writes=[Cst[d]])
                    P.emit("pool", lambda e, d=d: e.memset(Cbf[d].ap, 0.0), writes=[Cbf[d]])
                pS, pO, pKV = ps[step % 2], ps[2 + step % 2], ps[4 + step % 2]
                pt_, kt_ = ptT[d][step % 2], ktp[d][step % 2]
                for h in range(4):
                    hp, mch = 64 * (h % 2), h // 2
                    if 'S' in _skip or ('E' in _skip and h % 2 == 1) or ('O' in _skip and h % 2 == 0):
                        continue
                    P.emit("pe", lambda e, h=h, hp=hp, mch=mch, bj=bj, to=to, pS=pS: e.matmul(pS.ap[:, h * 128:(h + 1) * 128], lhsT=mkT[bj].ap[:, mch, to:to + 128], rhs=mqT[bj].ap[:, h, to:to + 128], start=True, stop=True),
                           reads=[mkT[bj], mqT[bj]], writes=[pS])
                for h in range(4):
                    hp = 64 * (h % 2)
                    if 'P' not in _skip:
                      P.emit("dve", lambda e, h=h, ti=ti, d=d, pS=pS, pt_=pt_, mask=mask: e.scalar_tensor_tensor(out=pt_.ap[:, h, :], in0=pS.ap[:, h * 128:(h + 1) * 128], scalar=eit[ti].ap[:, d * 4 + h:d * 4 + h + 1], in1=mask, op0=ALU.mult, op1=ALU.mult),
                             reads=[pS, eit[ti], consts], writes=[pt_])
                    if "K" in _skip:
                        continue
                    P.emit("pool", lambda e, h=h, hp=hp, ti=ti, d=d, kt_=kt_: e.tensor_scalar(out=kt_.ap[:, h, hp:hp + 64], in0=mk[ti].ap[:, h * 64:(h + 1) * 64], scalar1=eit[ti].ap[:, d * 4 + h:d * 4 + h + 1], scalar2=None, op0=ALU.mult),
                           reads=[mk[ti], eit[ti]], writes=[kt_])
                if _lvl < 2:
                    continue
                for h in range(4):
                    hp, mch = 64 * (h % 2), h // 2
                    P.emit("pe", lambda e, h=h, ti=ti, pO=pO, pt_=pt_: e.matmul(pO.ap[:, h * 65:(h + 1) * 65], lhsT=pt_.ap[:, h, :], rhs=mvA[ti].ap[:, h, 0:65], start=True, stop=False),
                           reads=[pt_, mvA[ti]], writes=[pO])
                    P.emit("pe", lambda e, h=h, hp=hp, mch=mch, bj=bj, to=to, d=d, pO=pO: e.matmul(pO.ap[:, h * 65:(h + 1) * 65], lhsT=mqT[bj].ap[:, h, to:to + 128], rhs=Cbf[d].ap[:, mch, 0:65], start=False, stop=True),
                           reads=[mqT[bj], Cbf[d]], writes=[pO])
                for mch in range(2):
                    for hh in range(2):
                        h = mch * 2 + hh
                        P.emit("pe", lambda e, h=h, mch=mch, hh=hh, ti=ti, pKV=pKV, kt_=kt_: e.matmul(pKV.ap[:, mch * 65:(mch + 1) * 65], lhsT=kt_.ap[:, h, :], rhs=mvA[ti].ap[:, h, 0:65], start=(hh == 0), stop=(hh == 1)),
                               reads=[kt_, mvA[ti]], writes=[pKV])
                if _lvl < 3:
                    continue
                Ov = pO.ap[:, 0:260].rearrange("p (h c) -> p h c", h=4)
                ebv = ebt[ti].ap[:, d * 4:(d + 1) * 4]
                P.emit("dve", lambda e, Ov=Ov, ebv=ebv: e.tensor_tensor(out=den4.ap, in0=Ov[:, :, 64], in1=ebv, op=ALU.mult), reads=[pO, ebt[ti]], writes=[den4])
                P.emit("act", lambda e: e.activation(out=den4.ap, in_=den4.ap, func=AF.Abs), reads=[den4], writes=[den4])
                P.emit("dve", lambda e: e.tensor_scalar(out=den4.ap, in0=den4.ap, scalar1=1.0, scalar2=None, op0=ALU.max), reads=[den4], writes=[den4])
                P.emit("dve", lambda e: e.reciprocal(out=den4.ap, in_=den4.ap), reads=[den4], writes=[den4])
                P.emit("dve", lambda e, ebv=ebv: e.tensor_tensor(out=r4.ap, in0=ebv, in1=den4.ap, op=ALU.mult), reads=[ebt[ti], den4], writes=[r4])
                hdst = hf[ti] if d == 0 else hb
                P.emit("dve", lambda e, Ov=Ov, hdst=hdst: e.tensor_tensor(out=hdst.ap.rearrange("p (h c) -> p h c", h=4), in0=Ov[:, :, 0:64], in1=bc_last(r4.ap, 64), op=ALU.mult),
                       reads=[pO, r4], writes=[hdst])
                if _lvl < 4:
                    continue
                KVv = pKV.ap[:, 0:130].rearrange("p (m c) -> p m c", m=2)
                P.emit("dve", lambda e, KVv=KVv, d=d: e.tensor_tensor(out=ctmp.ap, in0=KVv, in1=Cst[d].ap, op=ALU.add), reads=[pKV, Cst[d]], writes=[ctmp])
                for mch in range(2):
                    for hh in range(2):
                        h = mch * 2 + hh
                        hp = 64 * hh
                        P.emit("dve", lambda e, h=h, hp=hp, mch=mch, ti=ti, d=d: e.tensor_scalar(out=Cst[d].ap[hp:hp + 64, mch, :], in0=ctmp.ap[hp:hp + 64, mch, :], scalar1=eblt[ti].ap[hp:hp + 64, d * 4 + h:d * 4 + h + 1], scalar2=None, op0=ALU.mult),
                               reads=[ctmp, eblt[ti]], writes=[Cst[d]])
                P.emit("act", lambda e, d=d: e.activation(out=Cbf[d].ap[:, :, 0:65], in_=Cst[d].ap, func=AF.Identity), reads=[Cst[d]], writes=[Cbf[d]])
                if _lvl < 5:
                    continue
                if d == 1:
                    P.emit("pool", lambda e, ti=ti: e.tensor_tensor(out=hs.ap, in0=hf[ti].ap, in1=hb.ap, op=ALU.add), reads=[hf[ti], hb], writes=[hs])
                    P.emit("act", lambda e: e.activation(out=hq.ap, in_=hs.ap, func=AF.Square), reads=[hs], writes=[hq])
                    P.emit("dve", lambda e: e.reduce_sum(out=ss4.ap, in_=hq.ap.rearrange("p (h c) -> p h c", h=4), axis=AX.X), reads=[hq], writes=[ss4])
                    rsqrt_ops(P, ss4, ss4.ap, ss4, ss4.ap, 64 * EPS)
                    P.emit("dve", lambda e: e.tensor_tensor(out=hq.ap.rearrange("p (h c) -> p h c", h=4), in0=hs.ap.rearrange("p (h c) -> p h c", h=4), in1=bc_last(ss4.ap, 64), op=ALU.mult),
                           reads=[hs, ss4], writes=[hq])
                    P.emit("dve", lambda e, ti=ti: e.scalar_tensor_tensor(out=ybf.ap, in0=hq.ap, scalar=8.0, in1=moS[ti].ap, op0=ALU.mult, op1=ALU.mult), reads=[hq, moS[ti]], writes=[ybf])
                    ptr = ps[6 + step % 2]
                    pv = psb(ptr)
                    for c in range(2):
                        P.emit("pe", lambda e, c=c, pv=pv, ptr=ptr: e.transpose(pv[:, c * 128:(c + 1) * 128], ybf.ap[:, c * 128:(c + 1) * 128], identB), reads=[ybf, cb], writes=[ptr])
                    yb = ybT[bj % 2]
                    P.emit("act", lambda e, pv=pv, yb=yb, to=to, ptr=ptr: e.activation(out=yb.ap[:, :, to:to + 128], in_=pv[:, 0:256].rearrange("p (c n) -> p c n", c=2), func=AF.Identity), reads=[ptr], writes=[yb])
                    ybcount[bj] = ybcount.get(bj, 0) + 1
                    if ybcount[bj] == BLOCKS[bj][1] // 128:
                        c0, w = BLOCKS[bj]
                        for c in range(2):
                            dma("pool", MT_d[4 + c, :, c0:c0 + w], yb.ap[:, c, 0:w], [yb], [MTt[4 + c][bj]])
        P.barrier()
        P.release(m)

        phase_done('P4')
        m = P.mark()
        wbd = P.alloc([128, 8, 128], BF16, "wbd")
        load_cast(wbd, wbd.ap, wbd_d[l], 8, 128)
        wfm = P.alloc([128, 8, 4 * 128], BF16, "wfm_lru")
        load_cast(wfm, wfm.ap, winfm_d[l][:, 9 * 128:13 * 128].rearrange("(k p) n -> p k n", p=128), 8, 512)
        ublk = [P.alloc([128, 8, 512], BF16, f"ublkl{i}") for i in range(2)]
        xb = P.alloc([128, NT], F32, "xb")
        gb = P.alloc([128, NT], BF16, "gb")
        xs = P.alloc([128, NT], F32, "xs")
        xsb = P.alloc([128, NT], BF16, "xsb")
        At = P.alloc([128, NT], F32, "At")
        Ut = P.alloc([128, NT], F32, "Ut")
        Hf = P.alloc([128, NT], F32, "Hf")
        Hb = P.alloc([128, NT], F32, "Hb")
        tq = [P.alloc([128, 512], F32, f"tq{i}") for i in range(4)]
        ycb = [P.alloc([128, 512], BF16, f"ycb{i}") for i in range(2)]
        segs = [(0, NLAT), (NLAT, NCTX)]
        for c in range(2):
            for bj, (c0, w) in enumerate(BLOCKS):
                ub = ublk[bj % 2]
                dma("sp", ub.ap[:, :, 0:w], UT_d[:, :, c0:c0 + w].rearrange("k p n -> p k n"), [UTt[bj]], [ub])
                for which, dst in ((0, xb), (1, gb)):
                    pt = ps[(2 * bj + which) % 4]
                    ch = which * 2 + c
                    for k in range(8):
                        P.emit("pe", lambda e, k=k, ch=ch, pt=pt, w=w, ub=ub: e.matmul(pt.ap[:, 0:w], lhsT=wfm.ap[:, k, ch * 128:(ch + 1) * 128], rhs=ub.ap[:, k, 0:w], start=(k == 0), stop=(k == 7)),
                               reads=[wfm, ub], writes=[pt])
                    P.emit("act", lambda e, pt=pt, w=w, c0=c0, dst=dst: e.activation(out=dst.ap[:, c0:c0 + w], in_=pt.ap[:, 0:w], func=AF.Identity), reads=[pt], writes=[dst])
            cw = lambda j: sm[:, 3 + c * 4 + j: 4 + c * 4 + j]
            for (s0, sl) in segs:
                P.emit("dve", lambda e, s0=s0, sl=sl: e.tensor_scalar(out=xs.ap[:, s0:s0 + sl], in0=xb.ap[:, s0:s0 + sl], scalar1=cw(2), scalar2=sm[:, 11 + c:12 + c], op0=ALU.mult, op1=ALU.add),
                       reads=[xb, small], writes=[xs])
                P.emit("dve", lambda e, s0=s0, sl=sl: e.scalar_tensor_tensor(out=xs.ap[:, s0 + 2:s0 + sl], in0=xb.ap[:, s0:s0 + sl - 2], scalar=cw(0), in1=xs.ap[:, s0 + 2:s0 + sl], op0=ALU.mult, op1=ALU.add),
                       reads=[xb, xs, small], writes=[xs])
                P.emit("dve", lambda e, s0=s0, sl=sl: e.scalar_tensor_tensor(out=xs.ap[:, s0 + 1:s0 + sl], in0=xb.ap[:, s0:s0 + sl - 1], scalar=cw(1), in1=xs.ap[:, s0 + 1:s0 + sl], op0=ALU.mult, op1=ALU.add),
                       reads=[xb, xs, small], writes=[xs])
                P.emit("dve", lambda e, s0=s0, sl=sl: e.scalar_tensor_tensor(out=xs.ap[:, s0:s0 + sl - 1], in0=xb.ap[:, s0 + 1:s0 + sl], scalar=cw(3), in1=xs.ap[:, s0:s0 + sl - 1], op0=ALU.mult, op1=ALU.add),
                       reads=[xb, xs, small], writes=[xs])
            P.emit("pool", lambda e: e.tensor_copy(out=xsb.ap, in_=xs.ap), reads=[xs], writes=[xsb])
            for d in range(2):
                Hd = Hf if d == 0 else Hb
                for bj, (c0, w) in enumerate(BLOCKS):
                    pa, px = ps[(2 * bj) % 4 + 4 * 0], ps[(2 * bj + 1) % 4]
                    ia, ix = (d * 2 + 0) * 2 + c, (d * 2 + 1) * 2 + c
                    P.emit("pe", lambda e, ia=ia, pa=pa, c0=c0, w=w: e.matmul(pa.ap[:, 0:w], lhsT=wbd.ap[:, ia, :], rhs=xsb.ap[:, c0:c0 + w], start=True, stop=True), reads=[wbd, xsb], writes=[pa])
                    P.emit("pe", lambda e, ix=ix, px=px, c0=c0, w=w: e.matmul(px.ap[:, 0:w], lhsT=wbd.ap[:, ix, :], rhs=xsb.ap[:, c0:c0 + w], start=True, stop=True), reads=[wbd, xsb], writes=[px])
                    ta, tx = tq[2 * (bj % 2)], tq[2 * (bj % 2) + 1]
                    dc = d * 2 + c
                    sigmoid_ops(P, ta, ta.ap[:, 0:w], pa.ap[:, 0:w], [pa, small], bias=sm[:, 28 + dc:29 + dc], eng2="pool")
                    P.emit("act", lambda e, w=w, dc=dc, c0=c0: e.activation(out=At.ap[:, c0:c0 + w], in_=ta.ap[:, 0:w], func=AF.Exp, scale=sm[:, 36 + dc:37 + dc]), reads=[ta, small], writes=[At])
                    sigmoid_ops(P, tx, tx.ap[:, 0:w], px.ap[:, 0:w], [px, small], bias=sm[:, 32 + dc:33 + dc], eng2="pool")
                    P.emit("pool", lambda e, w=w, c0=c0: e.tensor_tensor(out=tx.ap[:, 0:w], in0=tx.ap[:, 0:w], in1=xs.ap[:, c0:c0 + w], op=ALU.mult), reads=[tx, xs], writes=[tx])
                    P.emit("pool", lambda e, w=w, c0=c0: e.tensor_tensor(out=ta.ap[:, 0:w], in0=At.ap[:, c0:c0 + w], in1=At.ap[:, c0:c0 + w], op=ALU.mult), reads=[At], writes=[ta])
                    P.emit("dve", lambda e, w=w: e.tensor_scalar(out=ta.ap[:, 0:w], in0=ta.ap[:, 0:w], scalar1=-1.0, scalar2=1.0, op0=ALU.mult, op1=ALU.add), reads=[ta], writes=[ta])
                    P.emit("act", lambda e, w=w: e.activation(out=ta.ap[:, 0:w], in_=ta.ap[:, 0:w], func=AF.Ln), reads=[ta], writes=[ta])
                    P.emit("act", lambda e, w=w: e.activation(out=ta.ap[:, 0:w], in_=ta.ap[:, 0:w], func=AF.Exp, scale=0.5), reads=[ta], writes=[ta])
                    P.emit("dve", lambda e, w=w, c0=c0: e.tensor_tensor(out=Ut.ap[:, c0:c0 + w], in0=ta.ap[:, 0:w], in1=tx.ap[:, 0:w], op=ALU.mult), reads=[ta, tx], writes=[Ut])
                SC = 1024
                if d == 0:
                    pieces = [(NLAT, NCTX)] + [(i * SC, SC) for i in range(NLAT // SC)]
                else:
                    pieces = [(NLAT, NCTX)] + [(i * SC, SC) for i in range(NLAT // SC - 1, -1, -1)]
                prev_last = None
                for (p0, pl) in pieces:
                    def view(t, p0=p0, pl=pl):
                        a = t.ap[:, p0:p0 + pl]
                        if d == 0:
                            return a
                        return AP(a.tensor, a.offset + pl - 1, [list(a.ap[0]), [-1, pl]])
                    init = 0.0 if prev_last is None else Hd.ap[:, prev_last:prev_last + 1]
                    P.emit("dve", lambda e, view=view, init=init, Hd=Hd: e.tensor_tensor_scan(out=view(Hd), data0=view(At), data1=view(Ut), initial=init, op0=ALU.mult, op1=ALU.add),
                           reads=[At, Ut, Hd], writes=[Hd])
                    prev_last = (p0 + pl - 1) if d == 0 else p0
            for bj, (c0, w) in enumerate(BLOCKS):
                ta, tx = tq[2 * (bj % 2)], tq[2 * (bj % 2) + 1]
                yc = ycb[bj % 2]
                g = gb.ap[:, c0:c0 + w]
                P.emit("pool", lambda e, g=g, w=w: e.tensor_tensor(out=ta.ap[:, 0:w], in0=g, in1=g, op=ALU.mult), reads=[gb], writes=[ta])
                P.emit("dve", lambda e, w=w: e.tensor_scalar(out=ta.ap[:, 0:w], in0=ta.ap[:, 0:w], scalar1=0.044715 * 0.7978845608028654, scalar2=0.7978845608028654, op0=ALU.mult, op1=ALU.add), reads=[ta], writes=[ta])
                P.emit("dve", lambda e, g=g, w=w: e.tensor_tensor(out=ta.ap[:, 0:w], in0=ta.ap[:, 0:w], in1=g, op=ALU.mult), reads=[ta, gb], writes=[ta])
                sigmoid_ops(P, ta, ta.ap[:, 0:w], ta.ap[:, 0:w], [ta], scale=2.0, eng2="pool")
                P.emit("dve", lambda e, g=g, w=w: e.tensor_tensor(out=ta.ap[:, 0:w], in0=ta.ap[:, 0:w], in1=g, op=ALU.mult), reads=[ta, gb], writes=[ta])
                P.emit("pool", lambda e, w=w, c0=c0: e.tensor_tensor(out=tx.ap[:, 0:w], in0=Hf.ap[:, c0:c0 + w], in1=Hb.ap[:, c0:c0 + w], op=ALU.add), reads=[Hf, Hb], writes=[tx])
                P.emit("dve", lambda e, w=w, yc=yc: e.tensor_tensor(out=yc.ap[:, 0:w], in0=ta.ap[:, 0:w], in1=tx.ap[:, 0:w], op=ALU.mult), reads=[ta, tx], writes=[yc])
                dma("pool", MT_d[6 + c, :, c0:c0 + w], yc.ap[:, 0:w], [yc], [MTt[6 + c][bj]])
        P.barrier()
        P.release(m)
        if debug and l == 0:
            mm = P.mark()
            dtile = [P.alloc([128, 8, 512], BF16, f"dbgm{i}") for i in range(2)]
            for bj, (c0, w) in enumerate(BLOCKS):
                dt_ = dtile[bj % 2]
                dma("sp", dt_.ap[:, :, 0:w], MT_d[:, :, c0:c0 + w].rearrange("k p n -> p k n"), [MTt[k][bj] for k in range(8)], [dt_])
                dma("pool", dbg["MT"][:, :, c0:c0 + w].rearrange("k p n -> p k n"), dt_.ap[:, :, 0:w], [dt_], [])
            P.barrier()
            P.release(mm)

        phase_done('P5')
        m = P.mark()
        wo = P.alloc([128, 8, D], BF16, "wo")
        load_cast(wo, wo.ap, wout_d[l].rearrange("(k p) n -> p k n", p=128), 8, D)
        mixb = [P.alloc([128, 8, 512], BF16, f"mixb{i}") for i in range(2)]
        xts = [P.alloc([128, D], F32, f"xto{i}") for i in range(2)]
        x1s = [P.alloc([128, D], F32, f"x1{i}") for i in range(2)]
        tmpo = P.alloc([128, D], F32, "tmpo")
        lnw = [(P.alloc([128, D], F32, f"sq2{i}"), P.alloc([128, 1], F32, f"ssum2{i}"), P.alloc([128, 1], F32, f"rs2{i}"), P.alloc([128, D], BF16, f"xn2{i}")) for i in range(2)]
        uTb = [P.alloc([128, 8, 512], BF16, f"uTb2{i}") for i in range(2)]
        for bj, (c0, w) in enumerate(BLOCKS):
            mb = mixb[bj % 2]
            ub = uTb[bj % 2]
            col = 1 if bj == 8 else 0
            dma("sp", mb.ap[:, :, 0:w], MT_d[:, :, c0:c0 + w].rearrange("k p n -> p k n"), [MTt[k][bj] for k in range(8)], [mb])
            for tt in range(w // 128):
                ti = c0 // 128 + tt
                xt, x1 = xts[ti % 2], x1s[ti % 2]
                dma("sp", xt.ap, xsrc(l)[ti * 128:(ti + 1) * 128, :], [Xt[ti]], [xt])
                for half in range(2):
                    pt = ps[2 + half]
                    for k in range(8):
                        P.emit("pe", lambda e, k=k, half=half, pt=pt, tt=tt, mb=mb: e.matmul(pt.ap, lhsT=mb.ap[:, k, tt * 128:(tt + 1) * 128], rhs=wo.ap[:, k, half * 512:(half + 1) * 512], start=(k == 0), stop=(k == 7)),
                               reads=[mb, wo], writes=[pt])
                    P.emit("dve", lambda e, half=half, pt=pt, col=col: e.tensor_tensor(out=tmpo.ap[:, half * 512:(half + 1) * 512], in0=pt.ap, in1=gbt.ap[:, col, half * 512:(half + 1) * 512], op=ALU.mult),
                           reads=[pt, gbt], writes=[tmpo])
                P.emit("pool", lambda e, xt=xt, x1=x1: e.tensor_tensor(out=x1.ap, in0=tmpo.ap, in1=xt.ap, op=ALU.add), reads=[tmpo, xt], writes=[x1])
                dma("pool", X_d[ti * 128:(ti + 1) * 128, :], x1.ap, [x1], [Xt[ti]])
                ln_tile(x1, ub, ub.ap, tt * 128, 24, 32, col, lnw[ti % 2] + (ps[ti % 2],))
            dma("pool", UT_d[:, :, c0:c0 + w].rearrange("k p n -> p k n"), ub.ap[:, :, 0:w], [ub], [UTt[bj]])
        P.barrier()
        P.release(m)

        phase_done('P6')
        last_layer = (l == L - 1)
        for hhalf in range(2):
            m = P.mark()
            w1 = P.alloc([128, 8, 2048], BF16, "w1")
            load_cast(w1, w1.ap, wff1_d[l][:, hhalf * 2048:(hhalf + 1) * 2048].rearrange("(k p) n -> p k n", p=128), 8, 2048)
            w2 = P.alloc([128, 16, D], BF16, "w2")
            load_cast(w2, w2.ap, wff2_d[l][hhalf * 2048:(hhalf + 1) * 2048, :].rearrange("(k p) n -> p k n", p=128), 16, D)
            ublk = [P.alloc([128, 8, 512], BF16, f"ublkf{i}") for i in range(2)]
            hT = [P.alloc([128, 16, 512], BF16, f"hT{i}") for i in range(2)]
            xts = [P.alloc([128, D], F32, f"xtf{i}") for i in range(2)]
            x2s = [P.alloc([128, D], F32, f"x2{i}") for i in range(2)]
            rtmp = [P.alloc([128, 512], F32, f"rtmp{i}") for i in range(2)]
            fg = None
            if last_layer and hhalf == 1:
                fg = P.alloc([128, D], F32, "fg")
                dma("sp", fg.ap, fing_d, [], [fg])
                sq = P.alloc([128, D], F32, "sq3")
                ssum = P.alloc([128, 1], F32, "ssum3")
                rs = P.alloc([128, 1], F32, "rs3")
                xo = [P.alloc([128, D], F32, f"xo{i}") for i in range(2)]
            for bj, (c0, w) in enumerate(BLOCKS):
                if last_layer and bj == 8:
                    continue
                ub = ublk[bj % 2]
                ht = hT[bj % 2]
                col = 1 if bj == 8 else 0
                dma("sp", ub.ap[:, :, 0:w], UT_d[:, :, c0:c0 + w].rearrange("k p n -> p k n"), [UTt[bj]], [ub])
                for j in range(16):
                    pt = ps[j % 2]
                    for k in range(8):
                        P.emit("pe", lambda e, k=k, j=j, pt=pt, w=w, ub=ub: e.matmul(pt.ap[:, 0:w], lhsT=w1.ap[:, k, j * 128:(j + 1) * 128], rhs=ub.ap[:, k, 0:w], start=(k == 0), stop=(k == 7)),
                               reads=[w1, ub], writes=[pt])
                    rt = rtmp[j % 2]
                    P.emit("act", lambda e, pt=pt, w=w, rt=rt: e.activation(out=rt.ap[:, 0:w], in_=pt.ap[:, 0:w], func=AF.Relu), reads=[pt], writes=[rt])
                    P.emit("dve" if j % 2 == 0 else "pool", lambda e, j=j, w=w, ht=ht, rt=rt: e.tensor_tensor(out=ht.ap[:, j, 0:w], in0=rt.ap[:, 0:w], in1=rt.ap[:, 0:w], op=ALU.mult), reads=[rt], writes=[ht])
                for tt in range(w // 128):
                    ti = c0 // 128 + tt
                    xt, x2 = xts[ti % 2], x2s[ti % 2]
                    dma("sp", xt.ap, X_d[ti * 128:(ti + 1) * 128, :], [Xt[ti]], [xt])
                    for half in range(2):
                        pt = ps[2 + half + 2 * (ti % 2)]
                        for k in range(16):
                            P.emit("pe", lambda e, k=k, half=half, pt=pt, tt=tt, ht=ht: e.matmul(pt.ap, lhsT=ht.ap[:, k, tt * 128:(tt + 1) * 128], rhs=w2.ap[:, k, half * 512:(half + 1) * 512], start=(k == 0), stop=(k == 15)),
                                   reads=[ht, w2], writes=[pt])
                        P.emit("dve", lambda e, half=half, pt=pt, col=col, x2=x2: e.tensor_tensor(out=x2.ap[:, half * 512:(half + 1) * 512], in0=pt.ap, in1=gbt.ap[:, 2 + col, half * 512:(half + 1) * 512], op=ALU.mult),
                               reads=[pt, gbt], writes=[x2])
                    P.emit("pool", lambda e, xt=xt, x2=x2: e.tensor_tensor(out=x2.ap, in0=x2.ap, in1=xt.ap, op=ALU.add), reads=[x2, xt], writes=[x2])
                    if fg is None:
                        dma("pool", X_d[ti * 128:(ti + 1) * 128, :], x2.ap, [x2], [Xt[ti]])
                    else:
                        o = xo[ti % 2]
                        P.emit("act", lambda e, x2=x2: e.activation(out=sq.ap, in_=x2.ap, func=AF.Square), reads=[x2], writes=[sq])
                        P.emit("dve", lambda e: e.reduce_sum(out=ssum.ap, in_=sq.ap, axis=AX.X), reads=[sq], writes=[ssum])
                        rsqrt_ops(P, rs, ssum.ap, ssum, rs.ap, D * EPS)
                        P.emit("dve", lambda e, x2=x2, o=o: e.tensor_scalar(out=o.ap, in0=x2.ap, scalar1=rs.ap[:, 0:1], scalar2=32.0, op0=ALU.mult, op1=ALU.mult), reads=[x2, rs], writes=[o])
                        P.emit("pool", lambda e, o=o: e.tensor_tensor(out=o.ap, in0=o.ap, in1=fg.ap, op=ALU.mult), reads=[o, fg], writes=[o])
                        dma("pool", out_d[ti * 128:(ti + 1) * 128, :], o.ap, [o], [OUTt[ti]])
            P.barrier()
            P.release(m)
        if debug and l == 0:
            mm = P.mark()
            dx = [P.alloc([128, D], F32, f"dbgx{i}") for i in range(2)]
            for ti in range(NTILE):
                dma("sp", dx[ti % 2].ap, X_d[ti * 128:(ti + 1) * 128, :], [Xt[ti]], [dx[ti % 2]])
                dma("pool", dbg["X"][ti * 128:(ti + 1) * 128, :], dx[ti % 2].ap, [dx[ti % 2]], [])
            P.barrier()
            P.release(mm)

      except _Stop as ex:
        print('build stopped after phase', ex)
        if debug:
            P.barrier()
            P.sb_off = 16512
            dtile = [P.alloc([128, 8, 512], BF16, f"dbgs{i}") for i in range(2)]
            for bj, (c0, w) in enumerate(BLOCKS):
                dt_ = dtile[bj % 2]
                dma("sp", dt_.ap[:, :, 0:w], MT_d[:, :, c0:c0 + w].rearrange("k p n -> p k n"), [MTt[k][bj] for k in range(8)], [dt_])
                dma("pool", dbg["MT"][:, :, c0:c0 + w].rearrange("k p n -> p k n"), dt_.ap[:, :, 0:w], [dt_], [])
        break

    P.barrier()
    P.finalize()
    return nc


def _rope_tables():
    t = np.arange(NLAT)
    row = (t // 64).astype(np.float32)
    colp = (t % 64).astype(np.float32)
    half = 32
    freqs = (1.0 / (10000.0 ** (np.arange(0, half, 2, dtype=np.float32) / half))).astype(np.float32)
    ang = np.concatenate([row[:, None] * freqs, colp[:, None] * freqs], axis=-1)
    cos, sin = np.cos(ang).astype(np.float32), np.sin(ang).astype(np.float32)
    C = np.ones((128, NT), np.float32)
    S = np.zeros((128, NT), np.float32)
    for r in range(128):
        rr = r % 64
        j = rr % 32
        C[r, :NLAT] = cos[:, j]
        S[r, :NLAT] = (-sin[:, j]) if rr < 32 else sin[:, j]
    return C, S


def _prep_shared(inp, L):
    f = lambda a: np.ascontiguousarray(np.asarray(a, dtype=np.float32))
    w_in = f(inp["w_in"])[:L]
    sw64 = np.concatenate([np.arange(32, 64), np.arange(0, 32)])
    kr = 384 + np.arange(64)
    fm_cols = np.concatenate([
        np.arange(0, 256), np.arange(256, 384), kr, kr, kr[sw64], kr[sw64],
        448 + np.arange(256), 704 + np.arange(256), 1488 + np.arange(256), 1744 + np.arange(256)])
    tm_cols = np.concatenate([704 + np.arange(256), 960 + np.arange(256), 1216 + np.arange(256), 1472 + np.arange(16)])
    w_uq = f(inp["mla_w_uq"])[:L]
    uq_cols = []
    for h in range(4):
        uq_cols.append(h * 192 + np.arange(128))
    rope = lambda h: h * 192 + 128 + np.arange(64)
    uq_cols += [rope(0), rope(1), rope(2), rope(3), rope(0)[sw64], rope(1)[sw64], rope(2)[sw64], rope(3)[sw64]]
    uq_cols = np.concatenate(uq_cols)
    w_ukv = f(inp["mla_w_ukv"])[:L]
    kcols = np.concatenate([h * 256 + np.arange(128) for h in range(4)])
    vcols = np.concatenate([h * 256 + 128 + np.arange(128) for h in range(4)])
    wa, wx = f(inp["lru_w_a"])[:L], f(inp["lru_w_x"])[:L]
    wbd = np.zeros((L, 128, 8, 128), np.float32)
    for d in range(2):
        for ax, wsrc in enumerate((wa, wx)):
            for c in range(2):
                idx = (d * 2 + ax) * 2 + c
                for g in range(2):
                    wbd[:, g * 64:(g + 1) * 64, idx, g * 64:(g + 1) * 64] = wsrc[:, d, 2 * c + g]
    small = np.zeros((128, L, 40), np.float32)
    gq, gkv = f(inp["mla_g_q"])[:L], f(inp["mla_g_kv"])[:L]
    cwv, cbv = f(inp["lru_conv_w"])[:L], f(inp["lru_conv_b"])[:L]
    ba, bx, lam = f(inp["lru_b_a"])[:L], f(inp["lru_b_x"])[:L], f(inp["lru_lam"])[:L]
    for l in range(L):
        small[:, l, 0:2] = gq[l].reshape(2, 128).T
        small[:, l, 2] = gkv[l]
        for c in range(2):
            for j in range(4):
                small[:, l, 3 + c * 4 + j] = cwv[l, j, c * 128:(c + 1) * 128]
            small[:, l, 11 + c] = cbv[l, c * 128:(c + 1) * 128]
            for d in range(2):
                small[:, l, 13 + d * 2 + c] = ba[l, d, c * 128:(c + 1) * 128]
                small[:, l, 17 + d * 2 + c] = bx[l, d, c * 128:(c + 1) * 128]
                small[:, l, 21 + d * 2 + c] = lam[l, d, c * 128:(c + 1) * 128]
    gbias = np.ascontiguousarray(np.broadcast_to(f(inp["ml_gate_bias"])[:L][None], (128, L, 16)))
    consts = np.zeros((128, 4, 128), np.float32)
    consts[:, 0, :] = np.eye(128, dtype=np.float32)
    consts[:, 1, :] = np.triu(np.ones((128, 128), np.float32))
    consts[:, 2, :] = np.tril(np.ones((128, 128), np.float32))
    consts[:, 3, :] = 1.0
    C, S = _rope_tables()
    return {
        "w_mod": f(inp["w_mod"])[:L], "b_mod": f(inp["b_mod"])[:L],
        "w_in_fm": np.ascontiguousarray(w_in[:, :, fm_cols]), "w_in_tm": np.ascontiguousarray(w_in[:, :, tm_cols]),
        "w_uq": np.ascontiguousarray(w_uq[:, :, uq_cols]),
        "w_k": np.ascontiguousarray(w_ukv[:, :, kcols]), "w_v": np.ascontiguousarray(w_ukv[:, :, vcols]),
        "w_out": f(inp["w_out"])[:L], "w_ff1": f(inp["w_ff1"])[:L], "w_ff2": f(inp["w_ff2"])[:L],
        "w_bd": wbd, "small": small, "gbias": gbias,
        "final_g": np.ascontiguousarray(np.broadcast_to(f(inp["final_g"])[None], (128, D))),
        "ropeC": C, "ropeS": S, "consts": consts,
    }


_NC_CACHE = {}


def run(inputs, depth=4, debug=False, n_cores=8):
    shared = _prep_shared(inputs, depth)
    x, c, ctx, c_ctx = (np.asarray(inputs[k], dtype=np.float32) for k in ("x", "c", "ctx", "c_ctx"))
    in_maps = []
    for core in range(n_cores):
        b = core % 4
        mm = dict(shared)
        mm["xin"] = np.ascontiguousarray(np.concatenate([x[b], ctx[b]], axis=0))
        cc = np.stack([c[b].reshape(8, 128).T, c_ctx.reshape(8, 128).T], axis=-1)
        mm["cc"] = np.ascontiguousarray(cc.astype(np.float32))
        in_maps.append(mm)
    key = (depth, debug)
    if key not in _NC_CACHE:
        _NC_CACHE[key] = build(depth, debug)
    res = run_bass_kernel_spmd(_NC_CACHE[key], in_maps, core_ids=list(range(n_cores)))
    return res


def kernel(**inputs):
    res = run(inputs)
    out = np.stack([np.asarray(res.results[b]["out"], dtype=np.float32) for b in range(4)], axis=0)
    return out
```

```python
import numpy as np
from contextlib import ExitStack
import concourse.bass as bass
import concourse.mybir as mybir
from concourse.bass_utils import run_bass_kernel_spmd
from concourse.ap import AP

F32 = mybir.dt.float32
BF16 = mybir.dt.bfloat16
AF = mybir.ActivationFunctionType
ALU = mybir.AluOpType
AX = mybir.AxisListType

D = 1024
NLAT = 4096
NCTX = 256
NT = NLAT + NCTX
NTILE = NT // 128
EPS = 1e-6
DFF = 4096
SCALE = (128 + 64) ** -0.5
ENG = ("pe", "act", "dve", "pool", "sp")
NSLOT = 24
BLOCKS = [(j * 512, 512) for j in range(8)] + [(4096, 256)]


class _Rec:
    def __getattr__(self, name):
        return lambda *a, **k: (name, a, k)


_REC = _Rec()


class T:
    __slots__ = ("ap", "w", "r")

    def __init__(self, ap):
        self.ap = ap
        self.w = []
        self.r = {}


class Prog:
    def __init__(self, nc):
        self.nc = nc
        self.ops = {e: [] for e in ENG}
        self.cnt = {}
        self.seen = {e: {} for e in ENG}
        self.layer = 0
        self.slot_cum = [0] * NSLOT
        self.rr = 0
        self.sb_off = 16512
        self.nalloc = 0

    def alloc(self, shape, dtype, name=None):
        nbytes = int(np.prod(shape[1:])) * (2 if dtype == BF16 else 4)
        off = (self.sb_off + 63) // 64 * 64
        self.sb_off = off + nbytes
        assert self.sb_off <= 229344, ("sbuf overflow", self.sb_off, name)
        self.nalloc += 1
        h = self.nc.alloc_sbuf_tensor_at(f"sb{self.nalloc}_{name or ''}", list(shape), dtype, offset=off)
        return T(h[:] if hasattr(h, "__getitem__") else h.ap())

    def mark(self):
        return self.sb_off

    def release(self, m):
        self.sb_off = m

    def emit(self, eng, fn, reads=(), writes=(), dma=False):
        raw, oth = {}, {}

        def add(d, tok):
            k, v = tok
            if d.get(k, 0) < v:
                d[k] = v

        for t in reads:
            for tok in t.w:
                add(raw, tok)
        for t in writes:
            for tok in t.w:
                add(oth, tok)
            for k, v in t.r.items():
                add(oth, (k, v))
        if dma:
            slot = self.rr % NSLOT
            self.rr += 1
            prev = self.slot_cum[slot]
            if prev:
                add(oth, (("dma", slot), prev))
            self.slot_cum[slot] = prev + 16
            mytok = (("dma", slot), prev + 16)
        else:
            key = (eng, self.layer)
            self.cnt[key] = self.cnt.get(key, 0) + 1
            mytok = (key, self.cnt[key])
        waits = []
        seen = self.seen[eng]
        for d, is_raw in ((raw, True), (oth, False)):
            for k, v in d.items():
                if (not is_raw) and (not dma) and k[0] == eng:
                    continue
                if seen.get(k, 0) >= v:
                    continue
                seen[k] = v
                waits.append((k, v))
        self.ops[eng].append((waits, fn(_REC), mytok, dma))
        for t in reads:
            k, v = mytok
            if t.r.get(k, 0) < v:
                t.r[k] = v
        for t in writes:
            t.w = [mytok]
            t.r = {}
        return mytok

    def barrier(self):
        toks = dict(self.cnt)
        for s in range(NSLOT):
            if self.slot_cum[s]:
                toks[("dma", s)] = self.slot_cum[s]
        for e in ENG:
            waits = []
            seen = self.seen[e]
            for k, v in toks.items():
                if k[0] == e:
                    continue
                if seen.get(k, 0) >= v:
                    continue
                seen[k] = v
                waits.append((k, v))
            if waits:
                self.ops[e].append((waits, None, None, False))

    def finalize(self):
        nc = self.nc
        keys = set(self.cnt.keys())
        for s in range(NSLOT):
            keys.add(("dma", s))
        with ExitStack() as st:
            sems = {}
            for k in sorted(keys, key=str):
                sems[k] = st.enter_context(nc.semaphore(f"s_{k[0]}_{k[1]}"))
            block = st.enter_context(nc.Block())

            def replay(name):
                def run(e):
                    for waits, fn, tok, dma in self.ops[name]:
                        for k, v in waits:
                            e.wait_ge(sems[k], v)
                        if fn is not None:
                            ins = getattr(e, fn[0])(*fn[1], **fn[2])
                            ins.then_inc(sems[tok[0]], 16 if dma else 1)
                return run

            block.tensor(replay("pe"))
            block.scalar(replay("act"))
            block.vector(replay("dve"))
            block.gpsimd(replay("pool"))
            block.sync(replay("sp"))


def bc_last(ap, n):
    return AP(ap.tensor, ap.offset, [list(x) for x in ap.ap] + [[0, n]])


class _Stop(Exception):
    pass


def build(depth=4, debug=False, stop=None):
    nc = bass.Bass("TRN2", target_bir_lowering=False)
    P = Prog(nc)
    L = depth

    def din(name, shape, dt=F32):
        return nc.dram_tensor(name, list(shape), dt, kind="ExternalInput").ap()

    xin = din("xin", [NT, D])
    cc_d = din("cc", [128, 8, 2])
    wmod_d = din("w_mod", [L, D, 6 * D])
    bmod_d = din("b_mod", [L, 6 * D])
    winfm_d = din("w_in_fm", [L, D, 13 * 128])
    wintm_d = din("w_in_tm", [L, D, 784])
    wuq_d = din("w_uq", [L, 256, 1024])
    wk_d = din("w_k", [L, 128, 512])
    wv_d = din("w_v", [L, 128, 512])
    wout_d = din("w_out", [L, D, D])
    wff1_d = din("w_ff1", [L, D, DFF])
    wff2_d = din("w_ff2", [L, DFF, D])
    wbd_d = din("w_bd", [L, 128, 8, 128])
    small_d = din("small", [128, L, 40])
    gbias_d = din("gbias", [128, L, 16])
    fing_d = din("final_g", [128, D])
    ropeC_d = din("ropeC", [128, NT])
    ropeS_d = din("ropeS", [128, NT])
    const_d = din("consts", [128, 4, 128])
    out_d = nc.dram_tensor("out", [NLAT, D], F32, kind="ExternalOutput").ap()
    X_d = nc.dram_tensor("Xs", [NT, D], F32, kind="Internal").ap()
    UT_d = nc.dram_tensor("UTs", [8, 128, NT], BF16, kind="Internal").ap()
    MT_d = nc.dram_tensor("MTs", [8, 128, NT], BF16, kind="Internal").ap()
    dbg = {}
    if debug:
        dbg["UT"] = nc.dram_tensor("dbgUT", [8, 128, NT], BF16, kind="ExternalOutput").ap()
        dbg["MT"] = nc.dram_tensor("dbgMT", [8, 128, NT], BF16, kind="ExternalOutput").ap()
        dbg["X"] = nc.dram_tensor("dbgX", [NT, D], F32, kind="ExternalOutput").ap()

    Xt = [T(None) for _ in range(NTILE)]
    UTt = [T(None) for _ in range(9)]
    MTt = [[T(None) for _ in range(9)] for _ in range(8)]
    OUTt = [T(None) for _ in range(32)]

    ps = []
    for i in range(8):
        h = nc.alloc_psum_tensor(f"ps{i}", [128, 512], F32)
        ps.append(T(h[:]))

    def psb(t):
        return t.ap.bitcast(BF16)

    consts = P.alloc([128, 4, 128], F32, "consts")
    identF = consts.ap[:, 0, :]
    maskU = consts.ap[:, 1, :]
    maskL = consts.ap[:, 2, :]
    onesF = consts.ap[:, 3, :]
    cb = P.alloc([128, 2, 128], BF16, "constb")
    identB = cb.ap[:, 0, :]
    onesB = cb.ap[:, 1, :]
    small = P.alloc([128, L, 40], F32, "small")
    gbias = P.alloc([128, L, 16], F32, "gbias")
    siluT = P.alloc([128, 8, 33], F32, "siluT")
    modT = P.alloc([128, 48, 2], F32, "modT")
    gbt = P.alloc([128, 4, D], F32, "gbt")
    stage = [P.alloc([128, 2048], F32, f"stage{i}") for i in range(2)]
    stage_rr = [0]


    def rsqrt_ops(P_, dst, src_ap, src_T, shape_ap, epsv):
        P_.emit("dve", lambda e: e.tensor_scalar(out=shape_ap, in0=src_ap, scalar1=epsv, scalar2=None, op0=ALU.add), reads=[src_T], writes=[dst])
        P_.emit("act", lambda e: e.activation(out=shape_ap, in_=shape_ap, func=AF.Ln), reads=[dst], writes=[dst])
        P_.emit("act", lambda e: e.activation(out=shape_ap, in_=shape_ap, func=AF.Exp, scale=-0.5), reads=[dst], writes=[dst])

    def sigmoid_ops(P_, dst, dst_ap, src_ap, src_Ts, scale=1.0, bias=None, eng2="dve"):
        if bias is None:
            P_.emit("act", lambda e: e.activation(out=dst_ap, in_=src_ap, func=AF.Exp, scale=-scale), reads=list(src_Ts), writes=[dst])
        else:
            P_.emit("act", lambda e: e.activation(out=dst_ap, in_=src_ap, func=AF.Exp, scale=-scale, bias=bias), reads=list(src_Ts), writes=[dst])
        if eng2 == "act":
            P_.emit("act", lambda e: e.activation(out=dst_ap, in_=dst_ap, func=AF.Identity, bias=1.0), reads=[dst], writes=[dst])
        else:
            P_.emit(eng2, lambda e: e.tensor_scalar(out=dst_ap, in0=dst_ap, scalar1=1.0, scalar2=None, op0=ALU.add), reads=[dst], writes=[dst])
        P_.emit("dve", lambda e: e.reciprocal(out=dst_ap, in_=dst_ap), reads=[dst], writes=[dst])

    def dma(eng, out_ap, in_ap, reads, writes):
        P.emit(eng, lambda e: e.dma_start(out=out_ap, in_=in_ap), reads=reads, writes=writes, dma=True)

    def load_cast(dst_T, dst_ap3, src_ap3, nk, ncols):
        per = max(1, 2048 // ncols)
        k = 0
        while k < nk:
            kk = min(per, nk - k)
            st = stage[stage_rr[0] % 2]
            stage_rr[0] += 1
            sv = st.ap[:, 0:kk * ncols].rearrange("p (k n) -> p k n", k=kk)
            dma("sp", sv, src_ap3[:, k:k + kk, :], [], [st])
            d = dst_ap3[:, k:k + kk, :]
            P.emit("pool", lambda e, d=d, sv=sv: e.tensor_copy(out=d, in_=sv), reads=[st], writes=[dst_T])
            k += kk

    dma("sp", consts.ap, const_d, [], [consts])
    dma("sp", small.ap, small_d, [], [small])
    dma("sp", gbias.ap, gbias_d, [], [gbias])
    P.emit("dve", lambda e: e.tensor_copy(out=identB, in_=identF), reads=[consts], writes=[cb])
    P.emit("dve", lambda e: e.tensor_copy(out=onesB, in_=onesF), reads=[consts], writes=[cb])
    m0 = P.mark()
    cct = P.alloc([128, 8, 2], F32, "cct")
    cth = P.alloc([128, 8, 2], F32, "cth")
    dma("sp", cct.ap, cc_d, [], [cct])
    P.emit("pool", lambda e: e.memset(siluT.ap, 0.0), writes=[siluT])
    sigmoid_ops(P, cth, cth.ap, cct.ap, [cct])
    for j, col in ((0, 0), (1, 32)):
        P.emit("dve", lambda e, j=j, col=col: e.tensor_tensor(out=siluT.ap[:, :, col], in0=cth.ap[:, :, j], in1=cct.ap[:, :, j], op=ALU.mult),
               reads=[cth, cct, siluT], writes=[siluT])
    for l in range(L):
        s = small.ap[:, l, :]
        P.emit("dve", lambda e, s=s: e.tensor_scalar(out=s[:, 25:27], in0=s[:, 0:2], scalar1=16.0, scalar2=None, op0=ALU.mult), reads=[small], writes=[small])
        P.emit("dve", lambda e, s=s: e.tensor_scalar(out=s[:, 27:28], in0=s[:, 2:3], scalar1=float(np.sqrt(128.0)), scalar2=None, op0=ALU.mult), reads=[small], writes=[small])
        P.emit("dve", lambda e, s=s: e.tensor_scalar(out=s[:, 28:36], in0=s[:, 13:21], scalar1=-1.0, scalar2=None, op0=ALU.mult), reads=[small], writes=[small])
        P.emit("act", lambda e, s=s: e.activation(out=s[:, 36:40], in_=s[:, 21:25], func=AF.Exp, scale=-1.0), reads=[small], writes=[small])
        P.emit("act", lambda e, s=s: e.activation(out=s[:, 36:40], in_=s[:, 36:40], func=AF.Ln, bias=1.0), reads=[small], writes=[small])
        P.emit("dve", lambda e, s=s: e.tensor_scalar(out=s[:, 36:40], in0=s[:, 36:40], scalar1=-8.0, scalar2=None, op0=ALU.mult), reads=[small], writes=[small])
    P.barrier()
    P.release(m0)
    base_mark = P.mark()

    def xsrc(l):
        return xin if l == 0 else X_d

    def ln_tile(xt, uT_T, uT_ap, c0, shc, scc, col, work):
        sq, ssum, rs, xn, pst = work
        P.emit("act", lambda e: e.activation(out=sq.ap, in_=xt.ap, func=AF.Square), reads=[xt], writes=[sq])
        P.emit("dve", lambda e: e.reduce_sum(out=ssum.ap, in_=sq.ap, axis=AX.X), reads=[sq], writes=[ssum])
        rsqrt_ops(P, rs, ssum.ap, ssum, rs.ap, D * EPS)
        P.emit("dve", lambda e: e.tensor_scalar(out=xn.ap, in0=xt.ap, scalar1=rs.ap[:, 0:1], scalar2=32.0, op0=ALU.mult, op1=ALU.mult),
               reads=[xt, rs], writes=[xn])
        pv = psb(pst)
        for k in range(8):
            P.emit("pe", lambda e, k=k: e.transpose(pv[:, k * 128:(k + 1) * 128], xn.ap[:, k * 128:(k + 1) * 128], identB),
                   reads=[xn, cb], writes=[pst])
        for k in range(8):
            P.emit("act", lambda e, k=k: e.activation(out=uT_ap[:, k, c0:c0 + 128], in_=pv[:, k * 128:(k + 1) * 128], func=AF.Identity,
                                                      scale=modT.ap[:, scc + k, col:col + 1], bias=modT.ap[:, shc + k, col:col + 1]),
                   reads=[pst, modT], writes=[uT_T])

    phase_ctr = [0]

    def phase_done(name):
        phase_ctr[0] += 1
        if stop is not None and phase_ctr[0] >= stop:
            raise _Stop(name)

    for l in range(L):
      try:
        P.layer = l
        sm = small.ap[:, l, :]
        P.release(base_mark)
        m = P.mark()
        modrow = P.alloc([33, 6 * D], F32, "modrow")
        bmrow = P.alloc([33, 6 * D], F32, "bmrow")
        wm = [P.alloc([128, 8, 512], F32, f"wm{i}") for i in range(2)]
        P.emit("pool", lambda e: e.memset(bmrow.ap, 0.0), writes=[bmrow])
        dma("sp", bmrow.ap[0:1, :], bmod_d[l:l + 1, :], [], [bmrow])
        dma("sp", bmrow.ap[32:33, :], bmod_d[l:l + 1, :], [], [bmrow])
        for nb in range(12):
            w = wm[nb % 2]
            dma("sp", w.ap, wmod_d[l, :, nb * 512:(nb + 1) * 512].rearrange("(k p) n -> p k n", p=128), [], [w])
            pt = ps[nb % 2]
            for k in range(8):
                P.emit("pe", lambda e, k=k, w=w, pt=pt: e.matmul(pt.ap[0:33, :], lhsT=siluT.ap[:, k, :], rhs=w.ap[:, k, :], start=(k == 0), stop=(k == 7)),
                       reads=[siluT, w], writes=[pt])
            P.emit("dve", lambda e, nb=nb, pt=pt: e.tensor_tensor(out=modrow.ap[:, nb * 512:(nb + 1) * 512], in0=pt.ap[0:33, :],
                                                                 in1=bmrow.ap[:, nb * 512:(nb + 1) * 512], op=ALU.add),
                   reads=[pt, bmrow], writes=[modrow])
        for c in list(range(0, 16)) + list(range(24, 40)):
            pt = ps[2 + c % 2]
            P.emit("pe", lambda e, c=c, pt=pt: e.transpose(pt.ap[:, 0:33], modrow.ap[:, c * 128:(c + 1) * 128], identF[0:33, 0:33]),
                   reads=[modrow, consts], writes=[pt])
            is_scale = (8 <= c < 16) or (32 <= c < 40)
            for j, col in ((0, 0), (1, 32)):
                P.emit("dve", lambda e, c=c, j=j, col=col, pt=pt, a=(1.0 if is_scale else 0.0):
                       e.tensor_scalar(out=modT.ap[:, c, j:j + 1], in0=pt.ap[:, col:col + 1], scalar1=a, scalar2=None, op0=ALU.add),
                       reads=[pt], writes=[modT])
        gi = 0
        for gcol in (2 * D, 5 * D):
            for row in (0, 32):
                for half in range(2):
                    pt = ps[4 + half]
                    P.emit("pe", lambda e, row=row, gcol=gcol, half=half, pt=pt:
                           e.matmul(pt.ap, lhsT=onesF[row:row + 1, :], rhs=modrow.ap[row:row + 1, gcol + half * 512: gcol + half * 512 + 512], start=True, stop=True),
                           reads=[consts, modrow], writes=[pt])
                    P.emit("act", lambda e, gi=gi, half=half, pt=pt: e.activation(out=gbt.ap[:, gi, half * 512:(half + 1) * 512], in_=pt.ap, func=AF.Identity),
                           reads=[pt], writes=[gbt])
                gi += 1
        P.barrier()
        P.release(m)

        phase_done('P0')
        m = P.mark()
        xts = [P.alloc([128, D], F32, f"xt{i}") for i in range(2)]
        lnw = [(P.alloc([128, D], F32, f"sq{i}"), P.alloc([128, 1], F32, f"ssum{i}"), P.alloc([128, 1], F32, f"rs{i}"), P.alloc([128, D], BF16, f"xn{i}")) for i in range(2)]
        uTb = [P.alloc([128, 8, 512], BF16, f"uTb{i}") for i in range(2)]
        for bj, (c0, w) in enumerate(BLOCKS):
            ub = uTb[bj % 2]
            col = 1 if bj == 8 else 0
            for tt in range(w // 128):
                ti = c0 // 128 + tt
                xt = xts[ti % 2]
                dma("sp", xt.ap, xsrc(l)[ti * 128:(ti + 1) * 128, :], [Xt[ti]], [xt])
                ln_tile(xt, ub, ub.ap, tt * 128, 0, 8, col, lnw[ti % 2] + (ps[ti % 2],))
            dma("pool", UT_d[:, :, c0:c0 + w].rearrange("k p n -> p k n"), ub.ap[:, :, 0:w], [ub], [UTt[bj]])
            if debug and l == 0:
                dma("pool", dbg["UT"][:, :, c0:c0 + w].rearrange("k p n -> p k n"), ub.ap[:, :, 0:w], [ub], [])
        P.barrier()
        P.release(m)

        phase_done('P1')
        m_mla = P.mark()
        wfm = P.alloc([128, 8, 5 * 128], BF16, "wfm")
        load_cast(wfm, wfm.ap, winfm_d[l][:, 0:5 * 128].rearrange("(k p) n -> p k n", p=128), 8, 5 * 128)
        wuq = P.alloc([128, 2, 1024], BF16, "wuq")
        load_cast(wuq, wuq.ap, wuq_d[l].rearrange("(k p) n -> p k n", p=128), 2, 1024)
        wkv = P.alloc([128, 2, 512], BF16, "wkv")
        load_cast(wkv, wkv.ap[:, 0:1, :], wk_d[l].rearrange("(k p) n -> p k n", p=128), 1, 512)
        load_cast(wkv, wkv.ap[:, 1:2, :], wv_d[l].rearrange("(k p) n -> p k n", p=128), 1, 512)
        knT = [P.alloc([128, 4, 512], BF16, f"knT{j}") for j in range(9)]
        krT = [P.alloc([128, 512], BF16, f"krT{j}") for j in range(9)]
        Vt = [P.alloc([128, 512], BF16, f"V{i}") for i in range(NTILE)]
        m_k = P.mark()
        ublk = [P.alloc([128, 8, 512], BF16, f"ublk{i}") for i in range(2)]
        ropc = [P.alloc([128, 512], F32, f"ropc{i}") for i in range(2)]
        rops = [P.alloc([128, 512], F32, f"rops{i}") for i in range(2)]
        sqb = P.alloc([128, 2, 512], BF16, "sqb")
        rstd = P.alloc([128, 512], F32, "rstd")
        ckvn = P.alloc([128, 512], BF16, "ckvn")
        t1 = P.alloc([128, 512], F32, "t1")
        t2 = P.alloc([128, 512], F32, "t2")

        def inproj_fm(ub, chunk, pt, w):
            for k in range(8):
                P.emit("pe", lambda e, k=k: e.matmul(pt.ap[:, 0:w], lhsT=wfm.ap[:, k, chunk * 128:(chunk + 1) * 128], rhs=ub.ap[:, k, 0:w],
                                                     start=(k == 0), stop=(k == 7)), reads=[wfm, ub], writes=[pt])

        for bj, (c0, w) in enumerate(BLOCKS):
            ub = ublk[bj % 2]
            rc, rsn = ropc[bj % 2], rops[bj % 2]
            dma("sp", ub.ap[:, :, 0:w], UT_d[:, :, c0:c0 + w].rearrange("k p n -> p k n"), [UTt[bj]], [ub])
            dma("sp", rc.ap[:, 0:w], ropeC_d[:, c0:c0 + w], [], [rc])
            dma("sp", rsn.ap[:, 0:w], ropeS_d[:, c0:c0 + w], [], [rsn])
            inproj_fm(ub, 2, ps[0], w)
            inproj_fm(ub, 3, ps[1], w)
            inproj_fm(ub, 4, ps[2], w)
            P.emit("act", lambda e, w=w: e.activation(out=sqb.ap[:, 0, 0:w], in_=ps[0].ap[:, 0:w], func=AF.Square), reads=[ps[0]], writes=[sqb])
            P.emit("pe", lambda e, w=w: e.matmul(ps[3].ap[:, 0:w], lhsT=onesB, rhs=sqb.ap[:, 0, 0:w], start=True, stop=True), reads=[cb, sqb], writes=[ps[3]])
            rsqrt_ops(P, rstd, ps[3].ap[:, 0:w], ps[3], rstd.ap[:, 0:w], 128 * EPS)
            P.emit("dve", lambda e, w=w: e.scalar_tensor_tensor(out=ckvn.ap[:, 0:w], in0=ps[0].ap[:, 0:w], scalar=sm[:, 27:28], in1=rstd.ap[:, 0:w], op0=ALU.mult, op1=ALU.mult),
                   reads=[ps[0], rstd, small], writes=[ckvn])
            for h in range(4):
                pt = ps[4 + h % 2]
                P.emit("pe", lambda e, h=h, pt=pt, w=w: e.matmul(pt.ap[:, 0:w], lhsT=wkv.ap[:, 0, h * 128:(h + 1) * 128], rhs=ckvn.ap[:, 0:w], start=True, stop=True),
                       reads=[wkv, ckvn], writes=[pt])
                P.emit("act", lambda e, h=h, pt=pt, w=w, bj=bj: e.activation(out=knT[bj].ap[:, h, 0:w], in_=pt.ap[:, 0:w], func=AF.Identity), reads=[pt], writes=[knT[bj]])
            for tt in range(w // 128):
                ti = c0 // 128 + tt
                pt = ps[6 + tt % 2]
                P.emit("pe", lambda e, tt=tt, pt=pt: e.matmul(pt.ap, lhsT=ckvn.ap[:, tt * 128:(tt + 1) * 128], rhs=wkv.ap[:, 1, :], start=True, stop=True),
                       reads=[ckvn, wkv], writes=[pt])
                P.emit("dve", lambda e, ti=ti, pt=pt: e.tensor_copy(out=Vt[ti].ap, in_=pt.ap), reads=[pt], writes=[Vt[ti]])
            P.emit("dve", lambda e, w=w, rc=rc: e.tensor_tensor(out=t1.ap[:, 0:w], in0=ps[1].ap[:, 0:w], in1=rc.ap[:, 0:w], op=ALU.mult), reads=[ps[1], rc], writes=[t1])
            P.emit("dve", lambda e, w=w, rsn=rsn: e.tensor_tensor(out=t2.ap[:, 0:w], in0=ps[2].ap[:, 0:w], in1=rsn.ap[:, 0:w], op=ALU.mult), reads=[ps[2], rsn], writes=[t2])
            P.emit("pool", lambda e, w=w, bj=bj: e.tensor_tensor(out=krT[bj].ap[:, 0:w], in0=t1.ap[:, 0:w], in1=t2.ap[:, 0:w], op=ALU.add), reads=[t1, t2], writes=[krT[bj]])
        P.barrier()
        P.release(m_k)

        phase_done('P2')
        ublk = [P.alloc([128, 8, 512], BF16, f"ublkq{i}") for i in range(2)]
        ropc = [P.alloc([128, 512], F32, f"ropcq{i}") for i in range(2)]
        rops = [P.alloc([128, 512], F32, f"ropsq{i}") for i in range(2)]
        sqb = P.alloc([128, 2, 512], BF16, "sqbq")
        rstd = P.alloc([128, 512], F32, "rstdq")
        cqn = P.alloc([128, 2, 512], BF16, "cqn")
        qnT = [P.alloc([128, 4, 512], BF16, f"qnT{i}") for i in range(2)]
        qrT = [P.alloc([128, 2, 512], BF16, f"qrT{i}") for i in range(2)]
        t1 = P.alloc([128, 512], F32, "t1q")
        t2 = P.alloc([128, 512], F32, "t2q")
        pT = [P.alloc([128, 512], BF16, f"pT{i}") for i in range(3)]
        rden = P.alloc([128, 512], F32, "rden")
        yT = [P.alloc([128, 512], BF16, f"yT{i}") for i in range(2)]
        pti = 0
        for bj, (c0, w) in enumerate(BLOCKS):
            ub = ublk[bj % 2]
            rc, rsn = ropc[bj % 2], rops[bj % 2]
            qn, qr = qnT[bj % 2], qrT[bj % 2]
            dma("sp", ub.ap[:, :, 0:w], UT_d[:, :, c0:c0 + w].rearrange("k p n -> p k n"), [UTt[bj]], [ub])
            dma("sp", rc.ap[:, 0:w], ropeC_d[:, c0:c0 + w], [], [rc])
            dma("sp", rsn.ap[:, 0:w], ropeS_d[:, c0:c0 + w], [], [rsn])
            inproj_fm(ub, 0, ps[0], w)
            inproj_fm(ub, 1, ps[1], w)
            for c in range(2):
                P.emit("act", lambda e, c=c, w=w: e.activation(out=sqb.ap[:, c, 0:w], in_=ps[c].ap[:, 0:w], func=AF.Square), reads=[ps[c]], writes=[sqb])
            for c in range(2):
                P.emit("pe", lambda e, c=c, w=w: e.matmul(ps[2].ap[:, 0:w], lhsT=onesB, rhs=sqb.ap[:, c, 0:w], start=(c == 0), stop=(c == 1)), reads=[cb, sqb], writes=[ps[2]])
            rsqrt_ops(P, rstd, ps[2].ap[:, 0:w], ps[2], rstd.ap[:, 0:w], 256 * EPS)
            for c in range(2):
                P.emit("dve", lambda e, c=c, w=w: e.scalar_tensor_tensor(out=cqn.ap[:, c, 0:w], in0=ps[c].ap[:, 0:w], scalar=sm[:, 25 + c:26 + c], in1=rstd.ap[:, 0:w], op0=ALU.mult, op1=ALU.mult),
                       reads=[ps[c], rstd, small], writes=[cqn])

            def qproj(ch, pt):
                for kc in range(2):
                    P.emit("pe", lambda e, kc=kc: e.matmul(pt.ap[:, 0:w], lhsT=wuq.ap[:, kc, ch * 128:(ch + 1) * 128], rhs=cqn.ap[:, kc, 0:w], start=(kc == 0), stop=(kc == 1)),
                           reads=[wuq, cqn], writes=[pt])
            for h in range(4):
                pt = ps[3 + h % 2]
                qproj(h, pt)
                P.emit("dve", lambda e, h=h, pt=pt, w=w, qn=qn: e.tensor_copy(out=qn.ap[:, h, 0:w], in_=pt.ap[:, 0:w]), reads=[pt], writes=[qn])
            for pr in range(2):
                qproj(4 + pr, ps[5])
                qproj(6 + pr, ps[6])
                P.emit("dve", lambda e, w=w, rc=rc: e.tensor_tensor(out=t1.ap[:, 0:w], in0=ps[5].ap[:, 0:w], in1=rc.ap[:, 0:w], op=ALU.mult), reads=[ps[5], rc], writes=[t1])
                P.emit("dve", lambda e, w=w, rsn=rsn: e.tensor_tensor(out=t2.ap[:, 0:w], in0=ps[6].ap[:, 0:w], in1=rsn.ap[:, 0:w], op=ALU.mult), reads=[ps[6], rsn], writes=[t2])
                P.emit("pool", lambda e, w=w, pr=pr, qr=qr: e.tensor_tensor(out=qr.ap[:, pr, 0:w], in0=t1.ap[:, 0:w], in1=t2.ap[:, 0:w], op=ALU.add), reads=[t1, t2], writes=[qr])
            ktiles = [32, 33] if bj == 8 else list(range(NTILE))
            for h in range(4):
                po, pd = ps[0], ps[1]
                hp = 64 * (h % 2)
                nkt = len(ktiles)
                pps = {}

                def emit_s(ki):
                    nonlocal pti
                    kt = ktiles[ki]
                    kb, ko = kt // 4, (kt % 4) * 128
                    pss = ps[2 + ki % 2]
                    pp = pT[pti % 3]
                    pti += 1
                    P.emit("pe", lambda e: e.matmul(pss.ap[:, 0:w], lhsT=knT[kb].ap[:, h, ko:ko + 128], rhs=qn.ap[:, h, 0:w], start=True, stop=False),
                           reads=[knT[kb], qn], writes=[pss])
                    P.emit("pe", lambda e: e.matmul(pss.ap[:, 0:w], lhsT=krT[kb].ap[hp:hp + 64, ko:ko + 128], rhs=qr.ap[hp:hp + 64, h // 2, 0:w], start=False, stop=True),
                           reads=[krT[kb], qr], writes=[pss])
                    P.emit("act", lambda e: e.activation(out=pp.ap[:, 0:w], in_=pss.ap[:, 0:w], func=AF.Exp, scale=SCALE), reads=[pss], writes=[pp])
                    pps[ki] = pp

                emit_s(0)
                for ki, kt in enumerate(ktiles):
                    if ki + 1 < nkt:
                        emit_s(ki + 1)
                    pp = pps.pop(ki)
                    first, last = ki == 0, ki == nkt - 1
                    P.emit("pe", lambda e, kt=kt, pp=pp, first=first, last=last: e.matmul(po.ap[:, 0:w], lhsT=Vt[kt].ap[:, h * 128:(h + 1) * 128], rhs=pp.ap[:, 0:w], start=first, stop=last),
                           reads=[Vt[kt], pp], writes=[po])
                    P.emit("pe", lambda e, pp=pp, first=first, last=last: e.matmul(pd.ap[:, 0:w], lhsT=onesB, rhs=pp.ap[:, 0:w], start=first, stop=last),
                           reads=[cb, pp], writes=[pd])
                yt = yT[h % 2]
                P.emit("dve", lambda e, w=w: e.reciprocal(out=rden.ap[:, 0:w], in_=pd.ap[:, 0:w]), reads=[pd], writes=[rden])
                P.emit("dve", lambda e, w=w, yt=yt: e.tensor_tensor(out=yt.ap[:, 0:w], in0=po.ap[:, 0:w], in1=rden.ap[:, 0:w], op=ALU.mult), reads=[po, rden], writes=[yt])
                dma("pool", MT_d[h, :, c0:c0 + w], yt.ap[:, 0:w], [yt], [MTt[h][bj]])
        P.barrier()
        P.release(m_mla)

        phase_done('P3')
        m = P.mark()
        wfm = P.alloc([128, 8, 4 * 128], BF16, "wfm_ml")
        load_cast(wfm, wfm.ap, winfm_d[l][:, 5 * 128:9 * 128].rearrange("(k p) n -> p k n", p=128), 8, 512)
        wtm = P.alloc([128, 8, 784], BF16, "wtm")
        load_cast(wtm, wtm.ap, wintm_d[l].rearrange("(k p) n -> p k n", p=128), 8, 784)
        mqT = [P.alloc([128, 4, 512], BF16, f"mqT{j}") for j in range(9)]
        mkT = [P.alloc([128, 2, 512], BF16, f"mkT{j}") for j in range(9)]
        mk = [P.alloc([128, 256], BF16, f"mk{i}") for i in range(NTILE)]
        mvA = [P.alloc([128, 4, 66], BF16, f"mvA{i}") for i in range(NTILE)]
        moS = [P.alloc([128, 256], BF16, f"moS{i}") for i in range(NTILE)]
        ebt = [P.alloc([128, 8], F32, f"eb{i}") for i in range(NTILE)]
        eit = [P.alloc([128, 8], F32, f"ei{i}") for i in range(NTILE)]
        eblt = [P.alloc([128, 8], F32, f"ebl{i}") for i in range(NTILE)]
        hf = [P.alloc([128, 256], BF16, f"hf{i}") for i in range(NTILE)]
        m2 = P.mark()
        ublk = [P.alloc([128, 8, 512], BF16, "ublkm0")] * 2
        mlw = [(P.alloc([128, 16], F32, f"gt{i}"), P.alloc([128, 8], F32, f"lf{i}"), P.alloc([128, 8], F32, f"dd{i}"), P.alloc([128, 256], F32, f"tnh{i}")) for i in range(2)]
        for bj, (c0, w) in enumerate(BLOCKS):
            ub = ublk[bj % 2]
            dma("sp", ub.ap[:, :, 0:w], UT_d[:, :, c0:c0 + w].rearrange("k p n -> p k n"), [UTt[bj]], [ub])
            for ch in range(4):
                pt = ps[ch % 2]
                for k in range(8):
                    P.emit("pe", lambda e, k=k, ch=ch, pt=pt, w=w, ub=ub: e.matmul(pt.ap[:, 0:w], lhsT=wfm.ap[:, k, ch * 128:(ch + 1) * 128], rhs=ub.ap[:, k, 0:w], start=(k == 0), stop=(k == 7)),
                           reads=[wfm, ub], writes=[pt])
                if ch < 2:
                    if ch == 0:
                        P.emit("pool", lambda e, bj=bj: e.memset(mqT[bj].ap, 0.0), writes=[mqT[bj]])
                    for hh in range(2):
                        P.emit("act", lambda e, ch=ch, hh=hh, pt=pt, w=w, bj=bj: e.activation(out=mqT[bj].ap[64 * hh:64 * hh + 64, 2 * ch + hh, 0:w], in_=pt.ap[64 * hh:64 * hh + 64, 0:w], func=AF.Identity, scale=0.125), reads=[pt], writes=[mqT[bj]])
                else:
                    P.emit("act", lambda e, ch=ch, pt=pt, w=w, bj=bj: e.activation(out=mkT[bj].ap[:, ch - 2, 0:w], in_=pt.ap[:, 0:w], func=AF.Identity), reads=[pt], writes=[mkT[bj]])
            for tt in range(w // 128):
                ti = c0 // 128 + tt
                gt, lf, dd, tnh = mlw[ti % 2]
                pa, pb = ps[2 + tt % 2], ps[4 + tt % 2]
                for k in range(8):
                    P.emit("pe", lambda e, k=k, tt=tt, pa=pa, ub=ub: e.matmul(pa.ap, lhsT=ub.ap[:, k, tt * 128:(tt + 1) * 128], rhs=wtm.ap[:, k, 0:512], start=(k == 0), stop=(k == 7)),
                           reads=[ub, wtm], writes=[pa])
                for k in range(8):
                    P.emit("pe", lambda e, k=k, tt=tt, pb=pb, ub=ub: e.matmul(pb.ap[:, 0:272], lhsT=ub.ap[:, k, tt * 128:(tt + 1) * 128], rhs=wtm.ap[:, k, 512:784], start=(k == 0), stop=(k == 7)),
                           reads=[ub, wtm], writes=[pb])
                P.emit("dve", lambda e, ti=ti, pa=pa: e.tensor_copy(out=mk[ti].ap, in_=pa.ap[:, 0:256]), reads=[pa], writes=[mk[ti]])
                P.emit("pool", lambda e, ti=ti: e.memset(mvA[ti].ap, 1.0), writes=[mvA[ti]])
                P.emit("dve", lambda e, ti=ti, pa=pa: e.tensor_copy(out=mvA[ti].ap[:, :, 0:64], in_=pa.ap[:, 256:512].rearrange("p (h d) -> p h d", h=4)), reads=[pa], writes=[mvA[ti]])
                sigmoid_ops(P, tnh, tnh.ap, pb.ap[:, 0:256], [pb], eng2="pool")
                P.emit("pool", lambda e, ti=ti: e.tensor_copy(out=moS[ti].ap, in_=tnh.ap), reads=[tnh], writes=[moS[ti]])
                P.emit("dve", lambda e, pb=pb: e.tensor_tensor(out=gt.ap, in0=pb.ap[:, 256:272], in1=gbias.ap[:, l, :], op=ALU.add), reads=[pb, gbias], writes=[gt])
                gv = gt.ap.rearrange("p (a b) -> p a b", a=4)
                lfv = lf.ap.rearrange("p (a b) -> p a b", a=2)
                P.emit("act", lambda e, gv=gv, lfv=lfv: e.activation(out=lfv, in_=gv[:, 1::2, :], func=AF.Exp, scale=-1.0), reads=[gt], writes=[lf])
                P.emit("act", lambda e: e.activation(out=lf.ap, in_=lf.ap, func=AF.Ln, bias=1.0), reads=[lf], writes=[lf])
                P.emit("dve", lambda e: e.tensor_scalar(out=lf.ap, in0=lf.ap, scalar1=-1.0, scalar2=None, op0=ALU.mult), reads=[lf], writes=[lf])
                pc = ps[6 + ti % 2]
                P.emit("pe", lambda e, pc=pc: e.matmul(pc.ap[:, 0:4], lhsT=maskU, rhs=lf.ap[:, 0:4], start=True, stop=True), reads=[consts, lf], writes=[pc])
                P.emit("pe", lambda e, pc=pc: e.matmul(pc.ap[:, 4:8], lhsT=maskL, rhs=lf.ap[:, 4:8], start=True, stop=True), reads=[consts, lf], writes=[pc])
                P.emit("pe", lambda e, pc=pc: e.matmul(pc.ap[:, 8:16], lhsT=onesF, rhs=lf.ap[:, 0:8], start=True, stop=True), reads=[consts, lf], writes=[pc])
                P.emit("act", lambda e, ti=ti, pc=pc: e.activation(out=ebt[ti].ap, in_=pc.ap[:, 0:8], func=AF.Exp), reads=[pc], writes=[ebt[ti]])
                P.emit("act", lambda e, ti=ti, pc=pc: e.activation(out=eblt[ti].ap, in_=pc.ap[:, 8:16], func=AF.Exp), reads=[pc], writes=[eblt[ti]])
                ddv = dd.ap.rearrange("p (a b) -> p a b", a=2)
                P.emit("dve", lambda e, pc=pc, gv=gv, ddv=ddv: e.tensor_tensor(out=ddv, in0=gv[:, 0::2, :], in1=pc.ap[:, 0:8].rearrange("p (a b) -> p a b", a=2), op=ALU.subtract),
                       reads=[gt, pc], writes=[dd])
                P.emit("act", lambda e, ti=ti: e.activation(out=eit[ti].ap, in_=dd.ap, func=AF.Exp), reads=[dd], writes=[eit[ti]])
        P.barrier()
        P.release(m2)
        phase_done('P4a')
        ptT = [[P.alloc([128, 4, 128], BF16, f"ptT{d}{i}") for i in range(2)] for d in range(2)]
        ktp = [[P.alloc([128, 4, 128], BF16, f"ktp{d}{i}") for i in range(2)] for d in range(2)]
        Cst = [P.alloc([128, 2, 65], F32, f"Cst{d}") for d in range(2)]
        Cbf = [P.alloc([128, 2, 66], BF16, f"Cbf{d}") for d in range(2)]
        ctmp = P.alloc([128, 2, 65], F32, "ctmp")
        den4 = P.alloc([128, 4], F32, "den4")
        r4 = P.alloc([128, 4], F32, "r4")
        hb = P.alloc([128, 256], F32, "hb")
        hs = P.alloc([128, 256], F32, "hs")
        hq = P.alloc([128, 256], F32, "hq")
        ss4 = P.alloc([128, 4], F32, "ss4")
        ybf = P.alloc([128, 256], BF16, "ybf")
        ybT = [P.alloc([128, 2, 512], BF16, f"ybT{i}") for i in range(2)]
        for d in range(2):
            for i in range(2):
                P.emit("pool", lambda e, d=d, i=i: e.memset(ktp[d][i].ap, 0.0), writes=[ktp[d][i]])
        fwd_order = [32, 33] + list(range(32))
        bwd_order = [33, 32] + list(range(31, -1, -1))
        ybcount = {}
        import os as _os
        _ns = int(_os.environ.get('DBG_STEPS', '99'))
        _nd = int(_os.environ.get('DBG_DIRS', '2'))
        _lvl = int(_os.environ.get('DBG_LVL', '99'))
        _skip = _os.environ.get('DBG_SKIP', '')
        for d, order in ((0, fwd_order[:_ns]), (1, bwd_order[:_ns]))[:_nd]:
            mask = maskU if d == 0 else maskL
            for step, ti in enumerate(order):
                bj, to = (8, (ti - 32) * 128) if ti >= 32 else (ti // 4, (ti % 4) * 128)
                if step == 0:
                    P.emit("pool", lambda e, d=d: e.memset(Cst[d].ap, 0.0), writes=[Cst[d]])
                    P.emit("pool", lambda e, d=d: e.memset(Cbf[d].ap, 0.0), writes=[Cbf[d]])
                pS, pO, pKV = ps[step % 2], ps[2 + step % 2], ps[4 + step % 2]
                pt_, kt_ = ptT[d][step % 2], ktp[d][step % 2]
                for h in range(4):
                    hp, mch = 64 * (h % 2), h // 2
                    if 'S' in _skip or ('E' in _skip and h % 2 == 1) or ('O' in _skip and h % 2 == 0):
                        continue
                    P.emit("pe", lambda e, h=h, hp=hp, mch=mch, bj=bj, to=to, pS=pS: e.matmul(pS.ap[:, h * 128:(h + 1) * 128], lhsT=mkT[bj].ap[:, mch, to:to + 128], rhs=mqT[bj].ap[:, h, to:to + 128], start=True, stop=True),
                           reads=[mkT[bj], mqT[bj]], writes=[pS])
                for h in range(4):
                    hp = 64 * (h % 2)
                    if 'P' not in _skip:
                      P.emit("dve", lambda e, h=h, ti=ti, d=d, pS=pS, pt_=pt_, mask=mask: e.scalar_tensor_tensor(out=pt_.ap[:, h, :], in0=pS.ap[:, h * 128:(h + 1) * 128], scalar=eit[ti].ap[:, d * 4 + h:d * 4 + h + 1], in1=mask, op0=ALU.mult, op1=ALU.mult),
                             reads=[pS, eit[ti], consts], writes=[pt_])
                    if "K" in _skip:
                        continue
                    P.emit("pool", lambda e, h=h, hp=hp, ti=ti, d=d, kt_=kt_: e.tensor_scalar(out=kt_.ap[:, h, hp:hp + 64], in0=mk[ti].ap[:, h * 64:(h + 1) * 64], scalar1=eit[ti].ap[:, d * 4 + h:d * 4 + h + 1], scalar2=None, op0=ALU.mult),
                           reads=[mk[ti], eit[ti]], writes=[kt_])
                if _lvl < 2:
                    continue
                for h in range(4):
                    hp, mch = 64 * (h % 2), h // 2
                    P.emit("pe", lambda e, h=h, ti=ti, pO=pO, pt_=pt_: e.matmul(pO.ap[:, h * 65:(h + 1) * 65], lhsT=pt_.ap[:, h, :], rhs=mvA[ti].ap[:, h, 0:65], start=True, stop=False),
                           reads=[pt_, mvA[ti]], writes=[pO])
                    P.emit("pe", lambda e, h=h, hp=hp, mch=mch, bj=bj, to=to, d=d, pO=pO: e.matmul(pO.ap[:, h * 65:(h + 1) * 65], lhsT=mqT[bj].ap[:, h, to:to + 128], rhs=Cbf[d].ap[:, mch, 0:65], start=False, stop=True),
                           reads=[mqT[bj], Cbf[d]], writes=[pO])
                for mch in range(2):
                    for hh in range(2):
                        h = mch * 2 + hh
                        P.emit("pe", lambda e, h=h, mch=mch, hh=hh, ti=ti, pKV=pKV, kt_=kt_: e.matmul(pKV.ap[:, mch * 65:(mch + 1) * 65], lhsT=kt_.ap[:, h, :], rhs=mvA[ti].ap[:, h, 0:65], start=(hh == 0), stop=(hh == 1)),
                               reads=[kt_, mvA[ti]], writes=[pKV])
                if _lvl < 3:
                    continue
                Ov = pO.ap[:, 0:260].rearrange("p (h c) -> p h c", h=4)
                ebv = ebt[ti].ap[:, d * 4:(d + 1) * 4]
                P.emit("dve", lambda e, Ov=Ov, ebv=ebv: e.tensor_tensor(out=den4.ap, in0=Ov[:, :, 64], in1=ebv, op=ALU.mult), reads=[pO, ebt[ti]], writes=[den4])
                P.emit("act", lambda e: e.activation(out=den4.ap, in_=den4.ap, func=AF.Abs), reads=[den4], writes=[den4])
                P.emit("dve", lambda e: e.tensor_scalar(out=den4.ap, in0=den4.ap, scalar1=1.0, scalar2=None, op0=ALU.max), reads=[den4], writes=[den4])
                P.emit("dve", lambda e: e.reciprocal(out=den4.ap, in_=den4.ap), reads=[den4], writes=[den4])
                P.emit("dve", lambda e, ebv=ebv: e.tensor_tensor(out=r4.ap, in0=ebv, in1=den4.ap, op=ALU.mult), reads=[ebt[ti], den4], writes=[r4])
                hdst = hf[ti] if d == 0 else hb
                P.emit("dve", lambda e, Ov=Ov, hdst=hdst: e.tensor_tensor(out=hdst.ap.rearrange("p (h c) -> p h c", h=4), in0=Ov[:, :, 0:64], in1=bc_last(r4.ap, 64), op=ALU.mult),
                       reads=[pO, r4], writes=[hdst])
                if _lvl < 4:
                    continue
                KVv = pKV.ap[:, 0:130].rearrange("p (m c) -> p m c", m=2)
                P.emit("dve", lambda e, KVv=KVv, d=d: e.tensor_tensor(out=ctmp.ap, in0=KVv, in1=Cst[d].ap, op=ALU.add), reads=[pKV, Cst[d]], writes=[ctmp])
                for mch in range(2):
                    for hh in range(2):
                        h = mch * 2 + hh
                        hp = 64 * hh
                        P.emit("dve", lambda e, h=h, hp=hp, mch=mch, ti=ti, d=d: e.tensor_scalar(out=Cst[d].ap[hp:hp + 64, mch, :], in0=ctmp.ap[hp:hp + 64, mch, :], scalar1=eblt[ti].ap[hp:hp + 64, d * 4 + h:d * 4 + h + 1], scalar2=None, op0=ALU.mult),
                               reads=[ctmp, eblt[ti]], writes=[Cst[d]])
                P.emit("act", lambda e, d=d: e.activation(out=Cbf[d].ap[:, :, 0:65], in_=Cst[d].ap, func=AF.Identity), reads=[Cst[d]], writes=[Cbf[d]])
                if _lvl < 5:
                    continue
                if d == 1:
                    P.emit("pool", lambda e, ti=ti: e.tensor_tensor(out=hs.ap, in0=hf[ti].ap, in1=hb.ap, op=ALU.add), reads=[hf[ti], hb], writes=[hs])
                    P.emit("act", lambda e: e.activation(out=hq.ap, in_=hs.ap, func=AF.Square), reads=[hs], writes=[hq])
                    P.emit("dve", lambda e: e.reduce_sum(out=ss4.ap, in_=hq.ap.rearrange("p (h c) -> p h c", h=4), axis=AX.X), reads=[hq], writes=[ss4])
                    rsqrt_ops(P, ss4, ss4.ap, ss4, ss4.ap, 64 * EPS)
                    P.emit("dve", lambda e: e.tensor_tensor(out=hq.ap.rearrange("p (h c) -> p h c", h=4), in0=hs.ap.rearrange("p (h c) -> p h c", h=4), in1=bc_last(ss4.ap, 64), op=ALU.mult),
                           reads=[hs, ss4], writes=[hq])
                    P.emit("dve", lambda e, ti=ti: e.scalar_tensor_tensor(out=ybf.ap, in0=hq.ap, scalar=8.0, in1=moS[ti].ap, op0=ALU.mult, op1=ALU.mult), reads=[hq, moS[ti]], writes=[ybf])
                    ptr = ps[6 + step % 2]
                    pv = psb(ptr)
                    for c in range(2):
                        P.emit("pe", lambda e, c=c, pv=pv, ptr=ptr: e.transpose(pv[:, c * 128:(c + 1) * 128], ybf.ap[:, c * 128:(c + 1) * 128], identB), reads=[ybf, cb], writes=[ptr])
                    yb = ybT[bj % 2]
                    P.emit("act", lambda e, pv=pv, yb=yb, to=to, ptr=ptr: e.activation(out=yb.ap[:, :, to:to + 128], in_=pv[:, 0:256].rearrange("p (c n) -> p c n", c=2), func=AF.Identity), reads=[ptr], writes=[yb])
                    ybcount[bj] = ybcount.get(bj, 0) + 1
                    if ybcount[bj] == BLOCKS[bj][1] // 128:
                        c0, w = BLOCKS[bj]
                        for c in range(2):
                            dma("pool", MT_d[4 + c, :, c0:c0 + w], yb.ap[:, c, 0:w], [yb], [MTt[4 + c][bj]])
        P.barrier()
        P.release(m)

        phase_done('P4')
        m = P.mark()
        wbd = P.alloc([128, 8, 128], BF16, "wbd")
        load_cast(wbd, wbd.ap, wbd_d[l], 8, 128)
        wfm = P.alloc([128, 8, 4 * 128], BF16, "wfm_lru")
        load_cast(wfm, wfm.ap, winfm_d[l][:, 9 * 128:13 * 128].rearrange("(k p) n -> p k n", p=128), 8, 512)
        ublk = [P.alloc([128, 8, 512], BF16, f"ublkl{i}") for i in range(2)]
        xb = P.alloc([128, NT], F32, "xb")
        gb = P.alloc([128, NT], BF16, "gb")
        xs = P.alloc([128, NT], F32, "xs")
        xsb = P.alloc([128, NT], BF16, "xsb")
        At = P.alloc([128, NT], F32, "At")
        Ut = P.alloc([128, NT], F32, "Ut")
        Hf = P.alloc([128, NT], F32, "Hf")
        Hb = P.alloc([128, NT], F32, "Hb")
        tq = [P.alloc([128, 512], F32, f"tq{i}") for i in range(4)]
        ycb = [P.alloc([128, 512], BF16, f"ycb{i}") for i in range(2)]
        segs = [(0, NLAT), (NLAT, NCTX)]
        for c in range(2):
            for bj, (c0, w) in enumerate(BLOCKS):
                ub = ublk[bj % 2]
                dma("sp", ub.ap[:, :, 0:w], UT_d[:, :, c0:c0 + w].rearrange("k p n -> p k n"), [UTt[bj]], [ub])
                for which, dst in ((0, xb), (1, gb)):
                    pt = ps[(2 * bj + which) % 4]
                    ch = which * 2 + c
                    for k in range(8):
                        P.emit("pe", lambda e, k=k, ch=ch, pt=pt, w=w, ub=ub: e.matmul(pt.ap[:, 0:w], lhsT=wfm.ap[:, k, ch * 128:(ch + 1) * 128], rhs=ub.ap[:, k, 0:w], start=(k == 0), stop=(k == 7)),
                               reads=[wfm, ub], writes=[pt])
                    P.emit("act", lambda e, pt=pt, w=w, c0=c0, dst=dst: e.activation(out=dst.ap[:, c0:c0 + w], in_=pt.ap[:, 0:w], func=AF.Identity), reads=[pt], writes=[dst])
            cw = lambda j: sm[:, 3 + c * 4 + j: 4 + c * 4 + j]
            for (s0, sl) in segs:
                P.emit("dve", lambda e, s0=s0, sl=sl: e.tensor_scalar(out=xs.ap[:, s0:s0 + sl], in0=xb.ap[:, s0:s0 + sl], scalar1=cw(2), scalar2=sm[:, 11 + c:12 + c], op0=ALU.mult, op1=ALU.add),
                       reads=[xb, small], writes=[xs])
                P.emit("dve", lambda e, s0=s0, sl=sl: e.scalar_tensor_tensor(out=xs.ap[:, s0 + 2:s0 + sl], in0=xb.ap[:, s0:s0 + sl - 2], scalar=cw(0), in1=xs.ap[:, s0 + 2:s0 + sl], op0=ALU.mult, op1=ALU.add),
                       reads=[xb, xs, small], writes=[xs])
                P.emit("dve", lambda e, s0=s0, sl=sl: e.scalar_tensor_tensor(out=xs.ap[:, s0 + 1:s0 + sl], in0=xb.ap[:, s0:s0 + sl - 1], scalar=cw(1), in1=xs.ap[:, s0 + 1:s0 + sl], op0=ALU.mult, op1=ALU.add),
                       reads=[xb, xs, small], writes=[xs])
                P.emit("dve", lambda e, s0=s0, sl=sl: e.scalar_tensor_tensor(out=xs.ap[:, s0:s0 + sl - 1], in0=xb.ap[:, s0 + 1:s0 + sl], scalar=cw(3), in1=xs.ap[:, s0:s0 + sl - 1], op0=ALU.mult, op1=ALU.add),
                       reads=[xb, xs, small], writes=[xs])
            P.emit("pool", lambda e: e.tensor_copy(out=xsb.ap, in_=xs.ap), reads=[xs], writes=[xsb])
            for d in range(2):
                Hd = Hf if d == 0 else Hb
                for bj, (c0, w) in enumerate(BLOCKS):
                    pa, px = ps[(2 * bj) % 4 + 4 * 0], ps[(2 * bj + 1) % 4]
                    ia, ix = (d * 2 + 0) * 2 + c, (d * 2 + 1) * 2 + c
                    P.emit("pe", lambda e, ia=ia, pa=pa, c0=c0, w=w: e.matmul(pa.ap[:, 0:w], lhsT=wbd.ap[:, ia, :], rhs=xsb.ap[:, c0:c0 + w], start=True, stop=True), reads=[wbd, xsb], writes=[pa])
                    P.emit("pe", lambda e, ix=ix, px=px, c0=c0, w=w: e.matmul(px.ap[:, 0:w], lhsT=wbd.ap[:, ix, :], rhs=xsb.ap[:, c0:c0 + w], start=True, stop=True), reads=[wbd, xsb], writes=[px])
                    ta, tx = tq[2 * (bj % 2)], tq[2 * (bj % 2) + 1]
                    dc = d * 2 + c
                    sigmoid_ops(P, ta, ta.ap[:, 0:w], pa.ap[:, 0:w], [pa, small], bias=sm[:, 28 + dc:29 + dc], eng2="act")
                    P.emit("act", lambda e, w=w, dc=dc, c0=c0: e.activation(out=At.ap[:, c0:c0 + w], in_=ta.ap[:, 0:w], func=AF.Exp, scale=sm[:, 36 + dc:37 + dc]), reads=[ta, small], writes=[At])
                    sigmoid_ops(P, tx, tx.ap[:, 0:w], px.ap[:, 0:w], [px, small], bias=sm[:, 32 + dc:33 + dc], eng2="act")
                    P.emit("pool", lambda e, w=w, c0=c0: e.tensor_tensor(out=tx.ap[:, 0:w], in0=tx.ap[:, 0:w], in1=xs.ap[:, c0:c0 + w], op=ALU.mult), reads=[tx, xs], writes=[tx])
                    P.emit("dve", lambda e, w=w, c0=c0: e.tensor_tensor(out=ta.ap[:, 0:w], in0=At.ap[:, c0:c0 + w], in1=At.ap[:, c0:c0 + w], op=ALU.mult), reads=[At], writes=[ta])
                    P.emit("dve", lambda e, w=w: e.tensor_scalar(out=ta.ap[:, 0:w], in0=ta.ap[:, 0:w], scalar1=-1.0, scalar2=1.0, op0=ALU.mult, op1=ALU.add), reads=[ta], writes=[ta])
                    P.emit("act", lambda e, w=w: e.activation(out=ta.ap[:, 0:w], in_=ta.ap[:, 0:w], func=AF.Ln), reads=[ta], writes=[ta])
                    P.emit("act", lambda e, w=w: e.activation(out=ta.ap[:, 0:w], in_=ta.ap[:, 0:w], func=AF.Exp, scale=0.5), reads=[ta], writes=[ta])
                    P.emit("dve", lambda e, w=w, c0=c0: e.tensor_tensor(out=Ut.ap[:, c0:c0 + w], in0=ta.ap[:, 0:w], in1=tx.ap[:, 0:w], op=ALU.mult), reads=[ta, tx], writes=[Ut])
                SC = 1024
                if d == 0:
                    pieces = [(NLAT, NCTX)] + [(i * SC, SC) for i in range(NLAT // SC)]
                else:
                    pieces = [(NLAT, NCTX)] + [(i * SC, SC) for i in range(NLAT // SC - 1, -1, -1)]
                prev_last = None
                for (p0, pl) in pieces:
                    def view(t, p0=p0, pl=pl):
                        a = t.ap[:, p0:p0 + pl]
                        if d == 0:
                            return a
                        return AP(a.tensor, a.offset + pl - 1, [list(a.ap[0]), [-1, pl]])
                    init = 0.0 if prev_last is None else Hd.ap[:, prev_last:prev_last + 1]
                    P.emit("dve", lambda e, view=view, init=init, Hd=Hd: e.tensor_tensor_scan(out=view(Hd), data0=view(At), data1=view(Ut), initial=init, op0=ALU.mult, op1=ALU.add),
                           reads=[At, Ut, Hd], writes=[Hd])
                    prev_last = (p0 + pl - 1) if d == 0 else p0
            for bj, (c0, w) in enumerate(BLOCKS):
                ta, tx = tq[2 * (bj % 2)], tq[2 * (bj % 2) + 1]
                yc = ycb[bj % 2]
                g = gb.ap[:, c0:c0 + w]
                P.emit("pool", lambda e, g=g, w=w: e.tensor_tensor(out=ta.ap[:, 0:w], in0=g, in1=g, op=ALU.mult), reads=[gb], writes=[ta])
                P.emit("dve", lambda e, w=w: e.tensor_scalar(out=ta.ap[:, 0:w], in0=ta.ap[:, 0:w], scalar1=0.044715 * 0.7978845608028654, scalar2=0.7978845608028654, op0=ALU.mult, op1=ALU.add), reads=[ta], writes=[ta])
                P.emit("dve", lambda e, g=g, w=w: e.tensor_tensor(out=ta.ap[:, 0:w], in0=ta.ap[:, 0:w], in1=g, op=ALU.mult), reads=[ta, gb], writes=[ta])
                sigmoid_ops(P, ta, ta.ap[:, 0:w], ta.ap[:, 0:w], [ta], scale=2.0, eng2="act")
                P.emit("dve", lambda e, g=g, w=w: e.tensor_tensor(out=ta.ap[:, 0:w], in0=ta.ap[:, 0:w], in1=g, op=ALU.mult), reads=[ta, gb], writes=[ta])
                P.emit("pool", lambda e, w=w, c0=c0: e.tensor_tensor(out=tx.ap[:, 0:w], in0=Hf.ap[:, c0:c0 + w], in1=Hb.ap[:, c0:c0 + w], op=ALU.add), reads=[Hf, Hb], writes=[tx])
                P.emit("dve", lambda e, w=w, yc=yc: e.tensor_tensor(out=yc.ap[:, 0:w], in0=ta.ap[:, 0:w], in1=tx.ap[:, 0:w], op=ALU.mult), reads=[ta, tx], writes=[yc])
                dma("pool", MT_d[6 + c, :, c0:c0 + w], yc.ap[:, 0:w], [yc], [MTt[6 + c][bj]])
        P.barrier()
        P.release(m)
        if debug and l == 0:
            mm = P.mark()
            dtile = [P.alloc([128, 8, 512], BF16, f"dbgm{i}") for i in range(2)]
            for bj, (c0, w) in enumerate(BLOCKS):
                dt_ = dtile[bj % 2]
                dma("sp", dt_.ap[:, :, 0:w], MT_d[:, :, c0:c0 + w].rearrange("k p n -> p k n"), [MTt[k][bj] for k in range(8)], [dt_])
                dma("pool", dbg["MT"][:, :, c0:c0 + w].rearrange("k p n -> p k n"), dt_.ap[:, :, 0:w], [dt_], [])
            P.barrier()
            P.release(mm)

        phase_done('P5')
        m = P.mark()
        wo = P.alloc([128, 8, D], BF16, "wo")
        load_cast(wo, wo.ap, wout_d[l].rearrange("(k p) n -> p k n", p=128), 8, D)
        mixb = [P.alloc([128, 8, 512], BF16, f"mixb{i}") for i in range(2)]
        xts = [P.alloc([128, D], F32, f"xto{i}") for i in range(2)]
        x1s = [P.alloc([128, D], F32, f"x1{i}") for i in range(2)]
        tmpo = P.alloc([128, D], F32, "tmpo")
        lnw = [(P.alloc([128, D], F32, f"sq2{i}"), P.alloc([128, 1], F32, f"ssum2{i}"), P.alloc([128, 1], F32, f"rs2{i}"), P.alloc([128, D], BF16, f"xn2{i}")) for i in range(2)]
        uTb = [P.alloc([128, 8, 512], BF16, f"uTb2{i}") for i in range(2)]
        for bj, (c0, w) in enumerate(BLOCKS):
            mb = mixb[bj % 2]
            ub = uTb[bj % 2]
            col = 1 if bj == 8 else 0
            dma("sp", mb.ap[:, :, 0:w], MT_d[:, :, c0:c0 + w].rearrange("k p n -> p k n"), [MTt[k][bj] for k in range(8)], [mb])
            for tt in range(w // 128):
                ti = c0 // 128 + tt
                xt, x1 = xts[ti % 2], x1s[ti % 2]
                dma("sp", xt.ap, xsrc(l)[ti * 128:(ti + 1) * 128, :], [Xt[ti]], [xt])
                for half in range(2):
                    pt = ps[2 + half]
                    for k in range(8):
                        P.emit("pe", lambda e, k=k, half=half, pt=pt, tt=tt, mb=mb: e.matmul(pt.ap, lhsT=mb.ap[:, k, tt * 128:(tt + 1) * 128], rhs=wo.ap[:, k, half * 512:(half + 1) * 512], start=(k == 0), stop=(k == 7)),
                               reads=[mb, wo], writes=[pt])
                    P.emit("dve", lambda e, half=half, pt=pt, col=col: e.tensor_tensor(out=tmpo.ap[:, half * 512:(half + 1) * 512], in0=pt.ap, in1=gbt.ap[:, col, half * 512:(half + 1) * 512], op=ALU.mult),
                           reads=[pt, gbt], writes=[tmpo])
                P.emit("pool", lambda e, xt=xt, x1=x1: e.tensor_tensor(out=x1.ap, in0=tmpo.ap, in1=xt.ap, op=ALU.add), reads=[tmpo, xt], writes=[x1])
                dma("pool", X_d[ti * 128:(ti + 1) * 128, :], x1.ap, [x1], [Xt[ti]])
                ln_tile(x1, ub, ub.ap, tt * 128, 24, 32, col, lnw[ti % 2] + (ps[ti % 2],))
            dma("pool", UT_d[:, :, c0:c0 + w].rearrange("k p n -> p k n"), ub.ap[:, :, 0:w], [ub], [UTt[bj]])
        P.barrier()
        P.release(m)

        phase_done('P6')
        last_layer = (l == L - 1)
        for hhalf in range(2):
            m = P.mark()
            w1 = P.alloc([128, 8, 2048], BF16, "w1")
            load_cast(w1, w1.ap, wff1_d[l][:, hhalf * 2048:(hhalf + 1) * 2048].rearrange("(k p) n -> p k n", p=128), 8, 2048)
            w2 = P.alloc([128, 16, D], BF16, "w2")
            load_cast(w2, w2.ap, wff2_d[l][hhalf * 2048:(hhalf + 1) * 2048, :].rearrange("(k p) n -> p k n", p=128), 16, D)
            ublk = [P.alloc([128, 8, 512], BF16, f"ublkf{i}") for i in range(2)]
            hT = [P.alloc([128, 16, 512], BF16, f"hT{i}") for i in range(2)]
            xts = [P.alloc([128, D], F32, f"xtf{i}") for i in range(2)]
            x2s = [P.alloc([128, D], F32, f"x2{i}") for i in range(2)]
            rtmp = [P.alloc([128, 512], F32, f"rtmp{i}") for i in range(2)]
            fg = None
            if last_layer and hhalf == 1:
                fg = P.alloc([128, D], F32, "fg")
                dma("sp", fg.ap, fing_d, [], [fg])
                sq = P.alloc([128, D], F32, "sq3")
                ssum = P.alloc([128, 1], F32, "ssum3")
                rs = P.alloc([128, 1], F32, "rs3")
                xo = [P.alloc([128, D], F32, f"xo{i}") for i in range(2)]
            for bj, (c0, w) in enumerate(BLOCKS):
                if last_layer and bj == 8:
                    continue
                ub = ublk[bj % 2]
                ht = hT[bj % 2]
                col = 1 if bj == 8 else 0
                dma("sp", ub.ap[:, :, 0:w], UT_d[:, :, c0:c0 + w].rearrange("k p n -> p k n"), [UTt[bj]], [ub])
                for j in range(16):
                    pt = ps[j % 2]
                    for k in range(8):
                        P.emit("pe", lambda e, k=k, j=j, pt=pt, w=w, ub=ub: e.matmul(pt.ap[:, 0:w], lhsT=w1.ap[:, k, j * 128:(j + 1) * 128], rhs=ub.ap[:, k, 0:w], start=(k == 0), stop=(k == 7)),
                               reads=[w1, ub], writes=[pt])
                    rt = rtmp[j % 2]
                    P.emit("act", lambda e, pt=pt, w=w, rt=rt: e.activation(out=rt.ap[:, 0:w], in_=pt.ap[:, 0:w], func=AF.Relu), reads=[pt], writes=[rt])
                    P.emit("dve" if j % 2 == 0 else "pool", lambda e, j=j, w=w, ht=ht, rt=rt: e.tensor_tensor(out=ht.ap[:, j, 0:w], in0=rt.ap[:, 0:w], in1=rt.ap[:, 0:w], op=ALU.mult), reads=[rt], writes=[ht])
                for tt in range(w // 128):
                    ti = c0 // 128 + tt
                    xt, x2 = xts[ti % 2], x2s[ti % 2]
                    dma("sp", xt.ap, X_d[ti * 128:(ti + 1) * 128, :], [Xt[ti]], [xt])
                    for half in range(2):
                        pt = ps[2 + half + 2 * (ti % 2)]
                        for k in range(16):
                            P.emit("pe", lambda e, k=k, half=half, pt=pt, tt=tt, ht=ht: e.matmul(pt.ap, lhsT=ht.ap[:, k, tt * 128:(tt + 1) * 128], rhs=w2.ap[:, k, half * 512:(half + 1) * 512], start=(k == 0), stop=(k == 15)),
                                   reads=[ht, w2], writes=[pt])
                        P.emit("dve", lambda e, half=half, pt=pt, col=col, x2=x2: e.tensor_tensor(out=x2.ap[:, half * 512:(half + 1) * 512], in0=pt.ap, in1=gbt.ap[:, 2 + col, half * 512:(half + 1) * 512], op=ALU.mult),
                               reads=[pt, gbt], writes=[x2])
                    P.emit("pool", lambda e, xt=xt, x2=x2: e.tensor_tensor(out=x2.ap, in0=x2.ap, in1=xt.ap, op=ALU.add), reads=[x2, xt], writes=[x2])
                    if fg is None:
                        dma("pool", X_d[ti * 128:(ti + 1) * 128, :], x2.ap, [x2], [Xt[ti]])
                    else:
                        o = xo[ti % 2]
                        P.emit("act", lambda e, x2=x2: e.activation(out=sq.ap, in_=x2.ap, func=AF.Square), reads=[x2], writes=[sq])
                        P.emit("dve", lambda e: e.reduce_sum(out=ssum.ap, in_=sq.ap, axis=AX.X), reads=[sq], writes=[ssum])
                        rsqrt_ops(P, rs, ssum.ap, ssum, rs.ap, D * EPS)
                        P.emit("dve", lambda e, x2=x2, o=o: e.tensor_scalar(out=o.ap, in0=x2.ap, scalar1=rs.ap[:, 0:1], scalar2=32.0, op0=ALU.mult, op1=ALU.mult), reads=[x2, rs], writes=[o])
                        P.emit("pool", lambda e, o=o: e.tensor_tensor(out=o.ap, in0=o.ap, in1=fg.ap, op=ALU.mult), reads=[o, fg], writes=[o])
                        dma("pool", out_d[ti * 128:(ti + 1) * 128, :], o.ap, [o], [OUTt[ti]])
            P.barrier()
            P.release(m)
        if debug and l == 0:
            mm = P.mark()
            dx = [P.alloc([128, D], F32, f"dbgx{i}") for i in range(2)]
            for ti in range(NTILE):
                dma("sp", dx[ti % 2].ap, X_d[ti * 128:(ti + 1) * 128, :], [Xt[ti]], [dx[ti % 2]])
                dma("pool", dbg["X"][ti * 128:(ti + 1) * 128, :], dx[ti % 2].ap, [dx[ti % 2]], [])
            P.barrier()
            P.release(mm)

      except _Stop as ex:
        print('build stopped after phase', ex)
        if debug:
            P.barrier()
            P.sb_off = 16512
            dtile = [P.alloc([128, 8, 512], BF16, f"dbgs{i}") for i in range(2)]
            for bj, (c0, w) in enumerate(BLOCKS):
                dt_ = dtile[bj % 2]
                dma("sp", dt_.ap[:, :, 0:w], MT_d[:, :, c0:c0 + w].rearrange("k p n -> p k n"), [MTt[k][bj] for k in range(8)], [dt_])
                dma("pool", dbg["MT"][:, :, c0:c0 + w].rearrange("k p n -> p k n"), dt_.ap[:, :, 0:w], [dt_], [])
        break

    P.barrier()
    P.finalize()
    return nc


def _rope_tables():
    t = np.arange(NLAT)
    row = (t // 64).astype(np.float32)
    colp = (t % 64).astype(np.float32)
    half = 32
    freqs = (1.0 / (10000.0 ** (np.arange(0, half, 2, dtype=np.float32) / half))).astype(np.float32)
    ang = np.concatenate([row[:, None] * freqs, colp[:, None] * freqs], axis=-1)
    cos, sin = np.cos(ang).astype(np.float32), np.sin(ang).astype(np.float32)
    C = np.ones((128, NT), np.float32)
    S = np.zeros((128, NT), np.float32)
    for r in range(128):
        rr = r % 64
        j = rr % 32
        C[r, :NLAT] = cos[:, j]
        S[r, :NLAT] = (-sin[:, j]) if rr < 32 else sin[:, j]
    return C, S


def _prep_shared(inp, L):
    f = lambda a: np.ascontiguousarray(np.asarray(a, dtype=np.float32))
    w_in = f(inp["w_in"])[:L]
    sw64 = np.concatenate([np.arange(32, 64), np.arange(0, 32)])
    kr = 384 + np.arange(64)
    fm_cols = np.concatenate([
        np.arange(0, 256), np.arange(256, 384), kr, kr, kr[sw64], kr[sw64],
        448 + np.arange(256), 704 + np.arange(256), 1488 + np.arange(256), 1744 + np.arange(256)])
    tm_cols = np.concatenate([704 + np.arange(256), 960 + np.arange(256), 1216 + np.arange(256), 1472 + np.arange(16)])
    w_uq = f(inp["mla_w_uq"])[:L]
    uq_cols = []
    for h in range(4):
        uq_cols.append(h * 192 + np.arange(128))
    rope = lambda h: h * 192 + 128 + np.arange(64)
    uq_cols += [rope(0), rope(1), rope(2), rope(3), rope(0)[sw64], rope(1)[sw64], rope(2)[sw64], rope(3)[sw64]]
    uq_cols = np.concatenate(uq_cols)
    w_ukv = f(inp["mla_w_ukv"])[:L]
    kcols = np.concatenate([h * 256 + np.arange(128) for h in range(4)])
    vcols = np.concatenate([h * 256 + 128 + np.arange(128) for h in range(4)])
    wa, wx = f(inp["lru_w_a"])[:L], f(inp["lru_w_x"])[:L]
    wbd = np.zeros((L, 128, 8, 128), np.float32)
    for d in range(2):
        for ax, wsrc in enumerate((wa, wx)):
            for c in range(2):
                idx = (d * 2 + ax) * 2 + c
                for g in range(2):
                    wbd[:, g * 64:(g + 1) * 64, idx, g * 64:(g + 1) * 64] = wsrc[:, d, 2 * c + g]
    small = np.zeros((128, L, 40), np.float32)
    gq, gkv = f(inp["mla_g_q"])[:L], f(inp["mla_g_kv"])[:L]
    cwv, cbv = f(inp["lru_conv_w"])[:L], f(inp["lru_conv_b"])[:L]
    ba, bx, lam = f(inp["lru_b_a"])[:L], f(inp["lru_b_x"])[:L], f(inp["lru_lam"])[:L]
    for l in range(L):
        small[:, l, 0:2] = gq[l].reshape(2, 128).T
        small[:, l, 2] = gkv[l]
        for c in range(2):
            for j in range(4):
                small[:, l, 3 + c * 4 + j] = cwv[l, j, c * 128:(c + 1) * 128]
            small[:, l, 11 + c] = cbv[l, c * 128:(c + 1) * 128]
            for d in range(2):
                small[:, l, 13 + d * 2 + c] = ba[l, d, c * 128:(c + 1) * 128]
                small[:, l, 17 + d * 2 + c] = bx[l, d, c * 128:(c + 1) * 128]
                small[:, l, 21 + d * 2 + c] = lam[l, d, c * 128:(c + 1) * 128]
    gbias = np.ascontiguousarray(np.broadcast_to(f(inp["ml_gate_bias"])[:L][None], (128, L, 16)))
    consts = np.zeros((128, 4, 128), np.float32)
    consts[:, 0, :] = np.eye(128, dtype=np.float32)
    consts[:, 1, :] = np.triu(np.ones((128, 128), np.float32))
    consts[:, 2, :] = np.tril(np.ones((128, 128), np.float32))
    consts[:, 3, :] = 1.0
    C, S = _rope_tables()
    return {
        "w_mod": f(inp["w_mod"])[:L], "b_mod": f(inp["b_mod"])[:L],
        "w_in_fm": np.ascontiguousarray(w_in[:, :, fm_cols]), "w_in_tm": np.ascontiguousarray(w_in[:, :, tm_cols]),
        "w_uq": np.ascontiguousarray(w_uq[:, :, uq_cols]),
        "w_k": np.ascontiguousarray(w_ukv[:, :, kcols]), "w_v": np.ascontiguousarray(w_ukv[:, :, vcols]),
        "w_out": f(inp["w_out"])[:L], "w_ff1": f(inp["w_ff1"])[:L], "w_ff2": f(inp["w_ff2"])[:L],
        "w_bd": wbd, "small": small, "gbias": gbias,
        "final_g": np.ascontiguousarray(np.broadcast_to(f(inp["final_g"])[None], (128, D))),
        "ropeC": C, "ropeS": S, "consts": consts,
    }


_NC_CACHE = {}


def run(inputs, depth=4, debug=False, n_cores=8):
    shared = _prep_shared(inputs, depth)
    x, c, ctx, c_ctx = (np.asarray(inputs[k], dtype=np.float32) for k in ("x", "c", "ctx", "c_ctx"))
    in_maps = []
    for core in range(n_cores):
        b = core % 4
        mm = dict(shared)
        mm["xin"] = np.ascontiguousarray(np.concatenate([x[b], ctx[b]], axis=0))
        cc = np.stack([c[b].reshape(8, 128).T, c_ctx.reshape(8, 128).T], axis=-1)
        mm["cc"] = np.ascontiguousarray(cc.astype(np.float32))
        in_maps.append(mm)
    key = (depth, debug)
    if key not in _NC_CACHE:
        _NC_CACHE[key] = build(depth, debug)
    res = run_bass_kernel_spmd(_NC_CACHE[key], in_maps, core_ids=list(range(n_cores)))
    return res


def kernel(**inputs):
    res = run(inputs)
    out = np.stack([np.asarray(res.results[b]["out"], dtype=np.float32) for b in range(4)], axis=0)
    return out
```

```python
import numpy as np
from contextlib import ExitStack
import concourse.bass as bass
import concourse.mybir as mybir
from concourse.bass_utils import run_bass_kernel_spmd
from concourse.ap import AP

F32 = mybir.dt.float32
BF16 = mybir.dt.bfloat16
AF = mybir.ActivationFunctionType
ALU = mybir.AluOpType
AX = mybir.AxisListType

D = 1024
NLAT = 4096
NCTX = 256
NT = NLAT + NCTX
NTILE = NT // 128
EPS = 1e-6
DFF = 4096
SCALE = (128 + 64) ** -0.5
ENG = ("pe", "act", "dve", "pool", "sp")
NSLOT = 24
BLOCKS = [(j * 512, 512) for j in range(8)] + [(4096, 256)]


class _Rec:
    def __getattr__(self, name):
        return lambda *a, **k: (name, a, k)


_REC = _Rec()


class T:
    __slots__ = ("ap", "w", "r")

    def __init__(self, ap):
        self.ap = ap
        self.w = []
        self.r = {}


class Prog:
    def __init__(self, nc):
        self.nc = nc
        self.ops = {e: [] for e in ENG}
        self.cnt = {}
        self.seen = {e: {} for e in ENG}
        self.layer = 0
        self.slot_cum = [0] * NSLOT
        self.rr = 0
        self.sb_off = 16512
        self.nalloc = 0

    def alloc(self, shape, dtype, name=None):
        nbytes = int(np.prod(shape[1:])) * (2 if dtype == BF16 else 4)
        off = (self.sb_off + 63) // 64 * 64
        self.sb_off = off + nbytes
        assert self.sb_off <= 229344, ("sbuf overflow", self.sb_off, name)
        self.nalloc += 1
        h = self.nc.alloc_sbuf_tensor_at(f"sb{self.nalloc}_{name or ''}", list(shape), dtype, offset=off)
        return T(h[:] if hasattr(h, "__getitem__") else h.ap())

    def mark(self):
        return self.sb_off

    def release(self, m):
        self.sb_off = m

    def emit(self, eng, fn, reads=(), writes=(), dma=False):
        raw, oth = {}, {}

        def add(d, tok):
            k, v = tok
            if d.get(k, 0) < v:
                d[k] = v

        for t in reads:
            for tok in t.w:
                add(raw, tok)
        for t in writes:
            for tok in t.w:
                add(oth, tok)
            for k, v in t.r.items():
                add(oth, (k, v))
        if dma:
            slot = self.rr % NSLOT
            self.rr += 1
            prev = self.slot_cum[slot]
            if prev:
                add(oth, (("dma", slot), prev))
            self.slot_cum[slot] = prev + 16
            mytok = (("dma", slot), prev + 16)
        else:
            key = (eng, self.layer)
            self.cnt[key] = self.cnt.get(key, 0) + 1
            mytok = (key, self.cnt[key])
        waits = []
        seen = self.seen[eng]
        for d, is_raw in ((raw, True), (oth, False)):
            for k, v in d.items():
                if (not is_raw) and (not dma) and k[0] == eng:
                    continue
                if seen.get(k, 0) >= v:
                    continue
                seen[k] = v
                waits.append((k, v))
        self.ops[eng].append((waits, fn(_REC), mytok, dma))
        for t in reads:
            k, v = mytok
            if t.r.get(k, 0) < v:
                t.r[k] = v
        for t in writes:
            t.w = [mytok]
            t.r = {}
        return mytok

    def barrier(self):
        toks = dict(self.cnt)
        for s in range(NSLOT):
            if self.slot_cum[s]:
                toks[("dma", s)] = self.slot_cum[s]
        for e in ENG:
            waits = []
            seen = self.seen[e]
            for k, v in toks.items():
                if k[0] == e:
                    continue
                if seen.get(k, 0) >= v:
                    continue
                seen[k] = v
                waits.append((k, v))
            if waits:
                self.ops[e].append((waits, None, None, False))

    def finalize(self):
        nc = self.nc
        keys = set(self.cnt.keys())
        for s in range(NSLOT):
            keys.add(("dma", s))
        with ExitStack() as st:
            sems = {}
            for k in sorted(keys, key=str):
                sems[k] = st.enter_context(nc.semaphore(f"s_{k[0]}_{k[1]}"))
            block = st.enter_context(nc.Block())

            def replay(name):
                def run(e):
                    for waits, fn, tok, dma in self.ops[name]:
                        for k, v in waits:
                            e.wait_ge(sems[k], v)
                        if fn is not None:
                            ins = getattr(e, fn[0])(*fn[1], **fn[2])
                            ins.then_inc(sems[tok[0]], 16 if dma else 1)
                return run

            block.tensor(replay("pe"))
            block.scalar(replay("act"))
            block.vector(replay("dve"))
            block.gpsimd(replay("pool"))
            block.sync(replay("sp"))


def bc_last(ap, n):
    return AP(ap.tensor, ap.offset, [list(x) for x in ap.ap] + [[0, n]])


class _Stop(Exception):
    pass


def build(depth=4, debug=False, stop=None):
    nc = bass.Bass("TRN2", target_bir_lowering=False)
    P = Prog(nc)
    L = depth

    def din(name, shape, dt=F32):
        return nc.dram_tensor(name, list(shape), dt, kind="ExternalInput").ap()

    xin = din("xin", [NT, D])
    cc_d = din("cc", [128, 8, 2])
    wmod_d = din("w_mod", [L, D, 6 * D])
    bmod_d = din("b_mod", [L, 6 * D])
    winfm_d = din("w_in_fm", [L, D, 13 * 128])
    wintm_d = din("w_in_tm", [L, D, 784])
    wuq_d = din("w_uq", [L, 256, 1024])
    wk_d = din("w_k", [L, 128, 512])
    wv_d = din("w_v", [L, 128, 512])
    wout_d = din("w_out", [L, D, D])
    wff1_d = din("w_ff1", [L, D, DFF])
    wff2_d = din("w_ff2", [L, DFF, D])
    wbd_d = din("w_bd", [L, 128, 8, 128])
    small_d = din("small", [128, L, 40])
    gbias_d = din("gbias", [128, L, 16])
    fing_d = din("final_g", [128, D])
    ropeC_d = din("ropeC", [128, NT])
    ropeS_d = din("ropeS", [128, NT])
    const_d = din("consts", [128, 4, 128])
    out_d = nc.dram_tensor("out", [NLAT, D], F32, kind="ExternalOutput").ap()
    X_d = nc.dram_tensor("Xs", [NT, D], F32, kind="Internal").ap()
    UT_d = nc.dram_tensor("UTs", [8, 128, NT], BF16, kind="Internal").ap()
    MT_d = nc.dram_tensor("MTs", [8, 128, NT], BF16, kind="Internal").ap()
    dbg = {}
    if debug:
        dbg["UT"] = nc.dram_tensor("dbgUT", [8, 128, NT], BF16, kind="ExternalOutput").ap()
        dbg["MT"] = nc.dram_tensor("dbgMT", [8, 128, NT], BF16, kind="ExternalOutput").ap()
        dbg["X"] = nc.dram_tensor("dbgX", [NT, D], F32, kind="ExternalOutput").ap()

    Xt = [T(None) for _ in range(NTILE)]
    UTt = [T(None) for _ in range(9)]
    MTt = [[T(None) for _ in range(9)] for _ in range(8)]
    OUTt = [T(None) for _ in range(32)]

    ps = []
    for i in range(8):
        h = nc.alloc_psum_tensor(f"ps{i}", [128, 512], F32)
        ps.append(T(h[:]))

    def psb(t):
        return t.ap.bitcast(BF16)

    consts = P.alloc([128, 4, 128], F32, "consts")
    identF = consts.ap[:, 0, :]
    maskU = consts.ap[:, 1, :]
    maskL = consts.ap[:, 2, :]
    onesF = consts.ap[:, 3, :]
    cb = P.alloc([128, 2, 128], BF16, "constb")
    identB = cb.ap[:, 0, :]
    onesB = cb.ap[:, 1, :]
    small = P.alloc([128, L, 40], F32, "small")
    gbias = P.alloc([128, L, 16], F32, "gbias")
    siluT = P.alloc([128, 8, 33], F32, "siluT")
    modT = P.alloc([128, 48, 2], F32, "modT")
    gbt = P.alloc([128, 4, D], F32, "gbt")
    stage = [P.alloc([128, 2048], F32, f"stage{i}") for i in range(2)]
    stage_rr = [0]


    def rsqrt_ops(P_, dst, src_ap, src_T, shape_ap, epsv):
        P_.emit("dve", lambda e: e.tensor_scalar(out=shape_ap, in0=src_ap, scalar1=epsv, scalar2=None, op0=ALU.add), reads=[src_T], writes=[dst])
        P_.emit("act", lambda e: e.activation(out=shape_ap, in_=shape_ap, func=AF.Ln), reads=[dst], writes=[dst])
        P_.emit("act", lambda e: e.activation(out=shape_ap, in_=shape_ap, func=AF.Exp, scale=-0.5), reads=[dst], writes=[dst])

    def sigmoid_ops(P_, dst, dst_ap, src_ap, src_Ts, scale=1.0, bias=None, eng2="dve"):
        if bias is None:
            P_.emit("act", lambda e: e.activation(out=dst_ap, in_=src_ap, func=AF.Exp, scale=-scale), reads=list(src_Ts), writes=[dst])
        else:
            P_.emit("act", lambda e: e.activation(out=dst_ap, in_=src_ap, func=AF.Exp, scale=-scale, bias=bias), reads=list(src_Ts), writes=[dst])
        if eng2 == "act":
            P_.emit("act", lambda e: e.activation(out=dst_ap, in_=dst_ap, func=AF.Identity, bias=1.0), reads=[dst], writes=[dst])
        else:
            P_.emit(eng2, lambda e: e.tensor_scalar(out=dst_ap, in0=dst_ap, scalar1=1.0, scalar2=None, op0=ALU.add), reads=[dst], writes=[dst])
        P_.emit("dve", lambda e: e.reciprocal(out=dst_ap, in_=dst_ap), reads=[dst], writes=[dst])

    def dma(eng, out_ap, in_ap, reads, writes):
        P.emit(eng, lambda e: e.dma_start(out=out_ap, in_=in_ap), reads=reads, writes=writes, dma=True)

    def load_cast(dst_T, dst_ap3, src_ap3, nk, ncols):
        per = max(1, 2048 // ncols)
        k = 0
        while k < nk:
            kk = min(per, nk - k)
            st = stage[stage_rr[0] % 2]
            stage_rr[0] += 1
            sv = st.ap[:, 0:kk * ncols].rearrange("p (k n) -> p k n", k=kk)
            dma("sp", sv, src_ap3[:, k:k + kk, :], [], [st])
            d = dst_ap3[:, k:k + kk, :]
            if stage_rr[0] % 2 == 0:
                P.emit("pool", lambda e, d=d, sv=sv: e.tensor_copy(out=d, in_=sv), reads=[st], writes=[dst_T])
            else:
                P.emit("act", lambda e, d=d, sv=sv: e.activation(out=d, in_=sv, func=AF.Identity), reads=[st], writes=[dst_T])
            k += kk

    dma("sp", consts.ap, const_d, [], [consts])
    dma("sp", small.ap, small_d, [], [small])
    dma("sp", gbias.ap, gbias_d, [], [gbias])
    P.emit("dve", lambda e: e.tensor_copy(out=identB, in_=identF), reads=[consts], writes=[cb])
    P.emit("dve", lambda e: e.tensor_copy(out=onesB, in_=onesF), reads=[consts], writes=[cb])
    m0 = P.mark()
    cct = P.alloc([128, 8, 2], F32, "cct")
    cth = P.alloc([128, 8, 2], F32, "cth")
    dma("sp", cct.ap, cc_d, [], [cct])
    P.emit("pool", lambda e: e.memset(siluT.ap, 0.0), writes=[siluT])
    sigmoid_ops(P, cth, cth.ap, cct.ap, [cct])
    for j, col in ((0, 0), (1, 32)):
        P.emit("dve", lambda e, j=j, col=col: e.tensor_tensor(out=siluT.ap[:, :, col], in0=cth.ap[:, :, j], in1=cct.ap[:, :, j], op=ALU.mult),
               reads=[cth, cct, siluT], writes=[siluT])
    for l in range(L):
        s = small.ap[:, l, :]
        P.emit("dve", lambda e, s=s: e.tensor_scalar(out=s[:, 25:27], in0=s[:, 0:2], scalar1=16.0, scalar2=None, op0=ALU.mult), reads=[small], writes=[small])
        P.emit("dve", lambda e, s=s: e.tensor_scalar(out=s[:, 27:28], in0=s[:, 2:3], scalar1=float(np.sqrt(128.0)), scalar2=None, op0=ALU.mult), reads=[small], writes=[small])
        P.emit("dve", lambda e, s=s: e.tensor_scalar(out=s[:, 28:36], in0=s[:, 13:21], scalar1=-1.0, scalar2=None, op0=ALU.mult), reads=[small], writes=[small])
        P.emit("act", lambda e, s=s: e.activation(out=s[:, 36:40], in_=s[:, 21:25], func=AF.Exp, scale=-1.0), reads=[small], writes=[small])
        P.emit("act", lambda e, s=s: e.activation(out=s[:, 36:40], in_=s[:, 36:40], func=AF.Ln, bias=1.0), reads=[small], writes=[small])
        P.emit("dve", lambda e, s=s: e.tensor_scalar(out=s[:, 36:40], in0=s[:, 36:40], scalar1=-8.0, scalar2=None, op0=ALU.mult), reads=[small], writes=[small])
    P.barrier()
    P.release(m0)
    base_mark = P.mark()

    def xsrc(l):
        return xin if l == 0 else X_d

    def ln_tile(xt, uT_T, uT_ap, c0, shc, scc, col, work):
        sq, ssum, rs, xn, pst = work
        P.emit("act", lambda e: e.activation(out=sq.ap, in_=xt.ap, func=AF.Square), reads=[xt], writes=[sq])
        P.emit("dve", lambda e: e.reduce_sum(out=ssum.ap, in_=sq.ap, axis=AX.X), reads=[sq], writes=[ssum])
        rsqrt_ops(P, rs, ssum.ap, ssum, rs.ap, D * EPS)
        P.emit("dve", lambda e: e.tensor_scalar(out=xn.ap, in0=xt.ap, scalar1=rs.ap[:, 0:1], scalar2=32.0, op0=ALU.mult, op1=ALU.mult),
               reads=[xt, rs], writes=[xn])
        pv = psb(pst)
        for k in range(8):
            P.emit("pe", lambda e, k=k: e.transpose(pv[:, k * 128:(k + 1) * 128], xn.ap[:, k * 128:(k + 1) * 128], identB),
                   reads=[xn, cb], writes=[pst])
        for k in range(8):
            P.emit("act", lambda e, k=k: e.activation(out=uT_ap[:, k, c0:c0 + 128], in_=pv[:, k * 128:(k + 1) * 128], func=AF.Identity,
                                                      scale=modT.ap[:, scc + k, col:col + 1], bias=modT.ap[:, shc + k, col:col + 1]),
                   reads=[pst, modT], writes=[uT_T])

    phase_ctr = [0]

    def phase_done(name):
        phase_ctr[0] += 1
        if stop is not None and phase_ctr[0] >= stop:
            raise _Stop(name)

    for l in range(L):
      try:
        P.layer = l
        sm = small.ap[:, l, :]
        P.release(base_mark)
        m = P.mark()
        modrow = P.alloc([33, 6 * D], F32, "modrow")
        bmrow = P.alloc([33, 6 * D], F32, "bmrow")
        wm = [P.alloc([128, 8, 512], F32, f"wm{i}") for i in range(2)]
        P.emit("pool", lambda e: e.memset(bmrow.ap, 0.0), writes=[bmrow])
        dma("sp", bmrow.ap[0:1, :], bmod_d[l:l + 1, :], [], [bmrow])
        dma("sp", bmrow.ap[32:33, :], bmod_d[l:l + 1, :], [], [bmrow])
        for nb in range(12):
            w = wm[nb % 2]
            dma("sp", w.ap, wmod_d[l, :, nb * 512:(nb + 1) * 512].rearrange("(k p) n -> p k n", p=128), [], [w])
            pt = ps[nb % 2]
            for k in range(8):
                P.emit("pe", lambda e, k=k, w=w, pt=pt: e.matmul(pt.ap[0:33, :], lhsT=siluT.ap[:, k, :], rhs=w.ap[:, k, :], start=(k == 0), stop=(k == 7)),
                       reads=[siluT, w], writes=[pt])
            P.emit("dve", lambda e, nb=nb, pt=pt: e.tensor_tensor(out=modrow.ap[:, nb * 512:(nb + 1) * 512], in0=pt.ap[0:33, :],
                                                                 in1=bmrow.ap[:, nb * 512:(nb + 1) * 512], op=ALU.add),
                   reads=[pt, bmrow], writes=[modrow])
        for c in list(range(0, 16)) + list(range(24, 40)):
            pt = ps[2 + c % 2]
            P.emit("pe", lambda e, c=c, pt=pt: e.transpose(pt.ap[:, 0:33], modrow.ap[:, c * 128:(c + 1) * 128], identF[0:33, 0:33]),
                   reads=[modrow, consts], writes=[pt])
            is_scale = (8 <= c < 16) or (32 <= c < 40)
            for j, col in ((0, 0), (1, 32)):
                P.emit("dve", lambda e, c=c, j=j, col=col, pt=pt, a=(1.0 if is_scale else 0.0):
                       e.tensor_scalar(out=modT.ap[:, c, j:j + 1], in0=pt.ap[:, col:col + 1], scalar1=a, scalar2=None, op0=ALU.add),
                       reads=[pt], writes=[modT])
        gi = 0
        for gcol in (2 * D, 5 * D):
            for row in (0, 32):
                for half in range(2):
                    pt = ps[4 + half]
                    P.emit("pe", lambda e, row=row, gcol=gcol, half=half, pt=pt:
                           e.matmul(pt.ap, lhsT=onesF[row:row + 1, :], rhs=modrow.ap[row:row + 1, gcol + half * 512: gcol + half * 512 + 512], start=True, stop=True),
                           reads=[consts, modrow], writes=[pt])
                    P.emit("act", lambda e, gi=gi, half=half, pt=pt: e.activation(out=gbt.ap[:, gi, half * 512:(half + 1) * 512], in_=pt.ap, func=AF.Identity),
                           reads=[pt], writes=[gbt])
                gi += 1
        P.barrier()
        P.release(m)

        phase_done('P0')
        m = P.mark()
        xts = [P.alloc([128, D], F32, f"xt{i}") for i in range(2)]
        lnw = [(P.alloc([128, D], F32, f"sq{i}"), P.alloc([128, 1], F32, f"ssum{i}"), P.alloc([128, 1], F32, f"rs{i}"), P.alloc([128, D], BF16, f"xn{i}")) for i in range(2)]
        uTb = [P.alloc([128, 8, 512], BF16, f"uTb{i}") for i in range(2)]
        for bj, (c0, w) in enumerate(BLOCKS):
            ub = uTb[bj % 2]
            col = 1 if bj == 8 else 0
            for tt in range(w // 128):
                ti = c0 // 128 + tt
                xt = xts[ti % 2]
                dma("sp", xt.ap, xsrc(l)[ti * 128:(ti + 1) * 128, :], [Xt[ti]], [xt])
                ln_tile(xt, ub, ub.ap, tt * 128, 0, 8, col, lnw[ti % 2] + (ps[ti % 2],))
            dma("pool", UT_d[:, :, c0:c0 + w].rearrange("k p n -> p k n"), ub.ap[:, :, 0:w], [ub], [UTt[bj]])
            if debug and l == 0:
                dma("pool", dbg["UT"][:, :, c0:c0 + w].rearrange("k p n -> p k n"), ub.ap[:, :, 0:w], [ub], [])
        P.barrier()
        P.release(m)

        phase_done('P1')
        m_mla = P.mark()
        wfm = P.alloc([128, 8, 5 * 128], BF16, "wfm")
        load_cast(wfm, wfm.ap, winfm_d[l][:, 0:5 * 128].rearrange("(k p) n -> p k n", p=128), 8, 5 * 128)
        wuq = P.alloc([128, 2, 1024], BF16, "wuq")
        load_cast(wuq, wuq.ap, wuq_d[l].rearrange("(k p) n -> p k n", p=128), 2, 1024)
        wkv = P.alloc([128, 2, 512], BF16, "wkv")
        load_cast(wkv, wkv.ap[:, 0:1, :], wk_d[l].rearrange("(k p) n -> p k n", p=128), 1, 512)
        load_cast(wkv, wkv.ap[:, 1:2, :], wv_d[l].rearrange("(k p) n -> p k n", p=128), 1, 512)
        knT = [P.alloc([128, 4, 512], BF16, f"knT{j}") for j in range(9)]
        krT = [P.alloc([128, 512], BF16, f"krT{j}") for j in range(9)]
        Vt = [P.alloc([128, 512], BF16, f"V{i}") for i in range(NTILE)]
        m_k = P.mark()
        ublk = [P.alloc([128, 8, 512], BF16, f"ublk{i}") for i in range(2)]
        ropc = [P.alloc([128, 512], F32, f"ropc{i}") for i in range(2)]
        rops = [P.alloc([128, 512], F32, f"rops{i}") for i in range(2)]
        sqb = P.alloc([128, 2, 512], BF16, "sqb")
        rstd = P.alloc([128, 512], F32, "rstd")
        ckvn = P.alloc([128, 512], BF16, "ckvn")
        t1 = P.alloc([128, 512], F32, "t1")
        t2 = P.alloc([128, 512], F32, "t2")

        def inproj_fm(ub, chunk, pt, w):
            for k in range(8):
                P.emit("pe", lambda e, k=k: e.matmul(pt.ap[:, 0:w], lhsT=wfm.ap[:, k, chunk * 128:(chunk + 1) * 128], rhs=ub.ap[:, k, 0:w],
                                                     start=(k == 0), stop=(k == 7)), reads=[wfm, ub], writes=[pt])

        for bj, (c0, w) in enumerate(BLOCKS):
            ub = ublk[bj % 2]
            rc, rsn = ropc[bj % 2], rops[bj % 2]
            dma("sp", ub.ap[:, :, 0:w], UT_d[:, :, c0:c0 + w].rearrange("k p n -> p k n"), [UTt[bj]], [ub])
            dma("sp", rc.ap[:, 0:w], ropeC_d[:, c0:c0 + w], [], [rc])
            dma("sp", rsn.ap[:, 0:w], ropeS_d[:, c0:c0 + w], [], [rsn])
            inproj_fm(ub, 2, ps[0], w)
            inproj_fm(ub, 3, ps[1], w)
            inproj_fm(ub, 4, ps[2], w)
            P.emit("act", lambda e, w=w: e.activation(out=sqb.ap[:, 0, 0:w], in_=ps[0].ap[:, 0:w], func=AF.Square), reads=[ps[0]], writes=[sqb])
            P.emit("pe", lambda e, w=w: e.matmul(ps[3].ap[:, 0:w], lhsT=onesB, rhs=sqb.ap[:, 0, 0:w], start=True, stop=True), reads=[cb, sqb], writes=[ps[3]])
            rsqrt_ops(P, rstd, ps[3].ap[:, 0:w], ps[3], rstd.ap[:, 0:w], 128 * EPS)
            P.emit("dve", lambda e, w=w: e.scalar_tensor_tensor(out=ckvn.ap[:, 0:w], in0=ps[0].ap[:, 0:w], scalar=sm[:, 27:28], in1=rstd.ap[:, 0:w], op0=ALU.mult, op1=ALU.mult),
                   reads=[ps[0], rstd, small], writes=[ckvn])
            for h in range(4):
                pt = ps[4 + h % 2]
                P.emit("pe", lambda e, h=h, pt=pt, w=w: e.matmul(pt.ap[:, 0:w], lhsT=wkv.ap[:, 0, h * 128:(h + 1) * 128], rhs=ckvn.ap[:, 0:w], start=True, stop=True),
                       reads=[wkv, ckvn], writes=[pt])
                P.emit("act", lambda e, h=h, pt=pt, w=w, bj=bj: e.activation(out=knT[bj].ap[:, h, 0:w], in_=pt.ap[:, 0:w], func=AF.Identity), reads=[pt], writes=[knT[bj]])
            for tt in range(w // 128):
                ti = c0 // 128 + tt
                pt = ps[6 + tt % 2]
                P.emit("pe", lambda e, tt=tt, pt=pt: e.matmul(pt.ap, lhsT=ckvn.ap[:, tt * 128:(tt + 1) * 128], rhs=wkv.ap[:, 1, :], start=True, stop=True),
                       reads=[ckvn, wkv], writes=[pt])
                P.emit("dve", lambda e, ti=ti, pt=pt: e.tensor_copy(out=Vt[ti].ap, in_=pt.ap), reads=[pt], writes=[Vt[ti]])
            P.emit("dve", lambda e, w=w, rc=rc: e.tensor_tensor(out=t1.ap[:, 0:w], in0=ps[1].ap[:, 0:w], in1=rc.ap[:, 0:w], op=ALU.mult), reads=[ps[1], rc], writes=[t1])
            P.emit("dve", lambda e, w=w, rsn=rsn: e.tensor_tensor(out=t2.ap[:, 0:w], in0=ps[2].ap[:, 0:w], in1=rsn.ap[:, 0:w], op=ALU.mult), reads=[ps[2], rsn], writes=[t2])
            P.emit("pool", lambda e, w=w, bj=bj: e.tensor_tensor(out=krT[bj].ap[:, 0:w], in0=t1.ap[:, 0:w], in1=t2.ap[:, 0:w], op=ALU.add), reads=[t1, t2], writes=[krT[bj]])
        P.barrier()
        P.release(m_k)

        phase_done('P2')
        ublk = [P.alloc([128, 8, 512], BF16, f"ublkq{i}") for i in range(2)]
        ropc = [P.alloc([128, 512], F32, f"ropcq{i}") for i in range(2)]
        rops = [P.alloc([128, 512], F32, f"ropsq{i}") for i in range(2)]
        sqb = P.alloc([128, 2, 512], BF16, "sqbq")
        rstd = P.alloc([128, 512], F32, "rstdq")
        cqn = P.alloc([128, 2, 512], BF16, "cqn")
        qnT = [P.alloc([128, 4, 512], BF16, f"qnT{i}") for i in range(2)]
        qrT = [P.alloc([128, 2, 512], BF16, f"qrT{i}") for i in range(2)]
        t1 = P.alloc([128, 512], F32, "t1q")
        t2 = P.alloc([128, 512], F32, "t2q")
        pT = [P.alloc([128, 512], BF16, f"pT{i}") for i in range(3)]
        rden = P.alloc([128, 512], F32, "rden")
        yT = [P.alloc([128, 512], BF16, f"yT{i}") for i in range(2)]
        pti = 0
        for bj, (c0, w) in enumerate(BLOCKS):
            ub = ublk[bj % 2]
            rc, rsn = ropc[bj % 2], rops[bj % 2]
            qn, qr = qnT[bj % 2], qrT[bj % 2]
            dma("sp", ub.ap[:, :, 0:w], UT_d[:, :, c0:c0 + w].rearrange("k p n -> p k n"), [UTt[bj]], [ub])
            dma("sp", rc.ap[:, 0:w], ropeC_d[:, c0:c0 + w], [], [rc])
            dma("sp", rsn.ap[:, 0:w], ropeS_d[:, c0:c0 + w], [], [rsn])
            inproj_fm(ub, 0, ps[0], w)
            inproj_fm(ub, 1, ps[1], w)
            for c in range(2):
                P.emit("act", lambda e, c=c, w=w: e.activation(out=sqb.ap[:, c, 0:w], in_=ps[c].ap[:, 0:w], func=AF.Square), reads=[ps[c]], writes=[sqb])
            for c in range(2):
                P.emit("pe", lambda e, c=c, w=w: e.matmul(ps[2].ap[:, 0:w], lhsT=onesB, rhs=sqb.ap[:, c, 0:w], start=(c == 0), stop=(c == 1)), reads=[cb, sqb], writes=[ps[2]])
            rsqrt_ops(P, rstd, ps[2].ap[:, 0:w], ps[2], rstd.ap[:, 0:w], 256 * EPS)
            for c in range(2):
                P.emit("dve", lambda e, c=c, w=w: e.scalar_tensor_tensor(out=cqn.ap[:, c, 0:w], in0=ps[c].ap[:, 0:w], scalar=sm[:, 25 + c:26 + c], in1=rstd.ap[:, 0:w], op0=ALU.mult, op1=ALU.mult),
                       reads=[ps[c], rstd, small], writes=[cqn])

            def qproj(ch, pt):
                for kc in range(2):
                    P.emit("pe", lambda e, kc=kc: e.matmul(pt.ap[:, 0:w], lhsT=wuq.ap[:, kc, ch * 128:(ch + 1) * 128], rhs=cqn.ap[:, kc, 0:w], start=(kc == 0), stop=(kc == 1)),
                           reads=[wuq, cqn], writes=[pt])
            for h in range(4):
                pt = ps[3 + h % 2]
                qproj(h, pt)
                P.emit("dve", lambda e, h=h, pt=pt, w=w, qn=qn: e.tensor_copy(out=qn.ap[:, h, 0:w], in_=pt.ap[:, 0:w]), reads=[pt], writes=[qn])
            for pr in range(2):
                qproj(4 + pr, ps[5])
                qproj(6 + pr, ps[6])
                P.emit("dve", lambda e, w=w, rc=rc: e.tensor_tensor(out=t1.ap[:, 0:w], in0=ps[5].ap[:, 0:w], in1=rc.ap[:, 0:w], op=ALU.mult), reads=[ps[5], rc], writes=[t1])
                P.emit("dve", lambda e, w=w, rsn=rsn: e.tensor_tensor(out=t2.ap[:, 0:w], in0=ps[6].ap[:, 0:w], in1=rsn.ap[:, 0:w], op=ALU.mult), reads=[ps[6], rsn], writes=[t2])
                P.emit("pool", lambda e, w=w, pr=pr, qr=qr: e.tensor_tensor(out=qr.ap[:, pr, 0:w], in0=t1.ap[:, 0:w], in1=t2.ap[:, 0:w], op=ALU.add), reads=[t1, t2], writes=[qr])
            ktiles = [32, 33] if bj == 8 else list(range(NTILE))
            for h in range(4):
                po, pd = ps[0], ps[1]
                hp = 64 * (h % 2)
                nkt = len(ktiles)
                pps = {}

                def emit_s(ki):
                    nonlocal pti
                    kt = ktiles[ki]
                    kb, ko = kt // 4, (kt % 4) * 128
                    pss = ps[2 + ki % 2]
                    pp = pT[pti % 3]
                    pti += 1
                    P.emit("pe", lambda e: e.matmul(pss.ap[:, 0:w], lhsT=knT[kb].ap[:, h, ko:ko + 128], rhs=qn.ap[:, h, 0:w], start=True, stop=False),
                           reads=[knT[kb], qn], writes=[pss])
                    P.emit("pe", lambda e: e.matmul(pss.ap[:, 0:w], lhsT=krT[kb].ap[hp:hp + 64, ko:ko + 128], rhs=qr.ap[hp:hp + 64, h // 2, 0:w], start=False, stop=True),
                           reads=[krT[kb], qr], writes=[pss])
                    P.emit("act", lambda e: e.activation(out=pp.ap[:, 0:w], in_=pss.ap[:, 0:w], func=AF.Exp, scale=SCALE), reads=[pss], writes=[pp])
                    pps[ki] = pp

                emit_s(0)
                for ki, kt in enumerate(ktiles):
                    if ki + 1 < nkt:
                        emit_s(ki + 1)
                    pp = pps.pop(ki)
                    first, last = ki == 0, ki == nkt - 1
                    P.emit("pe", lambda e, kt=kt, pp=pp, first=first, last=last: e.matmul(po.ap[:, 0:w], lhsT=Vt[kt].ap[:, h * 128:(h + 1) * 128], rhs=pp.ap[:, 0:w], start=first, stop=last),
                           reads=[Vt[kt], pp], writes=[po])
                    P.emit("pe", lambda e, pp=pp, first=first, last=last: e.matmul(pd.ap[:, 0:w], lhsT=onesB, rhs=pp.ap[:, 0:w], start=first, stop=last),
                           reads=[cb, pp], writes=[pd])
                yt = yT[h % 2]
                P.emit("dve", lambda e, w=w: e.reciprocal(out=rden.ap[:, 0:w], in_=pd.ap[:, 0:w]), reads=[pd], writes=[rden])
                P.emit("dve", lambda e, w=w, yt=yt: e.tensor_tensor(out=yt.ap[:, 0:w], in0=po.ap[:, 0:w], in1=rden.ap[:, 0:w], op=ALU.mult), reads=[po, rden], writes=[yt])
                dma("pool", MT_d[h, :, c0:c0 + w], yt.ap[:, 0:w], [yt], [MTt[h][bj]])
        P.barrier()
        P.release(m_mla)

        phase_done('P3')
        m = P.mark()
        wfm = P.alloc([128, 8, 4 * 128], BF16, "wfm_ml")
        load_cast(wfm, wfm.ap, winfm_d[l][:, 5 * 128:9 * 128].rearrange("(k p) n -> p k n", p=128), 8, 512)
        wtm = P.alloc([128, 8, 784], BF16, "wtm")
        load_cast(wtm, wtm.ap, wintm_d[l].rearrange("(k p) n -> p k n", p=128), 8, 784)
        mqT = [P.alloc([128, 4, 512], BF16, f"mqT{j}") for j in range(9)]
        mkT = [P.alloc([128, 2, 512], BF16, f"mkT{j}") for j in range(9)]
        mk = [P.alloc([128, 256], BF16, f"mk{i}") for i in range(NTILE)]
        mvA = [P.alloc([128, 4, 66], BF16, f"mvA{i}") for i in range(NTILE)]
        moS = [P.alloc([128, 256], BF16, f"moS{i}") for i in range(NTILE)]
        ebt = [P.alloc([128, 8], F32, f"eb{i}") for i in range(NTILE)]
        eit = [P.alloc([128, 8], F32, f"ei{i}") for i in range(NTILE)]
        eblt = [P.alloc([128, 8], F32, f"ebl{i}") for i in range(NTILE)]
        hf = [P.alloc([128, 256], BF16, f"hf{i}") for i in range(NTILE)]
        m2 = P.mark()
        ublk = [P.alloc([128, 8, 512], BF16, "ublkm0")] * 2
        mlw = [(P.alloc([128, 16], F32, f"gt{i}"), P.alloc([128, 8], F32, f"lf{i}"), P.alloc([128, 8], F32, f"dd{i}"), P.alloc([128, 256], F32, f"tnh{i}")) for i in range(2)]
        for bj, (c0, w) in enumerate(BLOCKS):
            ub = ublk[bj % 2]
            dma("sp", ub.ap[:, :, 0:w], UT_d[:, :, c0:c0 + w].rearrange("k p n -> p k n"), [UTt[bj]], [ub])
            for ch in range(4):
                pt = ps[ch % 2]
                for k in range(8):
                    P.emit("pe", lambda e, k=k, ch=ch, pt=pt, w=w, ub=ub: e.matmul(pt.ap[:, 0:w], lhsT=wfm.ap[:, k, ch * 128:(ch + 1) * 128], rhs=ub.ap[:, k, 0:w], start=(k == 0), stop=(k == 7)),
                           reads=[wfm, ub], writes=[pt])
                if ch < 2:
                    if ch == 0:
                        P.emit("pool", lambda e, bj=bj: e.memset(mqT[bj].ap, 0.0), writes=[mqT[bj]])
                    for hh in range(2):
                        P.emit("act", lambda e, ch=ch, hh=hh, pt=pt, w=w, bj=bj: e.activation(out=mqT[bj].ap[64 * hh:64 * hh + 64, 2 * ch + hh, 0:w], in_=pt.ap[64 * hh:64 * hh + 64, 0:w], func=AF.Identity, scale=0.125), reads=[pt], writes=[mqT[bj]])
                else:
                    P.emit("act", lambda e, ch=ch, pt=pt, w=w, bj=bj: e.activation(out=mkT[bj].ap[:, ch - 2, 0:w], in_=pt.ap[:, 0:w], func=AF.Identity), reads=[pt], writes=[mkT[bj]])
            for tt in range(w // 128):
                ti = c0 // 128 + tt
                gt, lf, dd, tnh = mlw[ti % 2]
                pa, pb = ps[2 + tt % 2], ps[4 + tt % 2]
                for k in range(8):
                    P.emit("pe", lambda e, k=k, tt=tt, pa=pa, ub=ub: e.matmul(pa.ap, lhsT=ub.ap[:, k, tt * 128:(tt + 1) * 128], rhs=wtm.ap[:, k, 0:512], start=(k == 0), stop=(k == 7)),
                           reads=[ub, wtm], writes=[pa])
                for k in range(8):
                    P.emit("pe", lambda e, k=k, tt=tt, pb=pb, ub=ub: e.matmul(pb.ap[:, 0:272], lhsT=ub.ap[:, k, tt * 128:(tt + 1) * 128], rhs=wtm.ap[:, k, 512:784], start=(k == 0), stop=(k == 7)),
                           reads=[ub, wtm], writes=[pb])
                P.emit("dve", lambda e, ti=ti, pa=pa: e.tensor_copy(out=mk[ti].ap, in_=pa.ap[:, 0:256]), reads=[pa], writes=[mk[ti]])
                P.emit("pool", lambda e, ti=ti: e.memset(mvA[ti].ap, 1.0), writes=[mvA[ti]])
                P.emit("dve", lambda e, ti=ti, pa=pa: e.tensor_copy(out=mvA[ti].ap[:, :, 0:64], in_=pa.ap[:, 256:512].rearrange("p (h d) -> p h d", h=4)), reads=[pa], writes=[mvA[ti]])
                sigmoid_ops(P, tnh, tnh.ap, pb.ap[:, 0:256], [pb], eng2="pool")
                P.emit("pool", lambda e, ti=ti: e.tensor_copy(out=moS[ti].ap, in_=tnh.ap), reads=[tnh], writes=[moS[ti]])
                P.emit("dve", lambda e, pb=pb: e.tensor_tensor(out=gt.ap, in0=pb.ap[:, 256:272], in1=gbias.ap[:, l, :], op=ALU.add), reads=[pb, gbias], writes=[gt])
                gv = gt.ap.rearrange("p (a b) -> p a b", a=4)
                lfv = lf.ap.rearrange("p (a b) -> p a b", a=2)
                P.emit("act", lambda e, gv=gv, lfv=lfv: e.activation(out=lfv, in_=gv[:, 1::2, :], func=AF.Exp, scale=-1.0), reads=[gt], writes=[lf])
                P.emit("act", lambda e: e.activation(out=lf.ap, in_=lf.ap, func=AF.Ln, bias=1.0), reads=[lf], writes=[lf])
                P.emit("dve", lambda e: e.tensor_scalar(out=lf.ap, in0=lf.ap, scalar1=-1.0, scalar2=None, op0=ALU.mult), reads=[lf], writes=[lf])
                pc = ps[6 + ti % 2]
                P.emit("pe", lambda e, pc=pc: e.matmul(pc.ap[:, 0:4], lhsT=maskU, rhs=lf.ap[:, 0:4], start=True, stop=True), reads=[consts, lf], writes=[pc])
                P.emit("pe", lambda e, pc=pc: e.matmul(pc.ap[:, 4:8], lhsT=maskL, rhs=lf.ap[:, 4:8], start=True, stop=True), reads=[consts, lf], writes=[pc])
                P.emit("pe", lambda e, pc=pc: e.matmul(pc.ap[:, 8:16], lhsT=onesF, rhs=lf.ap[:, 0:8], start=True, stop=True), reads=[consts, lf], writes=[pc])
                P.emit("act", lambda e, ti=ti, pc=pc: e.activation(out=ebt[ti].ap, in_=pc.ap[:, 0:8], func=AF.Exp), reads=[pc], writes=[ebt[ti]])
                P.emit("act", lambda e, ti=ti, pc=pc: e.activation(out=eblt[ti].ap, in_=pc.ap[:, 8:16], func=AF.Exp), reads=[pc], writes=[eblt[ti]])
                ddv = dd.ap.rearrange("p (a b) -> p a b", a=2)
                P.emit("dve", lambda e, pc=pc, gv=gv, ddv=ddv: e.tensor_tensor(out=ddv, in0=gv[:, 0::2, :], in1=pc.ap[:, 0:8].rearrange("p (a b) -> p a b", a=2), op=ALU.subtract),
                       reads=[gt, pc], writes=[dd])
                P.emit("act", lambda e, ti=ti: e.activation(out=eit[ti].ap, in_=dd.ap, func=AF.Exp), reads=[dd], writes=[eit[ti]])
        P.barrier()
        P.release(m2)
        phase_done('P4a')
        ptT = [[P.alloc([128, 4, 128], BF16, f"ptT{d}{i}") for i in range(2)] for d in range(2)]
        ktp = [[P.alloc([128, 4, 128], BF16, f"ktp{d}{i}") for i in range(2)] for d in range(2)]
        Cst = [P.alloc([128, 2, 65], F32, f"Cst{d}") for d in range(2)]
        Cbf = [P.alloc([128, 2, 66], BF16, f"Cbf{d}") for d in range(2)]
        ctmp = P.alloc([128, 2, 65], F32, "ctmp")
        den4 = P.alloc([128, 4], F32, "den4")
        r4 = P.alloc([128, 4], F32, "r4")
        hb = P.alloc([128, 256], F32, "hb")
        hs = P.alloc([128, 256], F32, "hs")
        hq = P.alloc([128, 256], F32, "hq")
        ss4 = P.alloc([128, 4], F32, "ss4")
        ybf = P.alloc([128, 256], BF16, "ybf")
        ybT = [P.alloc([128, 2, 512], BF16, f"ybT{i}") for i in range(2)]
        for d in range(2):
            for i in range(2):
                P.emit("pool", lambda e, d=d, i=i: e.memset(ktp[d][i].ap, 0.0), writes=[ktp[d][i]])
        fwd_order = [32, 33] + list(range(32))
        bwd_order = [33, 32] + list(range(31, -1, -1))
        ybcount = {}
        import os as _os
        _ns = int(_os.environ.get('DBG_STEPS', '99'))
        _nd = int(_os.environ.get('DBG_DIRS', '2'))
        _lvl = int(_os.environ.get('DBG_LVL', '99'))
        _skip = _os.environ.get('DBG_SKIP', '')
        for d, order in ((0, fwd_order[:_ns]), (1, bwd_order[:_ns]))[:_nd]:
            mask = maskU if d == 0 else maskL
            for step, ti in enumerate(order):
                bj, to = (8, (ti - 32) * 128) if ti >= 32 else (ti // 4, (ti % 4) * 128)
                if step == 0:
                    P.emit("pool", lambda e, d=d: e.memset(Cst[d].ap, 0.0), writes=[Cst[d]])
                    P.emit("pool", lambda e, d=d: e.memset(Cbf[d].ap, 0.0), writes=[Cbf[d]])
                pS, pO, pKV = ps[step % 2], ps[2 + step % 2], ps[4 + step % 2]
                pt_, kt_ = ptT[d][step % 2], ktp[d][step % 2]
                for h in range(4):
                    hp, mch = 64 * (h % 2), h // 2
                    if 'S' in _skip or ('E' in _skip and h % 2 == 1) or ('O' in _skip and h % 2 == 0):
                        continue
                    P.emit("pe", lambda e, h=h, hp=hp, mch=mch, bj=bj, to=to, pS=pS: e.matmul(pS.ap[:, h * 128:(h + 1) * 128], lhsT=mkT[bj].ap[:, mch, to:to + 128], rhs=mqT[bj].ap[:, h, to:to + 128], start=True, stop=True),
                           reads=[mkT[bj], mqT[bj]], writes=[pS])
                for h in range(4):
                    hp = 64 * (h % 2)
                    if 'P' not in _skip:
                      P.emit("dve", lambda e, h=h, ti=ti, d=d, pS=pS, pt_=pt_, mask=mask: e.scalar_tensor_tensor(out=pt_.ap[:, h, :], in0=pS.ap[:, h * 128:(h + 1) * 128], scalar=eit[ti].ap[:, d * 4 + h:d * 4 + h + 1], in1=mask, op0=ALU.mult, op1=ALU.mult),
                             reads=[pS, eit[ti], consts], writes=[pt_])
                    if "K" in _skip:
                        continue
                    P.emit("pool", lambda e, h=h, hp=hp, ti=ti, d=d, kt_=kt_: e.tensor_scalar(out=kt_.ap[:, h, hp:hp + 64], in0=mk[ti].ap[:, h * 64:(h + 1) * 64], scalar1=eit[ti].ap[:, d * 4 + h:d * 4 + h + 1], scalar2=None, op0=ALU.mult),
                           reads=[mk[ti], eit[ti]], writes=[kt_])
                if _lvl < 2:
                    continue
                for h in range(4):
                    hp, mch = 64 * (h % 2), h // 2
                    P.emit("pe", lambda e, h=h, ti=ti, pO=pO, pt_=pt_: e.matmul(pO.ap[:, h * 65:(h + 1) * 65], lhsT=pt_.ap[:, h, :], rhs=mvA[ti].ap[:, h, 0:65], start=True, stop=False),
                           reads=[pt_, mvA[ti]], writes=[pO])
                    P.emit("pe", lambda e, h=h, hp=hp, mch=mch, bj=bj, to=to, d=d, pO=pO: e.matmul(pO.ap[:, h * 65:(h + 1) * 65], lhsT=mqT[bj].ap[:, h, to:to + 128], rhs=Cbf[d].ap[:, mch, 0:65], start=False, stop=True),
                           reads=[mqT[bj], Cbf[d]], writes=[pO])
                for mch in range(2):
                    for hh in range(2):
                        h = mch * 2 + hh
                        P.emit("pe", lambda e, h=h, mch=mch, hh=hh, ti=ti, pKV=pKV, kt_=kt_: e.matmul(pKV.ap[:, mch * 65:(mch + 1) * 65], lhsT=kt_.ap[:, h, :], rhs=mvA[ti].ap[:, h, 0:65], start=(hh == 0), stop=(hh == 1)),
                               reads=[kt_, mvA[ti]], writes=[pKV])
                if _lvl < 3:
                    continue
                Ov = pO.ap[:, 0:260].rearrange("p (h c) -> p h c", h=4)
                ebv = ebt[ti].ap[:, d * 4:(d + 1) * 4]
                P.emit("dve", lambda e, Ov=Ov, ebv=ebv: e.tensor_tensor(out=den4.ap, in0=Ov[:, :, 64], in1=ebv, op=ALU.mult), reads=[pO, ebt[ti]], writes=[den4])
                P.emit("act", lambda e: e.activation(out=den4.ap, in_=den4.ap, func=AF.Abs), reads=[den4], writes=[den4])
                P.emit("dve", lambda e: e.tensor_scalar(out=den4.ap, in0=den4.ap, scalar1=1.0, scalar2=None, op0=ALU.max), reads=[den4], writes=[den4])
                P.emit("dve", lambda e: e.reciprocal(out=den4.ap, in_=den4.ap), reads=[den4], writes=[den4])
                P.emit("dve", lambda e, ebv=ebv: e.tensor_tensor(out=r4.ap, in0=ebv, in1=den4.ap, op=ALU.mult), reads=[ebt[ti], den4], writes=[r4])
                hdst = hf[ti] if d == 0 else hb
                P.emit("dve", lambda e, Ov=Ov, hdst=hdst: e.tensor_tensor(out=hdst.ap.rearrange("p (h c) -> p h c", h=4), in0=Ov[:, :, 0:64], in1=bc_last(r4.ap, 64), op=ALU.mult),
                       reads=[pO, r4], writes=[hdst])
                if _lvl < 4:
                    continue
                KVv = pKV.ap[:, 0:130].rearrange("p (m c) -> p m c", m=2)
                P.emit("dve", lambda e, KVv=KVv, d=d: e.tensor_tensor(out=ctmp.ap, in0=KVv, in1=Cst[d].ap, op=ALU.add), reads=[pKV, Cst[d]], writes=[ctmp])
                for mch in range(2):
                    for hh in range(2):
                        h = mch * 2 + hh
                        hp = 64 * hh
                        P.emit("dve", lambda e, h=h, hp=hp, mch=mch, ti=ti, d=d: e.tensor_scalar(out=Cst[d].ap[hp:hp + 64, mch, :], in0=ctmp.ap[hp:hp + 64, mch, :], scalar1=eblt[ti].ap[hp:hp + 64, d * 4 + h:d * 4 + h + 1], scalar2=None, op0=ALU.mult),
                               reads=[ctmp, eblt[ti]], writes=[Cst[d]])
                P.emit("act", lambda e, d=d: e.activation(out=Cbf[d].ap[:, :, 0:65], in_=Cst[d].ap, func=AF.Identity), reads=[Cst[d]], writes=[Cbf[d]])
                if _lvl < 5:
                    continue
                if d == 1:
                    P.emit("pool", lambda e, ti=ti: e.tensor_tensor(out=hs.ap, in0=hf[ti].ap, in1=hb.ap, op=ALU.add), reads=[hf[ti], hb], writes=[hs])
                    P.emit("act", lambda e: e.activation(out=hq.ap, in_=hs.ap, func=AF.Square), reads=[hs], writes=[hq])
                    P.emit("dve", lambda e: e.reduce_sum(out=ss4.ap, in_=hq.ap.rearrange("p (h c) -> p h c", h=4), axis=AX.X), reads=[hq], writes=[ss4])
                    rsqrt_ops(P, ss4, ss4.ap, ss4, ss4.ap, 64 * EPS)
                    P.emit("dve", lambda e: e.tensor_tensor(out=hq.ap.rearrange("p (h c) -> p h c", h=4), in0=hs.ap.rearrange("p (h c) -> p h c", h=4), in1=bc_last(ss4.ap, 64), op=ALU.mult),
                           reads=[hs, ss4], writes=[hq])
                    P.emit("dve", lambda e, ti=ti: e.scalar_tensor_tensor(out=ybf.ap, in0=hq.ap, scalar=8.0, in1=moS[ti].ap, op0=ALU.mult, op1=ALU.mult), reads=[hq, moS[ti]], writes=[ybf])
                    ptr = ps[6 + step % 2]
                    pv = psb(ptr)
                    for c in range(2):
                        P.emit("pe", lambda e, c=c, pv=pv, ptr=ptr: e.transpose(pv[:, c * 128:(c + 1) * 128], ybf.ap[:, c * 128:(c + 1) * 128], identB), reads=[ybf, cb], writes=[ptr])
                    yb = ybT[bj % 2]
                    P.emit("act", lambda e, pv=pv, yb=yb, to=to, ptr=ptr: e.activation(out=yb.ap[:, :, to:to + 128], in_=pv[:, 0:256].rearrange("p (c n) -> p c n", c=2), func=AF.Identity), reads=[ptr], writes=[yb])
                    ybcount[bj] = ybcount.get(bj, 0) + 1
                    if ybcount[bj] == BLOCKS[bj][1] // 128:
                        c0, w = BLOCKS[bj]
                        for c in range(2):
                            dma("pool", MT_d[4 + c, :, c0:c0 + w], yb.ap[:, c, 0:w], [yb], [MTt[4 + c][bj]])
        P.barrier()
        P.release(m)

        phase_done('P4')
        m = P.mark()
        wbd = P.alloc([128, 8, 128], BF16, "wbd")
        load_cast(wbd, wbd.ap, wbd_d[l], 8, 128)
        wfm = P.alloc([128, 8, 4 * 128], BF16, "wfm_lru")
        load_cast(wfm, wfm.ap, winfm_d[l][:, 9 * 128:13 * 128].rearrange("(k p) n -> p k n", p=128), 8, 512)
        ublk = [P.alloc([128, 8, 512], BF16, f"ublkl{i}") for i in range(2)]
        xb = P.alloc([128, NT], F32, "xb")
        gb = P.alloc([128, NT], BF16, "gb")
        xs = P.alloc([128, NT], F32, "xs")
        xsb = P.alloc([128, NT], BF16, "xsb")
        At = P.alloc([128, NT], F32, "At")
        Ut = P.alloc([128, NT], F32, "Ut")
        Hf = P.alloc([128, NT], F32, "Hf")
        Hb = P.alloc([128, NT], F32, "Hb")
        tq = [P.alloc([128, 512], F32, f"tq{i}") for i in range(4)]
        ycb = [P.alloc([128, 512], BF16, f"ycb{i}") for i in range(2)]
        segs = [(0, NLAT), (NLAT, NCTX)]
        for c in range(2):
            for bj, (c0, w) in enumerate(BLOCKS):
                ub = ublk[bj % 2]
                dma("sp", ub.ap[:, :, 0:w], UT_d[:, :, c0:c0 + w].rearrange("k p n -> p k n"), [UTt[bj]], [ub])
                for which, dst in ((0, xb), (1, gb)):
                    pt = ps[(2 * bj + which) % 4]
                    ch = which * 2 + c
                    for k in range(8):
                        P.emit("pe", lambda e, k=k, ch=ch, pt=pt, w=w, ub=ub: e.matmul(pt.ap[:, 0:w], lhsT=wfm.ap[:, k, ch * 128:(ch + 1) * 128], rhs=ub.ap[:, k, 0:w], start=(k == 0), stop=(k == 7)),
                               reads=[wfm, ub], writes=[pt])
                    P.emit("act", lambda e, pt=pt, w=w, c0=c0, dst=dst: e.activation(out=dst.ap[:, c0:c0 + w], in_=pt.ap[:, 0:w], func=AF.Identity), reads=[pt], writes=[dst])
            cw = lambda j: sm[:, 3 + c * 4 + j: 4 + c * 4 + j]
            for (s0, sl) in segs:
                P.emit("dve", lambda e, s0=s0, sl=sl: e.tensor_scalar(out=xs.ap[:, s0:s0 + sl], in0=xb.ap[:, s0:s0 + sl], scalar1=cw(2), scalar2=sm[:, 11 + c:12 + c], op0=ALU.mult, op1=ALU.add),
                       reads=[xb, small], writes=[xs])
                P.emit("dve", lambda e, s0=s0, sl=sl: e.scalar_tensor_tensor(out=xs.ap[:, s0 + 2:s0 + sl], in0=xb.ap[:, s0:s0 + sl - 2], scalar=cw(0), in1=xs.ap[:, s0 + 2:s0 + sl], op0=ALU.mult, op1=ALU.add),
                       reads=[xb, xs, small], writes=[xs])
                P.emit("dve", lambda e, s0=s0, sl=sl: e.scalar_tensor_tensor(out=xs.ap[:, s0 + 1:s0 + sl], in0=xb.ap[:, s0:s0 + sl - 1], scalar=cw(1), in1=xs.ap[:, s0 + 1:s0 + sl], op0=ALU.mult, op1=ALU.add),
                       reads=[xb, xs, small], writes=[xs])
                P.emit("dve", lambda e, s0=s0, sl=sl: e.scalar_tensor_tensor(out=xs.ap[:, s0:s0 + sl - 1], in0=xb.ap[:, s0 + 1:s0 + sl], scalar=cw(3), in1=xs.ap[:, s0:s0 + sl - 1], op0=ALU.mult, op1=ALU.add),
                       reads=[xb, xs, small], writes=[xs])
            P.emit("pool", lambda e: e.tensor_copy(out=xsb.ap, in_=xs.ap), reads=[xs], writes=[xsb])
            for d in range(2):
                Hd = Hf if d == 0 else Hb
                for bj, (c0, w) in enumerate(BLOCKS):
                    pa, px = ps[(2 * bj) % 4 + 4 * 0], ps[(2 * bj + 1) % 4]
                    ia, ix = (d * 2 + 0) * 2 + c, (d * 2 + 1) * 2 + c
                    P.emit("pe", lambda e, ia=ia, pa=pa, c0=c0, w=w: e.matmul(pa.ap[:, 0:w], lhsT=wbd.ap[:, ia, :], rhs=xsb.ap[:, c0:c0 + w], start=True, stop=True), reads=[wbd, xsb], writes=[pa])
                    P.emit("pe", lambda e, ix=ix, px=px, c0=c0, w=w: e.matmul(px.ap[:, 0:w], lhsT=wbd.ap[:, ix, :], rhs=xsb.ap[:, c0:c0 + w], start=True, stop=True), reads=[wbd, xsb], writes=[px])
                    ta, tx = tq[2 * (bj % 2)], tq[2 * (bj % 2) + 1]
                    dc = d * 2 + c
                    sigmoid_ops(P, ta, ta.ap[:, 0:w], pa.ap[:, 0:w], [pa, small], bias=sm[:, 28 + dc:29 + dc], eng2="act")
                    P.emit("act", lambda e, w=w, dc=dc, c0=c0: e.activation(out=At.ap[:, c0:c0 + w], in_=ta.ap[:, 0:w], func=AF.Exp, scale=sm[:, 36 + dc:37 + dc]), reads=[ta, small], writes=[At])
                    sigmoid_ops(P, tx, tx.ap[:, 0:w], px.ap[:, 0:w], [px, small], bias=sm[:, 32 + dc:33 + dc], eng2="act")
                    P.emit("pool", lambda e, w=w, c0=c0: e.tensor_tensor(out=tx.ap[:, 0:w], in0=tx.ap[:, 0:w], in1=xs.ap[:, c0:c0 + w], op=ALU.mult), reads=[tx, xs], writes=[tx])
                    P.emit("dve", lambda e, w=w, c0=c0: e.tensor_tensor(out=ta.ap[:, 0:w], in0=At.ap[:, c0:c0 + w], in1=At.ap[:, c0:c0 + w], op=ALU.mult), reads=[At], writes=[ta])
                    P.emit("dve", lambda e, w=w: e.tensor_scalar(out=ta.ap[:, 0:w], in0=ta.ap[:, 0:w], scalar1=-1.0, scalar2=1.0, op0=ALU.mult, op1=ALU.add), reads=[ta], writes=[ta])
                    P.emit("act", lambda e, w=w: e.activation(out=ta.ap[:, 0:w], in_=ta.ap[:, 0:w], func=AF.Ln), reads=[ta], writes=[ta])
                    P.emit("act", lambda e, w=w: e.activation(out=ta.ap[:, 0:w], in_=ta.ap[:, 0:w], func=AF.Exp, scale=0.5), reads=[ta], writes=[ta])
                    P.emit("dve", lambda e, w=w, c0=c0: e.tensor_tensor(out=Ut.ap[:, c0:c0 + w], in0=ta.ap[:, 0:w], in1=tx.ap[:, 0:w], op=ALU.mult), reads=[ta, tx], writes=[Ut])
                SC = 1024
                if d == 0:
                    pieces = [(NLAT, NCTX)] + [(i * SC, SC) for i in range(NLAT // SC)]
                else:
                    pieces = [(NLAT, NCTX)] + [(i * SC, SC) for i in range(NLAT // SC - 1, -1, -1)]
                prev_last = None
                for (p0, pl) in pieces:
                    def view(t, p0=p0, pl=pl):
                        a = t.ap[:, p0:p0 + pl]
                        if d == 0:
                            return a
                        return AP(a.tensor, a.offset + pl - 1, [list(a.ap[0]), [-1, pl]])
                    init = 0.0 if prev_last is None else Hd.ap[:, prev_last:prev_last + 1]
                    P.emit("dve", lambda e, view=view, init=init, Hd=Hd: e.tensor_tensor_scan(out=view(Hd), data0=view(At), data1=view(Ut), initial=init, op0=ALU.mult, op1=ALU.add),
                           reads=[At, Ut, Hd], writes=[Hd])
                    prev_last = (p0 + pl - 1) if d == 0 else p0
            for bj, (c0, w) in enumerate(BLOCKS):
                ta, tx = tq[2 * (bj % 2)], tq[2 * (bj % 2) + 1]
                yc = ycb[bj % 2]
                g = gb.ap[:, c0:c0 + w]
                P.emit("pool", lambda e, g=g, w=w: e.tensor_tensor(out=ta.ap[:, 0:w], in0=g, in1=g, op=ALU.mult), reads=[gb], writes=[ta])
                P.emit("dve", lambda e, w=w: e.tensor_scalar(out=ta.ap[:, 0:w], in0=ta.ap[:, 0:w], scalar1=0.044715 * 0.7978845608028654, scalar2=0.7978845608028654, op0=ALU.mult, op1=ALU.add), reads=[ta], writes=[ta])
                P.emit("dve", lambda e, g=g, w=w: e.tensor_tensor(out=ta.ap[:, 0:w], in0=ta.ap[:, 0:w], in1=g, op=ALU.mult), reads=[ta, gb], writes=[ta])
                sigmoid_ops(P, ta, ta.ap[:, 0:w], ta.ap[:, 0:w], [ta], scale=2.0, eng2="act")
                P.emit("dve", lambda e, g=g, w=w: e.tensor_tensor(out=ta.ap[:, 0:w], in0=ta.ap[:, 0:w], in1=g, op=ALU.mult), reads=[ta, gb], writes=[ta])
                P.emit("pool", lambda e, w=w, c0=c0: e.tensor_tensor(out=tx.ap[:, 0:w], in0=Hf.ap[:, c0:c0 + w], in1=Hb.ap[:, c0:c0 + w], op=ALU.add), reads=[Hf, Hb], writes=[tx])
                P.emit("dve", lambda e, w=w, yc=yc: e.tensor_tensor(out=yc.ap[:, 0:w], in0=ta.ap[:, 0:w], in1=tx.ap[:, 0:w], op=ALU.mult), reads=[ta, tx], writes=[yc])
                dma("pool", MT_d[6 + c, :, c0:c0 + w], yc.ap[:, 0:w], [yc], [MTt[6 + c][bj]])
        P.barrier()
        P.release(m)
        if debug and l == 0:
            mm = P.mark()
            dtile = [P.alloc([128, 8, 512], BF16, f"dbgm{i}") for i in range(2)]
            for bj, (c0, w) in enumerate(BLOCKS):
                dt_ = dtile[bj % 2]
                dma("sp", dt_.ap[:, :, 0:w], MT_d[:, :, c0:c0 + w].rearrange("k p n -> p k n"), [MTt[k][bj] for k in range(8)], [dt_])
                dma("pool", dbg["MT"][:, :, c0:c0 + w].rearrange("k p n -> p k n"), dt_.ap[:, :, 0:w], [dt_], [])
            P.barrier()
            P.release(mm)

        phase_done('P5')
        m = P.mark()
        wo = P.alloc([128, 8, D], BF16, "wo")
        load_cast(wo, wo.ap, wout_d[l].rearrange("(k p) n -> p k n", p=128), 8, D)
        mixb = [P.alloc([128, 8, 512], BF16, f"mixb{i}") for i in range(2)]
        xts = [P.alloc([128, D], F32, f"xto{i}") for i in range(2)]
        x1s = [P.alloc([128, D], F32, f"x1{i}") for i in range(2)]
        tmpo = P.alloc([128, D], F32, "tmpo")
        lnw = [(P.alloc([128, D], F32, f"sq2{i}"), P.alloc([128, 1], F32, f"ssum2{i}"), P.alloc([128, 1], F32, f"rs2{i}"), P.alloc([128, D], BF16, f"xn2{i}")) for i in range(2)]
        uTb = [P.alloc([128, 8, 512], BF16, f"uTb2{i}") for i in range(2)]
        for bj, (c0, w) in enumerate(BLOCKS):
            mb = mixb[bj % 2]
            ub = uTb[bj % 2]
            col = 1 if bj == 8 else 0
            dma("sp", mb.ap[:, :, 0:w], MT_d[:, :, c0:c0 + w].rearrange("k p n -> p k n"), [MTt[k][bj] for k in range(8)], [mb])
            for tt in range(w // 128):
                ti = c0 // 128 + tt
                xt, x1 = xts[ti % 2], x1s[ti % 2]
                dma("sp", xt.ap, xsrc(l)[ti * 128:(ti + 1) * 128, :], [Xt[ti]], [xt])
                for half in range(2):
                    pt = ps[2 + half]
                    for k in range(8):
                        P.emit("pe", lambda e, k=k, half=half, pt=pt, tt=tt, mb=mb: e.matmul(pt.ap, lhsT=mb.ap[:, k, tt * 128:(tt + 1) * 128], rhs=wo.ap[:, k, half * 512:(half + 1) * 512], start=(k == 0), stop=(k == 7)),
                               reads=[mb, wo], writes=[pt])
                    P.emit("dve", lambda e, half=half, pt=pt, col=col: e.tensor_tensor(out=tmpo.ap[:, half * 512:(half + 1) * 512], in0=pt.ap, in1=gbt.ap[:, col, half * 512:(half + 1) * 512], op=ALU.mult),
                           reads=[pt, gbt], writes=[tmpo])
                P.emit("pool", lambda e, xt=xt, x1=x1: e.tensor_tensor(out=x1.ap, in0=tmpo.ap, in1=xt.ap, op=ALU.add), reads=[tmpo, xt], writes=[x1])
                dma("pool", X_d[ti * 128:(ti + 1) * 128, :], x1.ap, [x1], [Xt[ti]])
                ln_tile(x1, ub, ub.ap, tt * 128, 24, 32, col, lnw[ti % 2] + (ps[ti % 2],))
            dma("pool", UT_d[:, :, c0:c0 + w].rearrange("k p n -> p k n"), ub.ap[:, :, 0:w], [ub], [UTt[bj]])
        P.barrier()
        P.release(m)

        phase_done('P6')
        last_layer = (l == L - 1)
        for hhalf in range(2):
            m = P.mark()
            w1 = P.alloc([128, 8, 2048], BF16, "w1")
            load_cast(w1, w1.ap, wff1_d[l][:, hhalf * 2048:(hhalf + 1) * 2048].rearrange("(k p) n -> p k n", p=128), 8, 2048)
            w2 = P.alloc([128, 16, D], BF16, "w2")
            load_cast(w2, w2.ap, wff2_d[l][hhalf * 2048:(hhalf + 1) * 2048, :].rearrange("(k p) n -> p k n", p=128), 16, D)
            ublk = [P.alloc([128, 8, 512], BF16, f"ublkf{i}") for i in range(2)]
            hT = [P.alloc([128, 16, 512], BF16, f"hT{i}") for i in range(2)]
            xts = [P.alloc([128, D], F32, f"xtf{i}") for i in range(2)]
            x2s = [P.alloc([128, D], F32, f"x2{i}") for i in range(2)]
            rtmp = [P.alloc([128, 512], F32, f"rtmp{i}") for i in range(2)]
            fg = None
            if last_layer and hhalf == 1:
                fg = P.alloc([128, D], F32, "fg")
                dma("sp", fg.ap, fing_d, [], [fg])
                sq = P.alloc([128, D], F32, "sq3")
                ssum = P.alloc([128, 1], F32, "ssum3")
                rs = P.alloc([128, 1], F32, "rs3")
                xo = [P.alloc([128, D], F32, f"xo{i}") for i in range(2)]
            for bj, (c0, w) in enumerate(BLOCKS):
                if last_layer and bj == 8:
                    continue
                ub = ublk[bj % 2]
                ht = hT[bj % 2]
                col = 1 if bj == 8 else 0
                dma("sp", ub.ap[:, :, 0:w], UT_d[:, :, c0:c0 + w].rearrange("k p n -> p k n"), [UTt[bj]], [ub])
                for j in range(16):
                    pt = ps[j % 2]
                    for k in range(8):
                        P.emit("pe", lambda e, k=k, j=j, pt=pt, w=w, ub=ub: e.matmul(pt.ap[:, 0:w], lhsT=w1.ap[:, k, j * 128:(j + 1) * 128], rhs=ub.ap[:, k, 0:w], start=(k == 0), stop=(k == 7)),
                               reads=[w1, ub], writes=[pt])
                    rt = rtmp[j % 2]
                    P.emit("act", lambda e, pt=pt, w=w, rt=rt: e.activation(out=rt.ap[:, 0:w], in_=pt.ap[:, 0:w], func=AF.Relu), reads=[pt], writes=[rt])
                    P.emit("dve" if j % 2 == 0 else "pool", lambda e, j=j, w=w, ht=ht, rt=rt: e.tensor_tensor(out=ht.ap[:, j, 0:w], in0=rt.ap[:, 0:w], in1=rt.ap[:, 0:w], op=ALU.mult), reads=[rt], writes=[ht])
                for tt in range(w // 128):
                    ti = c0 // 128 + tt
                    xt, x2 = xts[ti % 2], x2s[ti % 2]
                    dma("sp", xt.ap, X_d[ti * 128:(ti + 1) * 128, :], [Xt[ti]], [xt])
                    for half in range(2):
                        pt = ps[2 + half + 2 * (ti % 2)]
                        for k in range(16):
                            P.emit("pe", lambda e, k=k, half=half, pt=pt, tt=tt, ht=ht: e.matmul(pt.ap, lhsT=ht.ap[:, k, tt * 128:(tt + 1) * 128], rhs=w2.ap[:, k, half * 512:(half + 1) * 512], start=(k == 0), stop=(k == 15)),
                                   reads=[ht, w2], writes=[pt])
                        P.emit("dve", lambda e, half=half, pt=pt, col=col, x2=x2: e.tensor_tensor(out=x2.ap[:, half * 512:(half + 1) * 512], in0=pt.ap, in1=gbt.ap[:, 2 + col, half * 512:(half + 1) * 512], op=ALU.mult),
                               reads=[pt, gbt], writes=[x2])
                    P.emit("pool", lambda e, xt=xt, x2=x2: e.tensor_tensor(out=x2.ap, in0=x2.ap, in1=xt.ap, op=ALU.add), reads=[x2, xt], writes=[x2])
                    if fg is None:
                        dma("pool", X_d[ti * 128:(ti + 1) * 128, :], x2.ap, [x2], [Xt[ti]])
                    else:
                        o = xo[ti % 2]
                        P.emit("act", lambda e, x2=x2: e.activation(out=sq.ap, in_=x2.ap, func=AF.Square), reads=[x2], writes=[sq])
                        P.emit("dve", lambda e: e.reduce_sum(out=ssum.ap, in_=sq.ap, axis=AX.X), reads=[sq], writes=[ssum])
                        rsqrt_ops(P, rs, ssum.ap, ssum, rs.ap, D * EPS)
                        P.emit("dve", lambda e, x2=x2, o=o: e.tensor_scalar(out=o.ap, in0=x2.ap, scalar1=rs.ap[:, 0:1], scalar2=32.0, op0=ALU.mult, op1=ALU.mult), reads=[x2, rs], writes=[o])
                        P.emit("pool", lambda e, o=o: e.tensor_tensor(out=o.ap, in0=o.ap, in1=fg.ap, op=ALU.mult), reads=[o, fg], writes=[o])
                        dma("pool", out_d[ti * 128:(ti + 1) * 128, :], o.ap, [o], [OUTt[ti]])
            P.barrier()
            P.release(m)
        if debug and l == 0:
            mm = P.mark()
            dx = [P.alloc([128, D], F32, f"dbgx{i}") for i in range(2)]
            for ti in range(NTILE):
                dma("sp", dx[ti % 2].ap, X_d[ti * 128:(ti + 1) * 128, :], [Xt[ti]], [dx[ti % 2]])
                dma("pool", dbg["X"][ti * 128:(ti + 1) * 128, :], dx[ti % 2].ap, [dx[ti % 2]], [])
            P.barrier()
            P.release(mm)

      except _Stop as ex:
        print('build stopped after phase', ex)
        if debug:
            P.barrier()
            P.sb_off = 16512
            dtile = [P.alloc([128, 8, 512], BF16, f"dbgs{i}") for i in range(2)]
            for bj, (c0, w) in enumerate(BLOCKS):
                dt_ = dtile[bj % 2]
                dma("sp", dt_.ap[:, :, 0:w], MT_d[:, :, c0:c0 + w].rearrange("k p n -> p k n"), [MTt[k][bj] for k in range(8)], [dt_])
                dma("pool", dbg["MT"][:, :, c0:c0 + w].rearrange("k p n -> p k n"), dt_.ap[:, :, 0:w], [dt_], [])
        break

    P.barrier()
    P.finalize()
    return nc


def _rope_tables():
    t = np.arange(NLAT)
    row = (t // 64).astype(np.float32)
    colp = (t % 64).astype(np.float32)
    half = 32
    freqs = (1.0 / (10000.0 ** (np.arange(0, half, 2, dtype=np.float32) / half))).astype(np.float32)
    ang = np.concatenate([row[:, None] * freqs, colp[:, None] * freqs], axis=-1)
    cos, sin = np.cos(ang).astype(np.float32), np.sin(ang).astype(np.float32)
    C = np.ones((128, NT), np.float32)
    S = np.zeros((128, NT), np.float32)
    for r in range(128):
        rr = r % 64
        j = rr % 32
        C[r, :NLAT] = cos[:, j]
        S[r, :NLAT] = (-sin[:, j]) if rr < 32 else sin[:, j]
    return C, S


def _prep_shared(inp, L):
    f = lambda a: np.ascontiguousarray(np.asarray(a, dtype=np.float32))
    w_in = f(inp["w_in"])[:L]
    sw64 = np.concatenate([np.arange(32, 64), np.arange(0, 32)])
    kr = 384 + np.arange(64)
    fm_cols = np.concatenate([
        np.arange(0, 256), np.arange(256, 384), kr, kr, kr[sw64], kr[sw64],
        448 + np.arange(256), 704 + np.arange(256), 1488 + np.arange(256), 1744 + np.arange(256)])
    tm_cols = np.concatenate([704 + np.arange(256), 960 + np.arange(256), 1216 + np.arange(256), 1472 + np.arange(16)])
    w_uq = f(inp["mla_w_uq"])[:L]
    uq_cols = []
    for h in range(4):
        uq_cols.append(h * 192 + np.arange(128))
    rope = lambda h: h * 192 + 128 + np.arange(64)
    uq_cols += [rope(0), rope(1), rope(2), rope(3), rope(0)[sw64], rope(1)[sw64], rope(2)[sw64], rope(3)[sw64]]
    uq_cols = np.concatenate(uq_cols)
    w_ukv = f(inp["mla_w_ukv"])[:L]
    kcols = np.concatenate([h * 256 + np.arange(128) for h in range(4)])
    vcols = np.concatenate([h * 256 + 128 + np.arange(128) for h in range(4)])
    wa, wx = f(inp["lru_w_a"])[:L], f(inp["lru_w_x"])[:L]
    wbd = np.zeros((L, 128, 8, 128), np.float32)
    for d in range(2):
        for ax, wsrc in enumerate((wa, wx)):
            for c in range(2):
                idx = (d * 2 + ax) * 2 + c
                for g in range(2):
                    wbd[:, g * 64:(g + 1) * 64, idx, g * 64:(g + 1) * 64] = wsrc[:, d, 2 * c + g]
    small = np.zeros((128, L, 40), np.float32)
    gq, gkv = f(inp["mla_g_q"])[:L], f(inp["mla_g_kv"])[:L]
    cwv, cbv = f(inp["lru_conv_w"])[:L], f(inp["lru_conv_b"])[:L]
    ba, bx, lam = f(inp["lru_b_a"])[:L], f(inp["lru_b_x"])[:L], f(inp["lru_lam"])[:L]
    for l in range(L):
        small[:, l, 0:2] = gq[l].reshape(2, 128).T
        small[:, l, 2] = gkv[l]
        for c in range(2):
            for j in range(4):
                small[:, l, 3 + c * 4 + j] = cwv[l, j, c * 128:(c + 1) * 128]
            small[:, l, 11 + c] = cbv[l, c * 128:(c + 1) * 128]
            for d in range(2):
                small[:, l, 13 + d * 2 + c] = ba[l, d, c * 128:(c + 1) * 128]
                small[:, l, 17 + d * 2 + c] = bx[l, d, c * 128:(c + 1) * 128]
                small[:, l, 21 + d * 2 + c] = lam[l, d, c * 128:(c + 1) * 128]
    gbias = np.ascontiguousarray(np.broadcast_to(f(inp["ml_gate_bias"])[:L][None], (128, L, 16)))
    consts = np.zeros((128, 4, 128), np.float32)
    consts[:, 0, :] = np.eye(128, dtype=np.float32)
    consts[:, 1, :] = np.triu(np.ones((128, 128), np.float32))
    consts[:, 2, :] = np.tril(np.ones((128, 128), np.float32))
    consts[:, 3, :] = 1.0
    C, S = _rope_tables()
    return {
        "w_mod": f(inp["w_mod"])[:L], "b_mod": f(inp["b_mod"])[:L],
        "w_in_fm": np.ascontiguousarray(w_in[:, :, fm_cols]), "w_in_tm": np.ascontiguousarray(w_in[:, :, tm_cols]),
        "w_uq": np.ascontiguousarray(w_uq[:, :, uq_cols]),
        "w_k": np.ascontiguousarray(w_ukv[:, :, kcols]), "w_v": np.ascontiguousarray(w_ukv[:, :, vcols]),
        "w_out": f(inp["w_out"])[:L], "w_ff1": f(inp["w_ff1"])[:L], "w_ff2": f(inp["w_ff2"])[:L],
        "w_bd": wbd, "small": small, "gbias": gbias,
        "final_g": np.ascontiguousarray(np.broadcast_to(f(inp["final_g"])[None], (128, D))),
        "ropeC": C, "ropeS": S, "consts": consts,
    }


_NC_CACHE = {}


def run(inputs, depth=4, debug=False, n_cores=8):
    shared = _prep_shared(inputs, depth)
    x, c, ctx, c_ctx = (np.asarray(inputs[k], dtype=np.float32) for k in ("x", "c", "ctx", "c_ctx"))
    in_maps = []
    for core in range(n_cores):
        b = core % 4
        mm = dict(shared)
        mm["xin"] = np.ascontiguousarray(np.concatenate([x[b], ctx[b]], axis=0))
        cc = np.stack([c[b].reshape(8, 128).T, c_ctx.reshape(8, 128).T], axis=-1)
        mm["cc"] = np.ascontiguousarray(cc.astype(np.float32))
        in_maps.append(mm)
    key = (depth, debug)
    if key not in _NC_CACHE:
        _NC_CACHE[key] = build(depth, debug)
    res = run_bass_kernel_spmd(_NC_CACHE[key], in_maps, core_ids=list(range(n_cores)))
    return res


def kernel(**inputs):
    res = run(inputs)
    out = np.stack([np.asarray(res.results[b]["out"], dtype=np.float32) for b in range(4)], axis=0)
    return out
```

```python
import numpy as np
from contextlib import ExitStack
import concourse.bass as bass
import concourse.mybir as mybir
from concourse.bass_utils import run_bass_kernel_spmd
from concourse.ap import AP

F32 = mybir.dt.float32
BF16 = mybir.dt.bfloat16
AF = mybir.ActivationFunctionType
ALU = mybir.AluOpType
AX = mybir.AxisListType

D = 1024
NLAT = 4096
NCTX = 256
NT = NLAT + NCTX
NTILE = NT // 128
EPS = 1e-6
DFF = 4096
SCALE = (128 + 64) ** -0.5
ENG = ("pe", "act", "dve", "pool", "sp")
NSLOT = 24
BLOCKS = [(j * 512, 512) for j in range(8)] + [(4096, 256)]


class _Rec:
    def __getattr__(self, name):
        return lambda *a, **k: (name, a, k)


_REC = _Rec()


class T:
    __slots__ = ("ap", "w", "r")

    def __init__(self, ap):
        self.ap = ap
        self.w = []
        self.r = {}


class Prog:
    def __init__(self, nc):
        self.nc = nc
        self.ops = {e: [] for e in ENG}
        self.cnt = {}
        self.seen = {e: {} for e in ENG}
        self.layer = 0
        self.slot_cum = [0] * NSLOT
        self.rr = 0
        self.sb_off = 16512
        self.nalloc = 0

    def alloc(self, shape, dtype, name=None):
        nbytes = int(np.prod(shape[1:])) * (2 if dtype == BF16 else 4)
        off = (self.sb_off + 63) // 64 * 64
        self.sb_off = off + nbytes
        assert self.sb_off <= 229344, ("sbuf overflow", self.sb_off, name)
        self.nalloc += 1
        h = self.nc.alloc_sbuf_tensor_at(f"sb{self.nalloc}_{name or ''}", list(shape), dtype, offset=off)
        return T(h[:] if hasattr(h, "__getitem__") else h.ap())

    def mark(self):
        return self.sb_off

    def release(self, m):
        self.sb_off = m

    def emit(self, eng, fn, reads=(), writes=(), dma=False, multi=False):
        raw, oth = {}, {}

        def add(d, tok):
            k, v = tok
            if d.get(k, 0) < v:
                d[k] = v

        for t in reads:
            for tok in t.w:
                add(raw, tok)
        for t in writes:
            if not multi:
                for tok in t.w:
                    add(oth, tok)
            for k, v in t.r.items():
                add(oth, (k, v))
        if dma:
            slot = self.rr % NSLOT
            self.rr += 1
            prev = self.slot_cum[slot]
            if prev:
                add(oth, (("dma", slot), prev))
            self.slot_cum[slot] = prev + 16
            mytok = (("dma", slot), prev + 16)
        else:
            key = (eng, self.layer)
            self.cnt[key] = self.cnt.get(key, 0) + 1
            mytok = (key, self.cnt[key])
        waits = []
        seen = self.seen[eng]
        for d, is_raw in ((raw, True), (oth, False)):
            for k, v in d.items():
                if (not is_raw) and (not dma) and k[0] == eng:
                    continue
                if seen.get(k, 0) >= v:
                    continue
                seen[k] = v
                waits.append((k, v))
        self.ops[eng].append((waits, fn(_REC), mytok, dma))
        for t in reads:
            k, v = mytok
            if t.r.get(k, 0) < v:
                t.r[k] = v
        for t in writes:
            if multi:
                t.w = [x for x in t.w if x[0] != mytok[0]] + [mytok]
            else:
                t.w = [mytok]
                t.r = {}
        return mytok

    def barrier(self):
        toks = dict(self.cnt)
        for s in range(NSLOT):
            if self.slot_cum[s]:
                toks[("dma", s)] = self.slot_cum[s]
        for e in ENG:
            waits = []
            seen = self.seen[e]
            for k, v in toks.items():
                if k[0] == e:
                    continue
                if seen.get(k, 0) >= v:
                    continue
                seen[k] = v
                waits.append((k, v))
            if waits:
                self.ops[e].append((waits, None, None, False))

    def finalize(self):
        nc = self.nc
        keys = set(self.cnt.keys())
        for s in range(NSLOT):
            keys.add(("dma", s))
        with ExitStack() as st:
            sems = {}
            for k in sorted(keys, key=str):
                sems[k] = st.enter_context(nc.semaphore(f"s_{k[0]}_{k[1]}"))
            block = st.enter_context(nc.Block())

            def replay(name):
                def run(e):
                    for waits, fn, tok, dma in self.ops[name]:
                        for k, v in waits:
                            e.wait_ge(sems[k], v)
                        if fn is not None:
                            ins = getattr(e, fn[0])(*fn[1], **fn[2])
                            ins.then_inc(sems[tok[0]], 16 if dma else 1)
                return run

            block.tensor(replay("pe"))
            block.scalar(replay("act"))
            block.vector(replay("dve"))
            block.gpsimd(replay("pool"))
            block.sync(replay("sp"))


def bc_last(ap, n):
    return AP(ap.tensor, ap.offset, [list(x) for x in ap.ap] + [[0, n]])


class _Stop(Exception):
    pass


def build(depth=4, debug=False, stop=None):
    nc = bass.Bass("TRN2", target_bir_lowering=False)
    P = Prog(nc)
    L = depth

    def din(name, shape, dt=F32):
        return nc.dram_tensor(name, list(shape), dt, kind="ExternalInput").ap()

    xin = din("xin", [NT, D])
    cc_d = din("cc", [128, 8, 2])
    wmod_d = din("w_mod", [L, D, 6 * D])
    bmod_d = din("b_mod", [L, 6 * D])
    winfm_d = din("w_in_fm", [L, D, 13 * 128])
    wintm_d = din("w_in_tm", [L, D, 784])
    wuq_d = din("w_uq", [L, 256, 1024])
    wk_d = din("w_k", [L, 128, 512])
    wv_d = din("w_v", [L, 128, 512])
    wout_d = din("w_out", [L, D, D])
    wff1_d = din("w_ff1", [L, D, DFF])
    wff2_d = din("w_ff2", [L, DFF, D])
    wbd_d = din("w_bd", [L, 128, 8, 128])
    small_d = din("small", [128, L, 40])
    gbias_d = din("gbias", [128, L, 16])
    fing_d = din("final_g", [128, D])
    ropeC_d = din("ropeC", [128, NT])
    ropeS_d = din("ropeS", [128, NT])
    const_d = din("consts", [128, 4, 128])
    out_d = nc.dram_tensor("out", [NLAT, D], F32, kind="ExternalOutput").ap()
    X_d = nc.dram_tensor("Xs", [NT, D], F32, kind="Internal").ap()
    UT_d = nc.dram_tensor("UTs", [8, 128, NT], BF16, kind="Internal").ap()
    MT_d = nc.dram_tensor("MTs", [8, 128, NT], BF16, kind="Internal").ap()
    dbg = {}
    if debug:
        dbg["UT"] = nc.dram_tensor("dbgUT", [8, 128, NT], BF16, kind="ExternalOutput").ap()
        dbg["MT"] = nc.dram_tensor("dbgMT", [8, 128, NT], BF16, kind="ExternalOutput").ap()
        dbg["X"] = nc.dram_tensor("dbgX", [NT, D], F32, kind="ExternalOutput").ap()

    Xt = [T(None) for _ in range(NTILE)]
    UTt = [T(None) for _ in range(9)]
    MTt = [[T(None) for _ in range(9)] for _ in range(8)]
    OUTt = [T(None) for _ in range(32)]

    ps = []
    for i in range(8):
        h = nc.alloc_psum_tensor(f"ps{i}", [128, 512], F32)
        ps.append(T(h[:]))

    def psb(t):
        return t.ap.bitcast(BF16)

    consts = P.alloc([128, 4, 128], F32, "consts")
    identF = consts.ap[:, 0, :]
    maskU = consts.ap[:, 1, :]
    maskL = consts.ap[:, 2, :]
    onesF = consts.ap[:, 3, :]
    cb = P.alloc([128, 2, 128], BF16, "constb")
    identB = cb.ap[:, 0, :]
    onesB = cb.ap[:, 1, :]
    small = P.alloc([128, L, 40], F32, "small")
    gbias = P.alloc([128, L, 16], F32, "gbias")
    siluT = P.alloc([128, 8, 33], F32, "siluT")
    modT = P.alloc([128, 48, 2], F32, "modT")
    gbt = P.alloc([128, 4, D], F32, "gbt")
    stage = [P.alloc([128, 2048], F32, f"stage{i}") for i in range(2)]
    stage_rr = [0]


    def rsqrt_ops(P_, dst, src_ap, src_T, shape_ap, epsv):
        P_.emit("dve", lambda e: e.tensor_scalar(out=shape_ap, in0=src_ap, scalar1=epsv, scalar2=None, op0=ALU.add), reads=[src_T], writes=[dst])
        P_.emit("act", lambda e: e.activation(out=shape_ap, in_=shape_ap, func=AF.Ln), reads=[dst], writes=[dst])
        P_.emit("act", lambda e: e.activation(out=shape_ap, in_=shape_ap, func=AF.Exp, scale=-0.5), reads=[dst], writes=[dst])

    def sigmoid_ops(P_, dst, dst_ap, src_ap, src_Ts, scale=1.0, bias=None, eng2="dve"):
        if bias is None:
            P_.emit("act", lambda e: e.activation(out=dst_ap, in_=src_ap, func=AF.Exp, scale=-scale), reads=list(src_Ts), writes=[dst])
        else:
            P_.emit("act", lambda e: e.activation(out=dst_ap, in_=src_ap, func=AF.Exp, scale=-scale, bias=bias), reads=list(src_Ts), writes=[dst])
        if eng2 == "act":
            P_.emit("act", lambda e: e.activation(out=dst_ap, in_=dst_ap, func=AF.Identity, bias=1.0), reads=[dst], writes=[dst])
        else:
            P_.emit(eng2, lambda e: e.tensor_scalar(out=dst_ap, in0=dst_ap, scalar1=1.0, scalar2=None, op0=ALU.add), reads=[dst], writes=[dst])
        P_.emit("dve", lambda e: e.reciprocal(out=dst_ap, in_=dst_ap), reads=[dst], writes=[dst])

    def dma(eng, out_ap, in_ap, reads, writes):
        P.emit(eng, lambda e: e.dma_start(out=out_ap, in_=in_ap), reads=reads, writes=writes, dma=True)

    def load_cast(dst_T, dst_ap3, src_ap3, nk, ncols):
        per = max(1, 2048 // ncols)
        k = 0
        while k < nk:
            kk = min(per, nk - k)
            st = stage[stage_rr[0] % 2]
            stage_rr[0] += 1
            sv = st.ap[:, 0:kk * ncols].rearrange("p (k n) -> p k n", k=kk)
            dma("sp", sv, src_ap3[:, k:k + kk, :], [], [st])
            d = dst_ap3[:, k:k + kk, :]
            if stage_rr[0] % 2 == 0:
                P.emit("pool", lambda e, d=d, sv=sv: e.tensor_copy(out=d, in_=sv), reads=[st], writes=[dst_T], multi=True)
            else:
                P.emit("act", lambda e, d=d, sv=sv: e.activation(out=d, in_=sv, func=AF.Identity), reads=[st], writes=[dst_T], multi=True)
            k += kk

    dma("sp", consts.ap, const_d, [], [consts])
    dma("sp", small.ap, small_d, [], [small])
    dma("sp", gbias.ap, gbias_d, [], [gbias])
    P.emit("dve", lambda e: e.tensor_copy(out=identB, in_=identF), reads=[consts], writes=[cb])
    P.emit("dve", lambda e: e.tensor_copy(out=onesB, in_=onesF), reads=[consts], writes=[cb])
    m0 = P.mark()
    cct = P.alloc([128, 8, 2], F32, "cct")
    cth = P.alloc([128, 8, 2], F32, "cth")
    dma("sp", cct.ap, cc_d, [], [cct])
    P.emit("pool", lambda e: e.memset(siluT.ap, 0.0), writes=[siluT])
    sigmoid_ops(P, cth, cth.ap, cct.ap, [cct])
    for j, col in ((0, 0), (1, 32)):
        P.emit("dve", lambda e, j=j, col=col: e.tensor_tensor(out=siluT.ap[:, :, col], in0=cth.ap[:, :, j], in1=cct.ap[:, :, j], op=ALU.mult),
               reads=[cth, cct, siluT], writes=[siluT])
    for l in range(L):
        s = small.ap[:, l, :]
        P.emit("dve", lambda e, s=s: e.tensor_scalar(out=s[:, 25:27], in0=s[:, 0:2], scalar1=16.0, scalar2=None, op0=ALU.mult), reads=[small], writes=[small])
        P.emit("dve", lambda e, s=s: e.tensor_scalar(out=s[:, 27:28], in0=s[:, 2:3], scalar1=float(np.sqrt(128.0)), scalar2=None, op0=ALU.mult), reads=[small], writes=[small])
        P.emit("dve", lambda e, s=s: e.tensor_scalar(out=s[:, 28:36], in0=s[:, 13:21], scalar1=-1.0, scalar2=None, op0=ALU.mult), reads=[small], writes=[small])
        P.emit("act", lambda e, s=s: e.activation(out=s[:, 36:40], in_=s[:, 21:25], func=AF.Exp, scale=-1.0), reads=[small], writes=[small])
        P.emit("act", lambda e, s=s: e.activation(out=s[:, 36:40], in_=s[:, 36:40], func=AF.Ln, bias=1.0), reads=[small], writes=[small])
        P.emit("dve", lambda e, s=s: e.tensor_scalar(out=s[:, 36:40], in0=s[:, 36:40], scalar1=-8.0, scalar2=None, op0=ALU.mult), reads=[small], writes=[small])
    P.barrier()
    P.release(m0)
    base_mark = P.mark()

    def xsrc(l):
        return xin if l == 0 else X_d

    def ln_tile(xt, uT_T, uT_ap, c0, shc, scc, col, work):
        sq, ssum, rs, xn, pst = work
        P.emit("act", lambda e: e.activation(out=sq.ap, in_=xt.ap, func=AF.Square), reads=[xt], writes=[sq])
        P.emit("dve", lambda e: e.reduce_sum(out=ssum.ap, in_=sq.ap, axis=AX.X), reads=[sq], writes=[ssum])
        rsqrt_ops(P, rs, ssum.ap, ssum, rs.ap, D * EPS)
        P.emit("dve", lambda e: e.tensor_scalar(out=xn.ap, in0=xt.ap, scalar1=rs.ap[:, 0:1], scalar2=32.0, op0=ALU.mult, op1=ALU.mult),
               reads=[xt, rs], writes=[xn])
        pv = psb(pst)
        for k in range(8):
            P.emit("pe", lambda e, k=k: e.transpose(pv[:, k * 128:(k + 1) * 128], xn.ap[:, k * 128:(k + 1) * 128], identB),
                   reads=[xn, cb], writes=[pst])
        for k in range(8):
            P.emit("act", lambda e, k=k: e.activation(out=uT_ap[:, k, c0:c0 + 128], in_=pv[:, k * 128:(k + 1) * 128], func=AF.Identity,
                                                      scale=modT.ap[:, scc + k, col:col + 1], bias=modT.ap[:, shc + k, col:col + 1]),
                   reads=[pst, modT], writes=[uT_T])

    phase_ctr = [0]

    def phase_done(name):
        phase_ctr[0] += 1
        if stop is not None and phase_ctr[0] >= stop:
            raise _Stop(name)

    for l in range(L):
      try:
        P.layer = l
        sm = small.ap[:, l, :]
        P.release(base_mark)
        m = P.mark()
        modrow = P.alloc([33, 6 * D], F32, "modrow")
        bmrow = P.alloc([33, 6 * D], F32, "bmrow")
        wm = [P.alloc([128, 8, 512], F32, f"wm{i}") for i in range(2)]
        P.emit("pool", lambda e: e.memset(bmrow.ap, 0.0), writes=[bmrow])
        dma("sp", bmrow.ap[0:1, :], bmod_d[l:l + 1, :], [], [bmrow])
        dma("sp", bmrow.ap[32:33, :], bmod_d[l:l + 1, :], [], [bmrow])
        for nb in range(12):
            w = wm[nb % 2]
            dma("sp", w.ap, wmod_d[l, :, nb * 512:(nb + 1) * 512].rearrange("(k p) n -> p k n", p=128), [], [w])
            pt = ps[nb % 2]
            for k in range(8):
                P.emit("pe", lambda e, k=k, w=w, pt=pt: e.matmul(pt.ap[0:33, :], lhsT=siluT.ap[:, k, :], rhs=w.ap[:, k, :], start=(k == 0), stop=(k == 7)),
                       reads=[siluT, w], writes=[pt])
            P.emit("dve", lambda e, nb=nb, pt=pt: e.tensor_tensor(out=modrow.ap[:, nb * 512:(nb + 1) * 512], in0=pt.ap[0:33, :],
                                                                 in1=bmrow.ap[:, nb * 512:(nb + 1) * 512], op=ALU.add),
                   reads=[pt, bmrow], writes=[modrow])
        for c in list(range(0, 16)) + list(range(24, 40)):
            pt = ps[2 + c % 2]
            P.emit("pe", lambda e, c=c, pt=pt: e.transpose(pt.ap[:, 0:33], modrow.ap[:, c * 128:(c + 1) * 128], identF[0:33, 0:33]),
                   reads=[modrow, consts], writes=[pt])
            is_scale = (8 <= c < 16) or (32 <= c < 40)
            for j, col in ((0, 0), (1, 32)):
                P.emit("dve", lambda e, c=c, j=j, col=col, pt=pt, a=(1.0 if is_scale else 0.0):
                       e.tensor_scalar(out=modT.ap[:, c, j:j + 1], in0=pt.ap[:, col:col + 1], scalar1=a, scalar2=None, op0=ALU.add),
                       reads=[pt], writes=[modT])
        gi = 0
        for gcol in (2 * D, 5 * D):
            for row in (0, 32):
                for half in range(2):
                    pt = ps[4 + half]
                    P.emit("pe", lambda e, row=row, gcol=gcol, half=half, pt=pt:
                           e.matmul(pt.ap, lhsT=onesF[row:row + 1, :], rhs=modrow.ap[row:row + 1, gcol + half * 512: gcol + half * 512 + 512], start=True, stop=True),
                           reads=[consts, modrow], writes=[pt])
                    P.emit("act", lambda e, gi=gi, half=half, pt=pt: e.activation(out=gbt.ap[:, gi, half * 512:(half + 1) * 512], in_=pt.ap, func=AF.Identity),
                           reads=[pt], writes=[gbt])
                gi += 1
        P.barrier()
        P.release(m)

        phase_done('P0')
        m = P.mark()
        xts = [P.alloc([128, D], F32, f"xt{i}") for i in range(2)]
        lnw = [(P.alloc([128, D], F32, f"sq{i}"), P.alloc([128, 1], F32, f"ssum{i}"), P.alloc([128, 1], F32, f"rs{i}"), P.alloc([128, D], BF16, f"xn{i}")) for i in range(2)]
        uTb = [P.alloc([128, 8, 512], BF16, f"uTb{i}") for i in range(2)]
        for bj, (c0, w) in enumerate(BLOCKS):
            ub = uTb[bj % 2]
            col = 1 if bj == 8 else 0
            for tt in range(w // 128):
                ti = c0 // 128 + tt
                xt = xts[ti % 2]
                dma("sp", xt.ap, xsrc(l)[ti * 128:(ti + 1) * 128, :], [Xt[ti]], [xt])
                ln_tile(xt, ub, ub.ap, tt * 128, 0, 8, col, lnw[ti % 2] + (ps[ti % 2],))
            dma("pool", UT_d[:, :, c0:c0 + w].rearrange("k p n -> p k n"), ub.ap[:, :, 0:w], [ub], [UTt[bj]])
            if debug and l == 0:
                dma("pool", dbg["UT"][:, :, c0:c0 + w].rearrange("k p n -> p k n"), ub.ap[:, :, 0:w], [ub], [])
        P.barrier()
        P.release(m)

        phase_done('P1')
        m_mla = P.mark()
        wfm = P.alloc([128, 8, 5 * 128], BF16, "wfm")
        load_cast(wfm, wfm.ap, winfm_d[l][:, 0:5 * 128].rearrange("(k p) n -> p k n", p=128), 8, 5 * 128)
        wuq = P.alloc([128, 2, 1024], BF16, "wuq")
        load_cast(wuq, wuq.ap, wuq_d[l].rearrange("(k p) n -> p k n", p=128), 2, 1024)
        wkv = P.alloc([128, 2, 512], BF16, "wkv")
        load_cast(wkv, wkv.ap[:, 0:1, :], wk_d[l].rearrange("(k p) n -> p k n", p=128), 1, 512)
        load_cast(wkv, wkv.ap[:, 1:2, :], wv_d[l].rearrange("(k p) n -> p k n", p=128), 1, 512)
        knT = [P.alloc([128, 4, 512], BF16, f"knT{j}") for j in range(9)]
        krT = [P.alloc([128, 512], BF16, f"krT{j}") for j in range(9)]
        Vt = [P.alloc([128, 512], BF16, f"V{i}") for i in range(NTILE)]
        m_k = P.mark()
        ublk = [P.alloc([128, 8, 512], BF16, f"ublk{i}") for i in range(2)]
        ropc = [P.alloc([128, 512], F32, f"ropc{i}") for i in range(2)]
        rops = [P.alloc([128, 512], F32, f"rops{i}") for i in range(2)]
        sqb = P.alloc([128, 2, 512], BF16, "sqb")
        rstd = P.alloc([128, 512], F32, "rstd")
        ckvn = P.alloc([128, 512], BF16, "ckvn")
        t1 = P.alloc([128, 512], F32, "t1")
        t2 = P.alloc([128, 512], F32, "t2")

        def inproj_fm(ub, chunk, pt, w):
            for k in range(8):
                P.emit("pe", lambda e, k=k: e.matmul(pt.ap[:, 0:w], lhsT=wfm.ap[:, k, chunk * 128:(chunk + 1) * 128], rhs=ub.ap[:, k, 0:w],
                                                     start=(k == 0), stop=(k == 7)), reads=[wfm, ub], writes=[pt])

        for bj, (c0, w) in enumerate(BLOCKS):
            ub = ublk[bj % 2]
            rc, rsn = ropc[bj % 2], rops[bj % 2]
            dma("sp", ub.ap[:, :, 0:w], UT_d[:, :, c0:c0 + w].rearrange("k p n -> p k n"), [UTt[bj]], [ub])
            dma("sp", rc.ap[:, 0:w], ropeC_d[:, c0:c0 + w], [], [rc])
            dma("sp", rsn.ap[:, 0:w], ropeS_d[:, c0:c0 + w], [], [rsn])
            inproj_fm(ub, 2, ps[0], w)
            inproj_fm(ub, 3, ps[1], w)
            inproj_fm(ub, 4, ps[2], w)
            P.emit("act", lambda e, w=w: e.activation(out=sqb.ap[:, 0, 0:w], in_=ps[0].ap[:, 0:w], func=AF.Square), reads=[ps[0]], writes=[sqb])
            P.emit("pe", lambda e, w=w: e.matmul(ps[3].ap[:, 0:w], lhsT=onesB, rhs=sqb.ap[:, 0, 0:w], start=True, stop=True), reads=[cb, sqb], writes=[ps[3]])
            rsqrt_ops(P, rstd, ps[3].ap[:, 0:w], ps[3], rstd.ap[:, 0:w], 128 * EPS)
            P.emit("dve", lambda e, w=w: e.scalar_tensor_tensor(out=ckvn.ap[:, 0:w], in0=ps[0].ap[:, 0:w], scalar=sm[:, 27:28], in1=rstd.ap[:, 0:w], op0=ALU.mult, op1=ALU.mult),
                   reads=[ps[0], rstd, small], writes=[ckvn])
            for h in range(4):
                pt = ps[4 + h % 2]
                P.emit("pe", lambda e, h=h, pt=pt, w=w: e.matmul(pt.ap[:, 0:w], lhsT=wkv.ap[:, 0, h * 128:(h + 1) * 128], rhs=ckvn.ap[:, 0:w], start=True, stop=True),
                       reads=[wkv, ckvn], writes=[pt])
                P.emit("act", lambda e, h=h, pt=pt, w=w, bj=bj: e.activation(out=knT[bj].ap[:, h, 0:w], in_=pt.ap[:, 0:w], func=AF.Identity), reads=[pt], writes=[knT[bj]])
            for tt in range(w // 128):
                ti = c0 // 128 + tt
                pt = ps[6 + tt % 2]
                P.emit("pe", lambda e, tt=tt, pt=pt: e.matmul(pt.ap, lhsT=ckvn.ap[:, tt * 128:(tt + 1) * 128], rhs=wkv.ap[:, 1, :], start=True, stop=True),
                       reads=[ckvn, wkv], writes=[pt])
                P.emit("dve", lambda e, ti=ti, pt=pt: e.tensor_copy(out=Vt[ti].ap, in_=pt.ap), reads=[pt], writes=[Vt[ti]])
            P.emit("dve", lambda e, w=w, rc=rc: e.tensor_tensor(out=t1.ap[:, 0:w], in0=ps[1].ap[:, 0:w], in1=rc.ap[:, 0:w], op=ALU.mult), reads=[ps[1], rc], writes=[t1])
            P.emit("dve", lambda e, w=w, rsn=rsn: e.tensor_tensor(out=t2.ap[:, 0:w], in0=ps[2].ap[:, 0:w], in1=rsn.ap[:, 0:w], op=ALU.mult), reads=[ps[2], rsn], writes=[t2])
            P.emit("pool", lambda e, w=w, bj=bj: e.tensor_tensor(out=krT[bj].ap[:, 0:w], in0=t1.ap[:, 0:w], in1=t2.ap[:, 0:w], op=ALU.add), reads=[t1, t2], writes=[krT[bj]])
        P.barrier()
        P.release(m_k)

        phase_done('P2')
        ublk = [P.alloc([128, 8, 512], BF16, f"ublkq{i}") for i in range(2)]
        ropc = [P.alloc([128, 512], F32, f"ropcq{i}") for i in range(2)]
        rops = [P.alloc([128, 512], F32, f"ropsq{i}") for i in range(2)]
        sqb = P.alloc([128, 2, 512], BF16, "sqbq")
        rstd = P.alloc([128, 512], F32, "rstdq")
        cqn = P.alloc([128, 2, 512], BF16, "cqn")
        qnT = [P.alloc([128, 4, 512], BF16, f"qnT{i}") for i in range(2)]
        qrT = [P.alloc([128, 2, 512], BF16, f"qrT{i}") for i in range(2)]
        t1 = P.alloc([128, 512], F32, "t1q")
        t2 = P.alloc([128, 512], F32, "t2q")
        pT = [P.alloc([128, 512], BF16, f"pT{i}") for i in range(4)]
        rden = P.alloc([128, 512], F32, "rden")
        yT = [P.alloc([128, 512], BF16, f"yT{i}") for i in range(2)]
        pti = 0
        for bj, (c0, w) in enumerate(BLOCKS):
            ub = ublk[bj % 2]
            rc, rsn = ropc[bj % 2], rops[bj % 2]
            qn, qr = qnT[bj % 2], qrT[bj % 2]
            dma("sp", ub.ap[:, :, 0:w], UT_d[:, :, c0:c0 + w].rearrange("k p n -> p k n"), [UTt[bj]], [ub])
            dma("sp", rc.ap[:, 0:w], ropeC_d[:, c0:c0 + w], [], [rc])
            dma("sp", rsn.ap[:, 0:w], ropeS_d[:, c0:c0 + w], [], [rsn])
            inproj_fm(ub, 0, ps[0], w)
            inproj_fm(ub, 1, ps[1], w)
            for c in range(2):
                P.emit("act", lambda e, c=c, w=w: e.activation(out=sqb.ap[:, c, 0:w], in_=ps[c].ap[:, 0:w], func=AF.Square), reads=[ps[c]], writes=[sqb])
            for c in range(2):
                P.emit("pe", lambda e, c=c, w=w: e.matmul(ps[2].ap[:, 0:w], lhsT=onesB, rhs=sqb.ap[:, c, 0:w], start=(c == 0), stop=(c == 1)), reads=[cb, sqb], writes=[ps[2]])
            rsqrt_ops(P, rstd, ps[2].ap[:, 0:w], ps[2], rstd.ap[:, 0:w], 256 * EPS)
            for c in range(2):
                P.emit("dve", lambda e, c=c, w=w: e.scalar_tensor_tensor(out=cqn.ap[:, c, 0:w], in0=ps[c].ap[:, 0:w], scalar=sm[:, 25 + c:26 + c], in1=rstd.ap[:, 0:w], op0=ALU.mult, op1=ALU.mult),
                       reads=[ps[c], rstd, small], writes=[cqn])

            def qproj(ch, pt):
                for kc in range(2):
                    P.emit("pe", lambda e, kc=kc: e.matmul(pt.ap[:, 0:w], lhsT=wuq.ap[:, kc, ch * 128:(ch + 1) * 128], rhs=cqn.ap[:, kc, 0:w], start=(kc == 0), stop=(kc == 1)),
                           reads=[wuq, cqn], writes=[pt])
            for h in range(4):
                pt = ps[3 + h % 2]
                qproj(h, pt)
                P.emit("dve", lambda e, h=h, pt=pt, w=w, qn=qn: e.tensor_copy(out=qn.ap[:, h, 0:w], in_=pt.ap[:, 0:w]), reads=[pt], writes=[qn])
            for pr in range(2):
                qproj(4 + pr, ps[5])
                qproj(6 + pr, ps[6])
                P.emit("dve", lambda e, w=w, rc=rc: e.tensor_tensor(out=t1.ap[:, 0:w], in0=ps[5].ap[:, 0:w], in1=rc.ap[:, 0:w], op=ALU.mult), reads=[ps[5], rc], writes=[t1])
                P.emit("dve", lambda e, w=w, rsn=rsn: e.tensor_tensor(out=t2.ap[:, 0:w], in0=ps[6].ap[:, 0:w], in1=rsn.ap[:, 0:w], op=ALU.mult), reads=[ps[6], rsn], writes=[t2])
                P.emit("pool", lambda e, w=w, pr=pr, qr=qr: e.tensor_tensor(out=qr.ap[:, pr, 0:w], in0=t1.ap[:, 0:w], in1=t2.ap[:, 0:w], op=ALU.add), reads=[t1, t2], writes=[qr])
            ktiles = [32, 33] if bj == 8 else list(range(NTILE))
            for h in range(4):
                po, pd = ps[0], ps[1]
                hp = 64 * (h % 2)
                nkt = len(ktiles)
                pps = {}

                def emit_s(ki):
                    nonlocal pti
                    kt = ktiles[ki]
                    kb, ko = kt // 4, (kt % 4) * 128
                    pss = ps[2 + ki % 3]
                    pp = pT[pti % 4]
                    pti += 1
                    P.emit("pe", lambda e: e.matmul(pss.ap[:, 0:w], lhsT=knT[kb].ap[:, h, ko:ko + 128], rhs=qn.ap[:, h, 0:w], start=True, stop=False),
                           reads=[knT[kb], qn], writes=[pss])
                    P.emit("pe", lambda e: e.matmul(pss.ap[:, 0:w], lhsT=krT[kb].ap[hp:hp + 64, ko:ko + 128], rhs=qr.ap[hp:hp + 64, h // 2, 0:w], start=False, stop=True),
                           reads=[krT[kb], qr], writes=[pss])
                    P.emit("act", lambda e: e.activation(out=pp.ap[:, 0:w], in_=pss.ap[:, 0:w], func=AF.Exp, scale=SCALE), reads=[pss], writes=[pp])
                    pps[ki] = pp

                emit_s(0)
                if nkt > 1:
                    emit_s(1)
                for ki, kt in enumerate(ktiles):
                    if ki + 2 < nkt:
                        emit_s(ki + 2)
                    pp = pps.pop(ki)
                    first, last = ki == 0, ki == nkt - 1
                    P.emit("pe", lambda e, kt=kt, pp=pp, first=first, last=last: e.matmul(po.ap[:, 0:w], lhsT=Vt[kt].ap[:, h * 128:(h + 1) * 128], rhs=pp.ap[:, 0:w], start=first, stop=last),
                           reads=[Vt[kt], pp], writes=[po])
                    P.emit("pe", lambda e, pp=pp, first=first, last=last: e.matmul(pd.ap[:, 0:w], lhsT=onesB, rhs=pp.ap[:, 0:w], start=first, stop=last),
                           reads=[cb, pp], writes=[pd])
                yt = yT[h % 2]
                P.emit("dve", lambda e, w=w: e.reciprocal(out=rden.ap[:, 0:w], in_=pd.ap[:, 0:w]), reads=[pd], writes=[rden])
                P.emit("dve", lambda e, w=w, yt=yt: e.tensor_tensor(out=yt.ap[:, 0:w], in0=po.ap[:, 0:w], in1=rden.ap[:, 0:w], op=ALU.mult), reads=[po, rden], writes=[yt])
                dma("pool", MT_d[h, :, c0:c0 + w], yt.ap[:, 0:w], [yt], [MTt[h][bj]])
        P.barrier()
        P.release(m_mla)

        phase_done('P3')
        m = P.mark()
        wfm = P.alloc([128, 8, 4 * 128], BF16, "wfm_ml")
        load_cast(wfm, wfm.ap, winfm_d[l][:, 5 * 128:9 * 128].rearrange("(k p) n -> p k n", p=128), 8, 512)
        wtm = P.alloc([128, 8, 784], BF16, "wtm")
        load_cast(wtm, wtm.ap, wintm_d[l].rearrange("(k p) n -> p k n", p=128), 8, 784)
        mqT = [P.alloc([128, 4, 512], BF16, f"mqT{j}") for j in range(9)]
        mkT = [P.alloc([128, 2, 512], BF16, f"mkT{j}") for j in range(9)]
        mk = [P.alloc([128, 256], BF16, f"mk{i}") for i in range(NTILE)]
        mvA = [P.alloc([128, 4, 66], BF16, f"mvA{i}") for i in range(NTILE)]
        moS = [P.alloc([128, 256], BF16, f"moS{i}") for i in range(NTILE)]
        ebt = [P.alloc([128, 8], F32, f"eb{i}") for i in range(NTILE)]
        eit = [P.alloc([128, 8], F32, f"ei{i}") for i in range(NTILE)]
        eblt = [P.alloc([128, 8], F32, f"ebl{i}") for i in range(NTILE)]
        hf = [P.alloc([128, 256], BF16, f"hf{i}") for i in range(NTILE)]
        m2 = P.mark()
        ublk = [P.alloc([128, 8, 512], BF16, "ublkm0")] * 2
        mlw = [(P.alloc([128, 16], F32, f"gt{i}"), P.alloc([128, 8], F32, f"lf{i}"), P.alloc([128, 8], F32, f"dd{i}"), P.alloc([128, 256], F32, f"tnh{i}")) for i in range(2)]
        for bj, (c0, w) in enumerate(BLOCKS):
            ub = ublk[bj % 2]
            dma("sp", ub.ap[:, :, 0:w], UT_d[:, :, c0:c0 + w].rearrange("k p n -> p k n"), [UTt[bj]], [ub])
            for ch in range(4):
                pt = ps[ch % 2]
                for k in range(8):
                    P.emit("pe", lambda e, k=k, ch=ch, pt=pt, w=w, ub=ub: e.matmul(pt.ap[:, 0:w], lhsT=wfm.ap[:, k, ch * 128:(ch + 1) * 128], rhs=ub.ap[:, k, 0:w], start=(k == 0), stop=(k == 7)),
                           reads=[wfm, ub], writes=[pt])
                if ch < 2:
                    if ch == 0:
                        P.emit("pool", lambda e, bj=bj: e.memset(mqT[bj].ap, 0.0), writes=[mqT[bj]])
                    for hh in range(2):
                        P.emit("act", lambda e, ch=ch, hh=hh, pt=pt, w=w, bj=bj: e.activation(out=mqT[bj].ap[64 * hh:64 * hh + 64, 2 * ch + hh, 0:w], in_=pt.ap[64 * hh:64 * hh + 64, 0:w], func=AF.Identity, scale=0.125), reads=[pt], writes=[mqT[bj]])
                else:
                    P.emit("act", lambda e, ch=ch, pt=pt, w=w, bj=bj: e.activation(out=mkT[bj].ap[:, ch - 2, 0:w], in_=pt.ap[:, 0:w], func=AF.Identity), reads=[pt], writes=[mkT[bj]])
            for tt in range(w // 128):
                ti = c0 // 128 + tt
                gt, lf, dd, tnh = mlw[ti % 2]
                pa, pb = ps[2 + tt % 2], ps[4 + tt % 2]
                for k in range(8):
                    P.emit("pe", lambda e, k=k, tt=tt, pa=pa, ub=ub: e.matmul(pa.ap, lhsT=ub.ap[:, k, tt * 128:(tt + 1) * 128], rhs=wtm.ap[:, k, 0:512], start=(k == 0), stop=(k == 7)),
                           reads=[ub, wtm], writes=[pa])
                for k in range(8):
                    P.emit("pe", lambda e, k=k, tt=tt, pb=pb, ub=ub: e.matmul(pb.ap[:, 0:272], lhsT=ub.ap[:, k, tt * 128:(tt + 1) * 128], rhs=wtm.ap[:, k, 512:784], start=(k == 0), stop=(k == 7)),
                           reads=[ub, wtm], writes=[pb])
                P.emit("dve", lambda e, ti=ti, pa=pa: e.tensor_copy(out=mk[ti].ap, in_=pa.ap[:, 0:256]), reads=[pa], writes=[mk[ti]])
                P.emit("pool", lambda e, ti=ti: e.memset(mvA[ti].ap, 1.0), writes=[mvA[ti]])
                P.emit("dve", lambda e, ti=ti, pa=pa: e.tensor_copy(out=mvA[ti].ap[:, :, 0:64], in_=pa.ap[:, 256:512].rearrange("p (h d) -> p h d", h=4)), reads=[pa], writes=[mvA[ti]])
                sigmoid_ops(P, tnh, tnh.ap, pb.ap[:, 0:256], [pb], eng2="pool")
                P.emit("pool", lambda e, ti=ti: e.tensor_copy(out=moS[ti].ap, in_=tnh.ap), reads=[tnh], writes=[moS[ti]])
                P.emit("dve", lambda e, pb=pb: e.tensor_tensor(out=gt.ap, in0=pb.ap[:, 256:272], in1=gbias.ap[:, l, :], op=ALU.add), reads=[pb, gbias], writes=[gt])
                gv = gt.ap.rearrange("p (a b) -> p a b", a=4)
                lfv = lf.ap.rearrange("p (a b) -> p a b", a=2)
                P.emit("act", lambda e, gv=gv, lfv=lfv: e.activation(out=lfv, in_=gv[:, 1::2, :], func=AF.Exp, scale=-1.0), reads=[gt], writes=[lf])
                P.emit("act", lambda e: e.activation(out=lf.ap, in_=lf.ap, func=AF.Ln, bias=1.0), reads=[lf], writes=[lf])
                P.emit("dve", lambda e: e.tensor_scalar(out=lf.ap, in0=lf.ap, scalar1=-1.0, scalar2=None, op0=ALU.mult), reads=[lf], writes=[lf])
                pc = ps[6 + ti % 2]
                P.emit("pe", lambda e, pc=pc: e.matmul(pc.ap[:, 0:4], lhsT=maskU, rhs=lf.ap[:, 0:4], start=True, stop=True), reads=[consts, lf], writes=[pc])
                P.emit("pe", lambda e, pc=pc: e.matmul(pc.ap[:, 4:8], lhsT=maskL, rhs=lf.ap[:, 4:8], start=True, stop=True), reads=[consts, lf], writes=[pc])
                P.emit("pe", lambda e, pc=pc: e.matmul(pc.ap[:, 8:16], lhsT=onesF, rhs=lf.ap[:, 0:8], start=True, stop=True), reads=[consts, lf], writes=[pc])
                P.emit("act", lambda e, ti=ti, pc=pc: e.activation(out=ebt[ti].ap, in_=pc.ap[:, 0:8], func=AF.Exp), reads=[pc], writes=[ebt[ti]])
                P.emit("act", lambda e, ti=ti, pc=pc: e.activation(out=eblt[ti].ap, in_=pc.ap[:, 8:16], func=AF.Exp), reads=[pc], writes=[eblt[ti]])
                ddv = dd.ap.rearrange("p (a b) -> p a b", a=2)
                P.emit("dve", lambda e, pc=pc, gv=gv, ddv=ddv: e.tensor_tensor(out=ddv, in0=gv[:, 0::2, :], in1=pc.ap[:, 0:8].rearrange("p (a b) -> p a b", a=2), op=ALU.subtract),
                       reads=[gt, pc], writes=[dd])
                P.emit("act", lambda e, ti=ti: e.activation(out=eit[ti].ap, in_=dd.ap, func=AF.Exp), reads=[dd], writes=[eit[ti]])
        P.barrier()
        P.release(m2)
        phase_done('P4a')
        ptT = [[P.alloc([128, 4, 128], BF16, f"ptT{d}{i}") for i in range(2)] for d in range(2)]
        ktp = [[P.alloc([128, 4, 128], BF16, f"ktp{d}{i}") for i in range(2)] for d in range(2)]
        Cst = [P.alloc([128, 2, 65], F32, f"Cst{d}") for d in range(2)]
        Cbf = [P.alloc([128, 2, 66], BF16, f"Cbf{d}") for d in range(2)]
        ctmp = P.alloc([128, 2, 65], F32, "ctmp")
        den4 = P.alloc([128, 4], F32, "den4")
        r4 = P.alloc([128, 4], F32, "r4")
        hb = P.alloc([128, 256], F32, "hb")
        hs = P.alloc([128, 256], F32, "hs")
        hq = P.alloc([128, 256], F32, "hq")
        ss4 = P.alloc([128, 4], F32, "ss4")
        ybf = P.alloc([128, 256], BF16, "ybf")
        ybT = [P.alloc([128, 2, 512], BF16, f"ybT{i}") for i in range(2)]
        for d in range(2):
            for i in range(2):
                P.emit("pool", lambda e, d=d, i=i: e.memset(ktp[d][i].ap, 0.0), writes=[ktp[d][i]])
        fwd_order = [32, 33] + list(range(32))
        bwd_order = [33, 32] + list(range(31, -1, -1))
        ybcount = {}
        import os as _os
        _ns = int(_os.environ.get('DBG_STEPS', '99'))
        _nd = int(_os.environ.get('DBG_DIRS', '2'))
        _lvl = int(_os.environ.get('DBG_LVL', '99'))
        _skip = _os.environ.get('DBG_SKIP', '')
        for d, order in ((0, fwd_order[:_ns]), (1, bwd_order[:_ns]))[:_nd]:
            mask = maskU if d == 0 else maskL
            for step, ti in enumerate(order):
                bj, to = (8, (ti - 32) * 128) if ti >= 32 else (ti // 4, (ti % 4) * 128)
                if step == 0:
                    P.emit("pool", lambda e, d=d: e.memset(Cst[d].ap, 0.0), writes=[Cst[d]])
                    P.emit("pool", lambda e, d=d: e.memset(Cbf[d].ap, 0.0), writes=[Cbf[d]])
                pS, pO, pKV = ps[step % 2], ps[2 + step % 2], ps[4 + step % 2]
                pt_, kt_ = ptT[d][step % 2], ktp[d][step % 2]
                for h in range(4):
                    hp, mch = 64 * (h % 2), h // 2
                    if 'S' in _skip or ('E' in _skip and h % 2 == 1) or ('O' in _skip and h % 2 == 0):
                        continue
                    P.emit("pe", lambda e, h=h, hp=hp, mch=mch, bj=bj, to=to, pS=pS: e.matmul(pS.ap[:, h * 128:(h + 1) * 128], lhsT=mkT[bj].ap[:, mch, to:to + 128], rhs=mqT[bj].ap[:, h, to:to + 128], start=True, stop=True),
                           reads=[mkT[bj], mqT[bj]], writes=[pS])
                for h in range(4):
                    hp = 64 * (h % 2)
                    if 'P' not in _skip:
                      P.emit("dve", lambda e, h=h, ti=ti, d=d, pS=pS, pt_=pt_, mask=mask: e.scalar_tensor_tensor(out=pt_.ap[:, h, :], in0=pS.ap[:, h * 128:(h + 1) * 128], scalar=eit[ti].ap[:, d * 4 + h:d * 4 + h + 1], in1=mask, op0=ALU.mult, op1=ALU.mult),
                             reads=[pS, eit[ti], consts], writes=[pt_])
                    if "K" in _skip:
                        continue
                    P.emit("pool", lambda e, h=h, hp=hp, ti=ti, d=d, kt_=kt_: e.tensor_scalar(out=kt_.ap[:, h, hp:hp + 64], in0=mk[ti].ap[:, h * 64:(h + 1) * 64], scalar1=eit[ti].ap[:, d * 4 + h:d * 4 + h + 1], scalar2=None, op0=ALU.mult),
                           reads=[mk[ti], eit[ti]], writes=[kt_])
                if _lvl < 2:
                    continue
                for h in range(4):
                    hp, mch = 64 * (h % 2), h // 2
                    P.emit("pe", lambda e, h=h, ti=ti, pO=pO, pt_=pt_: e.matmul(pO.ap[:, h * 65:(h + 1) * 65], lhsT=pt_.ap[:, h, :], rhs=mvA[ti].ap[:, h, 0:65], start=True, stop=False),
                           reads=[pt_, mvA[ti]], writes=[pO])
                    P.emit("pe", lambda e, h=h, hp=hp, mch=mch, bj=bj, to=to, d=d, pO=pO: e.matmul(pO.ap[:, h * 65:(h + 1) * 65], lhsT=mqT[bj].ap[:, h, to:to + 128], rhs=Cbf[d].ap[:, mch, 0:65], start=False, stop=True),
                           reads=[mqT[bj], Cbf[d]], writes=[pO])
                for mch in range(2):
                    for hh in range(2):
                        h = mch * 2 + hh
                        P.emit("pe", lambda e, h=h, mch=mch, hh=hh, ti=ti, pKV=pKV, kt_=kt_: e.matmul(pKV.ap[:, mch * 65:(mch + 1) * 65], lhsT=kt_.ap[:, h, :], rhs=mvA[ti].ap[:, h, 0:65], start=(hh == 0), stop=(hh == 1)),
                               reads=[kt_, mvA[ti]], writes=[pKV])
                if _lvl < 3:
                    continue
                Ov = pO.ap[:, 0:260].rearrange("p (h c) -> p h c", h=4)
                ebv = ebt[ti].ap[:, d * 4:(d + 1) * 4]
                P.emit("dve", lambda e, Ov=Ov, ebv=ebv: e.tensor_tensor(out=den4.ap, in0=Ov[:, :, 64], in1=ebv, op=ALU.mult), reads=[pO, ebt[ti]], writes=[den4])
                P.emit("act", lambda e: e.activation(out=den4.ap, in_=den4.ap, func=AF.Abs), reads=[den4], writes=[den4])
                P.emit("dve", lambda e: e.tensor_scalar(out=den4.ap, in0=den4.ap, scalar1=1.0, scalar2=None, op0=ALU.max), reads=[den4], writes=[den4])
                P.emit("dve", lambda e: e.reciprocal(out=den4.ap, in_=den4.ap), reads=[den4], writes=[den4])
                P.emit("dve", lambda e, ebv=ebv: e.tensor_tensor(out=r4.ap, in0=ebv, in1=den4.ap, op=ALU.mult), reads=[ebt[ti], den4], writes=[r4])
                hdst = hf[ti] if d == 0 else hb
                P.emit("dve", lambda e, Ov=Ov, hdst=hdst: e.tensor_tensor(out=hdst.ap.rearrange("p (h c) -> p h c", h=4), in0=Ov[:, :, 0:64], in1=bc_last(r4.ap, 64), op=ALU.mult),
                       reads=[pO, r4], writes=[hdst])
                if _lvl < 4:
                    continue
                KVv = pKV.ap[:, 0:130].rearrange("p (m c) -> p m c", m=2)
                P.emit("dve", lambda e, KVv=KVv, d=d: e.tensor_tensor(out=ctmp.ap, in0=KVv, in1=Cst[d].ap, op=ALU.add), reads=[pKV, Cst[d]], writes=[ctmp])
                for mch in range(2):
                    for hh in range(2):
                        h = mch * 2 + hh
                        hp = 64 * hh
                        P.emit("act", lambda e, h=h, hp=hp, mch=mch, ti=ti, d=d: e.activation(out=Cst[d].ap[hp:hp + 64, mch, :], in_=ctmp.ap[hp:hp + 64, mch, :], func=AF.Identity, scale=eblt[ti].ap[hp:hp + 64, d * 4 + h:d * 4 + h + 1]),
                               reads=[ctmp, eblt[ti]], writes=[Cst[d]])
                P.emit("act", lambda e, d=d: e.activation(out=Cbf[d].ap[:, :, 0:65], in_=Cst[d].ap, func=AF.Identity), reads=[Cst[d]], writes=[Cbf[d]])
                if _lvl < 5:
                    continue
                if d == 1:
                    P.emit("pool", lambda e, ti=ti: e.tensor_tensor(out=hs.ap, in0=hf[ti].ap, in1=hb.ap, op=ALU.add), reads=[hf[ti], hb], writes=[hs])
                    P.emit("act", lambda e: e.activation(out=hq.ap, in_=hs.ap, func=AF.Square), reads=[hs], writes=[hq])
                    P.emit("dve", lambda e: e.reduce_sum(out=ss4.ap, in_=hq.ap.rearrange("p (h c) -> p h c", h=4), axis=AX.X), reads=[hq], writes=[ss4])
                    rsqrt_ops(P, ss4, ss4.ap, ss4, ss4.ap, 64 * EPS)
                    P.emit("dve", lambda e: e.tensor_tensor(out=hq.ap.rearrange("p (h c) -> p h c", h=4), in0=hs.ap.rearrange("p (h c) -> p h c", h=4), in1=bc_last(ss4.ap, 64), op=ALU.mult),
                           reads=[hs, ss4], writes=[hq])
                    P.emit("dve", lambda e, ti=ti: e.scalar_tensor_tensor(out=ybf.ap, in0=hq.ap, scalar=8.0, in1=moS[ti].ap, op0=ALU.mult, op1=ALU.mult), reads=[hq, moS[ti]], writes=[ybf])
                    ptr = ps[6 + step % 2]
                    pv = psb(ptr)
                    for c in range(2):
                        P.emit("pe", lambda e, c=c, pv=pv, ptr=ptr: e.transpose(pv[:, c * 128:(c + 1) * 128], ybf.ap[:, c * 128:(c + 1) * 128], identB), reads=[ybf, cb], writes=[ptr])
                    yb = ybT[bj % 2]
                    P.emit("act", lambda e, pv=pv, yb=yb, to=to, ptr=ptr: e.activation(out=yb.ap[:, :, to:to + 128], in_=pv[:, 0:256].rearrange("p (c n) -> p c n", c=2), func=AF.Identity), reads=[ptr], writes=[yb])
                    ybcount[bj] = ybcount.get(bj, 0) + 1
                    if ybcount[bj] == BLOCKS[bj][1] // 128:
                        c0, w = BLOCKS[bj]
                        for c in range(2):
                            dma("pool", MT_d[4 + c, :, c0:c0 + w], yb.ap[:, c, 0:w], [yb], [MTt[4 + c][bj]])
        P.barrier()
        P.release(m)

        phase_done('P4')
        m = P.mark()
        wbd = P.alloc([128, 8, 128], BF16, "wbd")
        load_cast(wbd, wbd.ap, wbd_d[l], 8, 128)
        wfm = P.alloc([128, 8, 4 * 128], BF16, "wfm_lru")
        load_cast(wfm, wfm.ap, winfm_d[l][:, 9 * 128:13 * 128].rearrange("(k p) n -> p k n", p=128), 8, 512)
        ublk = [P.alloc([128, 8, 512], BF16, f"ublkl{i}") for i in range(2)]
        xb = P.alloc([128, NT], F32, "xb")
        gb = P.alloc([128, NT], BF16, "gb")
        xs = P.alloc([128, NT], F32, "xs")
        xsb = P.alloc([128, NT], BF16, "xsb")
        At = P.alloc([128, NT], F32, "At")
        Ut = P.alloc([128, NT], F32, "Ut")
        Hf = P.alloc([128, NT], F32, "Hf")
        Hb = P.alloc([128, NT], F32, "Hb")
        tq = [P.alloc([128, 512], F32, f"tq{i}") for i in range(4)]
        ycb = [P.alloc([128, 512], BF16, f"ycb{i}") for i in range(2)]
        segs = [(0, NLAT), (NLAT, NCTX)]
        for c in range(2):
            for bj, (c0, w) in enumerate(BLOCKS):
                ub = ublk[bj % 2]
                dma("sp", ub.ap[:, :, 0:w], UT_d[:, :, c0:c0 + w].rearrange("k p n -> p k n"), [UTt[bj]], [ub])
                for which, dst in ((0, xb), (1, gb)):
                    pt = ps[(2 * bj + which) % 4]
                    ch = which * 2 + c
                    for k in range(8):
                        P.emit("pe", lambda e, k=k, ch=ch, pt=pt, w=w, ub=ub: e.matmul(pt.ap[:, 0:w], lhsT=wfm.ap[:, k, ch * 128:(ch + 1) * 128], rhs=ub.ap[:, k, 0:w], start=(k == 0), stop=(k == 7)),
                               reads=[wfm, ub], writes=[pt])
                    P.emit("act", lambda e, pt=pt, w=w, c0=c0, dst=dst: e.activation(out=dst.ap[:, c0:c0 + w], in_=pt.ap[:, 0:w], func=AF.Identity), reads=[pt], writes=[dst])
            cw = lambda j: sm[:, 3 + c * 4 + j: 4 + c * 4 + j]
            for (s0, sl) in segs:
                P.emit("dve", lambda e, s0=s0, sl=sl: e.tensor_scalar(out=xs.ap[:, s0:s0 + sl], in0=xb.ap[:, s0:s0 + sl], scalar1=cw(2), scalar2=sm[:, 11 + c:12 + c], op0=ALU.mult, op1=ALU.add),
                       reads=[xb, small], writes=[xs])
                P.emit("dve", lambda e, s0=s0, sl=sl: e.scalar_tensor_tensor(out=xs.ap[:, s0 + 2:s0 + sl], in0=xb.ap[:, s0:s0 + sl - 2], scalar=cw(0), in1=xs.ap[:, s0 + 2:s0 + sl], op0=ALU.mult, op1=ALU.add),
                       reads=[xb, xs, small], writes=[xs])
                P.emit("dve", lambda e, s0=s0, sl=sl: e.scalar_tensor_tensor(out=xs.ap[:, s0 + 1:s0 + sl], in0=xb.ap[:, s0:s0 + sl - 1], scalar=cw(1), in1=xs.ap[:, s0 + 1:s0 + sl], op0=ALU.mult, op1=ALU.add),
                       reads=[xb, xs, small], writes=[xs])
                P.emit("dve", lambda e, s0=s0, sl=sl: e.scalar_tensor_tensor(out=xs.ap[:, s0:s0 + sl - 1], in0=xb.ap[:, s0 + 1:s0 + sl], scalar=cw(3), in1=xs.ap[:, s0:s0 + sl - 1], op0=ALU.mult, op1=ALU.add),
                       reads=[xb, xs, small], writes=[xs])
            P.emit("pool", lambda e: e.tensor_copy(out=xsb.ap, in_=xs.ap), reads=[xs], writes=[xsb])
            for d in range(2):
                Hd = Hf if d == 0 else Hb
                for bj, (c0, w) in enumerate(BLOCKS):
                    pa, px = ps[(2 * bj) % 4 + 4 * 0], ps[(2 * bj + 1) % 4]
                    ia, ix = (d * 2 + 0) * 2 + c, (d * 2 + 1) * 2 + c
                    P.emit("pe", lambda e, ia=ia, pa=pa, c0=c0, w=w: e.matmul(pa.ap[:, 0:w], lhsT=wbd.ap[:, ia, :], rhs=xsb.ap[:, c0:c0 + w], start=True, stop=True), reads=[wbd, xsb], writes=[pa])
                    P.emit("pe", lambda e, ix=ix, px=px, c0=c0, w=w: e.matmul(px.ap[:, 0:w], lhsT=wbd.ap[:, ix, :], rhs=xsb.ap[:, c0:c0 + w], start=True, stop=True), reads=[wbd, xsb], writes=[px])
                    ta, tx = tq[2 * (bj % 2)], tq[2 * (bj % 2) + 1]
                    dc = d * 2 + c
                    sigmoid_ops(P, ta, ta.ap[:, 0:w], pa.ap[:, 0:w], [pa, small], bias=sm[:, 28 + dc:29 + dc], eng2="act")
                    P.emit("act", lambda e, w=w, dc=dc, c0=c0: e.activation(out=At.ap[:, c0:c0 + w], in_=ta.ap[:, 0:w], func=AF.Exp, scale=sm[:, 36 + dc:37 + dc]), reads=[ta, small], writes=[At])
                    sigmoid_ops(P, tx, tx.ap[:, 0:w], px.ap[:, 0:w], [px, small], bias=sm[:, 32 + dc:33 + dc], eng2="act")
                    P.emit("pool", lambda e, w=w, c0=c0: e.tensor_tensor(out=tx.ap[:, 0:w], in0=tx.ap[:, 0:w], in1=xs.ap[:, c0:c0 + w], op=ALU.mult), reads=[tx, xs], writes=[tx])
                    P.emit("dve", lambda e, w=w, c0=c0: e.tensor_tensor(out=ta.ap[:, 0:w], in0=At.ap[:, c0:c0 + w], in1=At.ap[:, c0:c0 + w], op=ALU.mult), reads=[At], writes=[ta])
                    P.emit("act", lambda e, w=w: e.activation(out=ta.ap[:, 0:w], in_=ta.ap[:, 0:w], func=AF.Ln, scale=-1.0, bias=1.0), reads=[ta], writes=[ta])
                    P.emit("act", lambda e, w=w: e.activation(out=ta.ap[:, 0:w], in_=ta.ap[:, 0:w], func=AF.Exp, scale=0.5), reads=[ta], writes=[ta])
                    P.emit("dve", lambda e, w=w, c0=c0: e.tensor_tensor(out=Ut.ap[:, c0:c0 + w], in0=ta.ap[:, 0:w], in1=tx.ap[:, 0:w], op=ALU.mult), reads=[ta, tx], writes=[Ut])
                SC = 1024
                if d == 0:
                    pieces = [(NLAT, NCTX)] + [(i * SC, SC) for i in range(NLAT // SC)]
                else:
                    pieces = [(NLAT, NCTX)] + [(i * SC, SC) for i in range(NLAT // SC - 1, -1, -1)]
                prev_last = None
                for (p0, pl) in pieces:
                    def view(t, p0=p0, pl=pl):
                        a = t.ap[:, p0:p0 + pl]
                        if d == 0:
                            return a
                        return AP(a.tensor, a.offset + pl - 1, [list(a.ap[0]), [-1, pl]])
                    init = 0.0 if prev_last is None else Hd.ap[:, prev_last:prev_last + 1]
                    P.emit("dve", lambda e, view=view, init=init, Hd=Hd: e.tensor_tensor_scan(out=view(Hd), data0=view(At), data1=view(Ut), initial=init, op0=ALU.mult, op1=ALU.add),
                           reads=[At, Ut, Hd], writes=[Hd])
                    prev_last = (p0 + pl - 1) if d == 0 else p0
            for bj, (c0, w) in enumerate(BLOCKS):
                ta, tx = tq[2 * (bj % 2)], tq[2 * (bj % 2) + 1]
                yc = ycb[bj % 2]
                g = gb.ap[:, c0:c0 + w]
                P.emit("pool", lambda e, g=g, w=w: e.tensor_tensor(out=ta.ap[:, 0:w], in0=g, in1=g, op=ALU.mult), reads=[gb], writes=[ta])
                P.emit("dve", lambda e, w=w: e.tensor_scalar(out=ta.ap[:, 0:w], in0=ta.ap[:, 0:w], scalar1=0.044715 * 0.7978845608028654, scalar2=0.7978845608028654, op0=ALU.mult, op1=ALU.add), reads=[ta], writes=[ta])
                P.emit("dve", lambda e, g=g, w=w: e.tensor_tensor(out=ta.ap[:, 0:w], in0=ta.ap[:, 0:w], in1=g, op=ALU.mult), reads=[ta, gb], writes=[ta])
                sigmoid_ops(P, ta, ta.ap[:, 0:w], ta.ap[:, 0:w], [ta], scale=2.0, eng2="act")
                P.emit("dve", lambda e, g=g, w=w: e.tensor_tensor(out=ta.ap[:, 0:w], in0=ta.ap[:, 0:w], in1=g, op=ALU.mult), reads=[ta, gb], writes=[ta])
                P.emit("pool", lambda e, w=w, c0=c0: e.tensor_tensor(out=tx.ap[:, 0:w], in0=Hf.ap[:, c0:c0 + w], in1=Hb.ap[:, c0:c0 + w], op=ALU.add), reads=[Hf, Hb], writes=[tx])
                P.emit("dve", lambda e, w=w, yc=yc: e.tensor_tensor(out=yc.ap[:, 0:w], in0=ta.ap[:, 0:w], in1=tx.ap[:, 0:w], op=ALU.mult), reads=[ta, tx], writes=[yc])
                dma("pool", MT_d[6 + c, :, c0:c0 + w], yc.ap[:, 0:w], [yc], [MTt[6 + c][bj]])
        P.barrier()
        P.release(m)
        if debug and l == 0:
            mm = P.mark()
            dtile = [P.alloc([128, 8, 512], BF16, f"dbgm{i}") for i in range(2)]
            for bj, (c0, w) in enumerate(BLOCKS):
                dt_ = dtile[bj % 2]
                dma("sp", dt_.ap[:, :, 0:w], MT_d[:, :, c0:c0 + w].rearrange("k p n -> p k n"), [MTt[k][bj] for k in range(8)], [dt_])
                dma("pool", dbg["MT"][:, :, c0:c0 + w].rearrange("k p n -> p k n"), dt_.ap[:, :, 0:w], [dt_], [])
            P.barrier()
            P.release(mm)

        phase_done('P5')
        m = P.mark()
        wo = P.alloc([128, 8, D], BF16, "wo")
        load_cast(wo, wo.ap, wout_d[l].rearrange("(k p) n -> p k n", p=128), 8, D)
        mixb = [P.alloc([128, 8, 512], BF16, f"mixb{i}") for i in range(2)]
        xts = [P.alloc([128, D], F32, f"xto{i}") for i in range(2)]
        x1s = [P.alloc([128, D], F32, f"x1{i}") for i in range(2)]
        tmpo = P.alloc([128, D], F32, "tmpo")
        lnw = [(P.alloc([128, D], F32, f"sq2{i}"), P.alloc([128, 1], F32, f"ssum2{i}"), P.alloc([128, 1], F32, f"rs2{i}"), P.alloc([128, D], BF16, f"xn2{i}")) for i in range(2)]
        uTb = [P.alloc([128, 8, 512], BF16, f"uTb2{i}") for i in range(2)]
        for bj, (c0, w) in enumerate(BLOCKS):
            mb = mixb[bj % 2]
            ub = uTb[bj % 2]
            col = 1 if bj == 8 else 0
            dma("sp", mb.ap[:, :, 0:w], MT_d[:, :, c0:c0 + w].rearrange("k p n -> p k n"), [MTt[k][bj] for k in range(8)], [mb])
            for tt in range(w // 128):
                ti = c0 // 128 + tt
                xt, x1 = xts[ti % 2], x1s[ti % 2]
                dma("sp", xt.ap, xsrc(l)[ti * 128:(ti + 1) * 128, :], [Xt[ti]], [xt])
                for half in range(2):
                    pt = ps[2 + half]
                    for k in range(8):
                        P.emit("pe", lambda e, k=k, half=half, pt=pt, tt=tt, mb=mb: e.matmul(pt.ap, lhsT=mb.ap[:, k, tt * 128:(tt + 1) * 128], rhs=wo.ap[:, k, half * 512:(half + 1) * 512], start=(k == 0), stop=(k == 7)),
                               reads=[mb, wo], writes=[pt])
                    P.emit("dve", lambda e, half=half, pt=pt, col=col: e.tensor_tensor(out=tmpo.ap[:, half * 512:(half + 1) * 512], in0=pt.ap, in1=gbt.ap[:, col, half * 512:(half + 1) * 512], op=ALU.mult),
                           reads=[pt, gbt], writes=[tmpo])
                P.emit("pool", lambda e, xt=xt, x1=x1: e.tensor_tensor(out=x1.ap, in0=tmpo.ap, in1=xt.ap, op=ALU.add), reads=[tmpo, xt], writes=[x1])
                dma("pool", X_d[ti * 128:(ti + 1) * 128, :], x1.ap, [x1], [Xt[ti]])
                ln_tile(x1, ub, ub.ap, tt * 128, 24, 32, col, lnw[ti % 2] + (ps[ti % 2],))
            dma("pool", UT_d[:, :, c0:c0 + w].rearrange("k p n -> p k n"), ub.ap[:, :, 0:w], [ub], [UTt[bj]])
        P.barrier()
        P.release(m)

        phase_done('P6')
        last_layer = (l == L - 1)
        for hhalf in range(2):
            m = P.mark()
            w1 = P.alloc([128, 8, 2048], BF16, "w1")
            load_cast(w1, w1.ap, wff1_d[l][:, hhalf * 2048:(hhalf + 1) * 2048].rearrange("(k p) n -> p k n", p=128), 8, 2048)
            w2 = P.alloc([128, 16, D], BF16, "w2")
            load_cast(w2, w2.ap, wff2_d[l][hhalf * 2048:(hhalf + 1) * 2048, :].rearrange("(k p) n -> p k n", p=128), 16, D)
            ublk = [P.alloc([128, 8, 512], BF16, f"ublkf{i}") for i in range(2)]
            hT = [P.alloc([128, 16, 512], BF16, f"hT{i}") for i in range(2)]
            xts = [P.alloc([128, D], F32, f"xtf{i}") for i in range(2)]
            x2s = [P.alloc([128, D], F32, f"x2{i}") for i in range(2)]
            rtmp = [P.alloc([128, 512], F32, f"rtmp{i}") for i in range(2)]
            fg = None
            if last_layer and hhalf == 1:
                fg = P.alloc([128, D], F32, "fg")
                dma("sp", fg.ap, fing_d, [], [fg])
                sq = P.alloc([128, D], F32, "sq3")
                ssum = P.alloc([128, 1], F32, "ssum3")
                rs = P.alloc([128, 1], F32, "rs3")
                xo = [P.alloc([128, D], F32, f"xo{i}") for i in range(2)]
            for bj, (c0, w) in enumerate(BLOCKS):
                if last_layer and bj == 8:
                    continue
                ub = ublk[bj % 2]
                ht = hT[bj % 2]
                col = 1 if bj == 8 else 0
                dma("sp", ub.ap[:, :, 0:w], UT_d[:, :, c0:c0 + w].rearrange("k p n -> p k n"), [UTt[bj]], [ub])
                for j in range(16):
                    pt = ps[j % 2]
                    for k in range(8):
                        P.emit("pe", lambda e, k=k, j=j, pt=pt, w=w, ub=ub: e.matmul(pt.ap[:, 0:w], lhsT=w1.ap[:, k, j * 128:(j + 1) * 128], rhs=ub.ap[:, k, 0:w], start=(k == 0), stop=(k == 7)),
                               reads=[w1, ub], writes=[pt])
                    rt = rtmp[j % 2]
                    P.emit("act", lambda e, pt=pt, w=w, rt=rt: e.activation(out=rt.ap[:, 0:w], in_=pt.ap[:, 0:w], func=AF.Relu), reads=[pt], writes=[rt])
                    P.emit("dve" if j % 2 == 0 else "pool", lambda e, j=j, w=w, ht=ht, rt=rt: e.tensor_tensor(out=ht.ap[:, j, 0:w], in0=rt.ap[:, 0:w], in1=rt.ap[:, 0:w], op=ALU.mult), reads=[rt], writes=[ht])
                for tt in range(w // 128):
                    ti = c0 // 128 + tt
                    xt, x2 = xts[ti % 2], x2s[ti % 2]
                    dma("sp", xt.ap, X_d[ti * 128:(ti + 1) * 128, :], [Xt[ti]], [xt])
                    for half in range(2):
                        pt = ps[2 + half + 2 * (ti % 2)]
                        for k in range(16):
                            P.emit("pe", lambda e, k=k, half=half, pt=pt, tt=tt, ht=ht: e.matmul(pt.ap, lhsT=ht.ap[:, k, tt * 128:(tt + 1) * 128], rhs=w2.ap[:, k, half * 512:(half + 1) * 512], start=(k == 0), stop=(k == 15)),
                                   reads=[ht, w2], writes=[pt])
                        P.emit("dve", lambda e, half=half, pt=pt, col=col, x2=x2: e.tensor_tensor(out=x2.ap[:, half * 512:(half + 1) * 512], in0=pt.ap, in1=gbt.ap[:, 2 + col, half * 512:(half + 1) * 512], op=ALU.mult),
                               reads=[pt, gbt], writes=[x2])
                    P.emit("pool", lambda e, xt=xt, x2=x2: e.tensor_tensor(out=x2.ap, in0=x2.ap, in1=xt.ap, op=ALU.add), reads=[x2, xt], writes=[x2])
                    if fg is None:
                        dma("pool", X_d[ti * 128:(ti + 1) * 128, :], x2.ap, [x2], [Xt[ti]])
                    else:
                        o = xo[ti % 2]
                        P.emit("act", lambda e, x2=x2: e.activation(out=sq.ap, in_=x2.ap, func=AF.Square), reads=[x2], writes=[sq])
                        P.emit("dve", lambda e: e.reduce_sum(out=ssum.ap, in_=sq.ap, axis=AX.X), reads=[sq], writes=[ssum])
                        rsqrt_ops(P, rs, ssum.ap, ssum, rs.ap, D * EPS)
                        P.emit("dve", lambda e, x2=x2, o=o: e.tensor_scalar(out=o.ap, in0=x2.ap, scalar1=rs.ap[:, 0:1], scalar2=32.0, op0=ALU.mult, op1=ALU.mult), reads=[x2, rs], writes=[o])
                        P.emit("pool", lambda e, o=o: e.tensor_tensor(out=o.ap, in0=o.ap, in1=fg.ap, op=ALU.mult), reads=[o, fg], writes=[o])
                        dma("pool", out_d[ti * 128:(ti + 1) * 128, :], o.ap, [o], [OUTt[ti]])
            P.barrier()
            P.release(m)
        if debug and l == 0:
            mm = P.mark()
            dx = [P.alloc([128, D], F32, f"dbgx{i}") for i in range(2)]
            for ti in range(NTILE):
                dma("sp", dx[ti % 2].ap, X_d[ti * 128:(ti + 1) * 128, :], [Xt[ti]], [dx[ti % 2]])
                dma("pool", dbg["X"][ti * 128:(ti + 1) * 128, :], dx[ti % 2].ap, [dx[ti % 2]], [])
            P.barrier()
            P.release(mm)

      except _Stop as ex:
        print('build stopped after phase', ex)
        if debug:
            P.barrier()
            P.sb_off = 16512
            dtile = [P.alloc([128, 8, 512], BF16, f"dbgs{i}") for i in range(2)]
            for bj, (c0, w) in enumerate(BLOCKS):
                dt_ = dtile[bj % 2]
                dma("sp", dt_.ap[:, :, 0:w], MT_d[:, :, c0:c0 + w].rearrange("k p n -> p k n"), [MTt[k][bj] for k in range(8)], [dt_])
                dma("pool", dbg["MT"][:, :, c0:c0 + w].rearrange("k p n -> p k n"), dt_.ap[:, :, 0:w], [dt_], [])
        break

    P.barrier()
    P.finalize()
    return nc


def _rope_tables():
    t = np.arange(NLAT)
    row = (t // 64).astype(np.float32)
    colp = (t % 64).astype(np.float32)
    half = 32
    freqs = (1.0 / (10000.0 ** (np.arange(0, half, 2, dtype=np.float32) / half))).astype(np.float32)
    ang = np.concatenate([row[:, None] * freqs, colp[:, None] * freqs], axis=-1)
    cos, sin = np.cos(ang).astype(np.float32), np.sin(ang).astype(np.float32)
    C = np.ones((128, NT), np.float32)
    S = np.zeros((128, NT), np.float32)
    for r in range(128):
        rr = r % 64
        j = rr % 32
        C[r, :NLAT] = cos[:, j]
        S[r, :NLAT] = (-sin[:, j]) if rr < 32 else sin[:, j]
    return C, S


def _prep_shared(inp, L):
    f = lambda a: np.ascontiguousarray(np.asarray(a, dtype=np.float32))
    w_in = f(inp["w_in"])[:L]
    sw64 = np.concatenate([np.arange(32, 64), np.arange(0, 32)])
    kr = 384 + np.arange(64)
    fm_cols = np.concatenate([
        np.arange(0, 256), np.arange(256, 384), kr, kr, kr[sw64], kr[sw64],
        448 + np.arange(256), 704 + np.arange(256), 1488 + np.arange(256), 1744 + np.arange(256)])
    tm_cols = np.concatenate([704 + np.arange(256), 960 + np.arange(256), 1216 + np.arange(256), 1472 + np.arange(16)])
    w_uq = f(inp["mla_w_uq"])[:L]
    uq_cols = []
    for h in range(4):
        uq_cols.append(h * 192 + np.arange(128))
    rope = lambda h: h * 192 + 128 + np.arange(64)
    uq_cols += [rope(0), rope(1), rope(2), rope(3), rope(0)[sw64], rope(1)[sw64], rope(2)[sw64], rope(3)[sw64]]
    uq_cols = np.concatenate(uq_cols)
    w_ukv = f(inp["mla_w_ukv"])[:L]
    kcols = np.concatenate([h * 256 + np.arange(128) for h in range(4)])
    vcols = np.concatenate([h * 256 + 128 + np.arange(128) for h in range(4)])
    wa, wx = f(inp["lru_w_a"])[:L], f(inp["lru_w_x"])[:L]
    wbd = np.zeros((L, 128, 8, 128), np.float32)
    for d in range(2):
        for ax, wsrc in enumerate((wa, wx)):
            for c in range(2):
                idx = (d * 2 + ax) * 2 + c
                for g in range(2):
                    wbd[:, g * 64:(g + 1) * 64, idx, g * 64:(g + 1) * 64] = wsrc[:, d, 2 * c + g]
    small = np.zeros((128, L, 40), np.float32)
    gq, gkv = f(inp["mla_g_q"])[:L], f(inp["mla_g_kv"])[:L]
    cwv, cbv = f(inp["lru_conv_w"])[:L], f(inp["lru_conv_b"])[:L]
    ba, bx, lam = f(inp["lru_b_a"])[:L], f(inp["lru_b_x"])[:L], f(inp["lru_lam"])[:L]
    for l in range(L):
        small[:, l, 0:2] = gq[l].reshape(2, 128).T
        small[:, l, 2] = gkv[l]
        for c in range(2):
            for j in range(4):
                small[:, l, 3 + c * 4 + j] = cwv[l, j, c * 128:(c + 1) * 128]
            small[:, l, 11 + c] = cbv[l, c * 128:(c + 1) * 128]
            for d in range(2):
                small[:, l, 13 + d * 2 + c] = ba[l, d, c * 128:(c + 1) * 128]
                small[:, l, 17 + d * 2 + c] = bx[l, d, c * 128:(c + 1) * 128]
                small[:, l, 21 + d * 2 + c] = lam[l, d, c * 128:(c + 1) * 128]
    gbias = np.ascontiguousarray(np.broadcast_to(f(inp["ml_gate_bias"])[:L][None], (128, L, 16)))
    consts = np.zeros((128, 4, 128), np.float32)
    consts[:, 0, :] = np.eye(128, dtype=np.float32)
    consts[:, 1, :] = np.triu(np.ones((128, 128), np.float32))
    consts[:, 2, :] = np.tril(np.ones((128, 128), np.float32))
    consts[:, 3, :] = 1.0
    C, S = _rope_tables()
    return {
        "w_mod": f(inp["w_mod"])[:L], "b_mod": f(inp["b_mod"])[:L],
        "w_in_fm": np.ascontiguousarray(w_in[:, :, fm_cols]), "w_in_tm": np.ascontiguousarray(w_in[:, :, tm_cols]),
        "w_uq": np.ascontiguousarray(w_uq[:, :, uq_cols]),
        "w_k": np.ascontiguousarray(w_ukv[:, :, kcols]), "w_v": np.ascontiguousarray(w_ukv[:, :, vcols]),
        "w_out": f(inp["w_out"])[:L], "w_ff1": f(inp["w_ff1"])[:L], "w_ff2": f(inp["w_ff2"])[:L],
        "w_bd": wbd, "small": small, "gbias": gbias,
        "final_g": np.ascontiguousarray(np.broadcast_to(f(inp["final_g"])[None], (128, D))),
        "ropeC": C, "ropeS": S, "consts": consts,
    }


_NC_CACHE = {}


def run(inputs, depth=4, debug=False, n_cores=8):
    shared = _prep_shared(inputs, depth)
    x, c, ctx, c_ctx = (np.asarray(inputs[k], dtype=np.float32) for k in ("x", "c", "ctx", "c_ctx"))
    in_maps = []
    for core in range(n_cores):
        b = core % 4
        mm = dict(shared)
        mm["xin"] = np.ascontiguousarray(np.concatenate([x[b], ctx[b]], axis=0))
        cc = np.stack([c[b].reshape(8, 128).T, c_ctx.reshape(8, 128).T], axis=-1)
        mm["cc"] = np.ascontiguousarray(cc.astype(np.float32))
        in_maps.append(mm)
    key = (depth, debug)
    if key not in _NC_CACHE:
        _NC_CACHE[key] = build(depth, debug)
    res = run_bass_kernel_spmd(_NC_CACHE[key], in_maps, core_ids=list(range(n_cores)))
    return res


def kernel(**inputs):
    res = run(inputs)
    out = np.stack([np.asarray(res.results[b]["out"], dtype=np.float32) for b in range(4)], axis=0)
    return out
```
